# Optimizing a Trainium2 kernel written in Bass

```python
import math
import numpy as np
import jax
import jax.numpy as jnp
from jax import lax

D_MODEL = 2048
BATCH = 2
SEQ = 4096
DEPTH = 4
DEC_BATCH = 8
DEC_SEQ = 4
PAST_LEN = 16384
PAGE_SIZE = 128

BRANCH_WIDTH = D_MODEL // 2
HEAD_DIM = 128
N_HEADS = BRANCH_WIDTH // HEAD_DIM
GQA_REP = 4
N_KV_HEADS = N_HEADS // GQA_REP
ROPE_THETA = 10000.0
CMP_LEN = 32
CMP_STRIDE = 16
SLC_BLK = 64
N_SEL = 16
WINDOW = 512
Q_BLOCK = 128
SEL_BONUS = 1.0e4
POOL_WINDOWS = (2, 4, 8, 16)
N_POOL_GROUPS = len(POOL_WINDOWS)
POOL_GROUP = BRANCH_WIDTH // N_POOL_GROUPS
POOL_STATE = max(POOL_WINDOWS) - 1
M_HEAD_DIM = 64
M_HEADS = BRANCH_WIDTH // M_HEAD_DIM
M_STATE = 128
M_GROUPS = 2
CONV_W = 4
CONV_CH = BRANCH_WIDTH + 2 * M_GROUPS * M_STATE
SSD_CHUNK = 128
EPS = 1e-6
IN_WIDTHS = (N_HEADS * HEAD_DIM, 6 * N_KV_HEADS * HEAD_DIM, 3 * N_HEADS, BRANCH_WIDTH, BRANCH_WIDTH,
             BRANCH_WIDTH, BRANCH_WIDTH, CONV_CH, M_HEADS, 3 * D_MODEL)
N_IN = sum(IN_WIDTHS)

kernel_name = 'nsa_pool_ssd_parallel_hybrid_step'


def _rmsnorm(x, g):
    x32 = x.astype(jnp.float32)
    r = x32 * lax.rsqrt(jnp.mean(x32 * x32, axis=-1, keepdims=True) + EPS)
    return (r * g.astype(jnp.float32)).astype(x.dtype)


def _split(z, widths):
    return jnp.split(z, np.cumsum(widths)[:-1].tolist(), axis=-1)


def _rope_tables(pos0, length):
    inv = ROPE_THETA ** (-jnp.arange(0, HEAD_DIM, 2, dtype=jnp.float32) / HEAD_DIM)
    ang = (pos0 + jnp.arange(length, dtype=jnp.float32))[:, None] * inv[None, :]
    return jnp.cos(ang), jnp.sin(ang)


def _rope(x, cos, sin):
    x32 = x.astype(jnp.float32)
    x1, x2 = jnp.split(x32, 2, axis=-1)
    return jnp.concatenate([x1 * cos - x2 * sin, x2 * cos + x1 * sin], axis=-1).astype(x.dtype)


def _masked_softmax(s, mask):
    s = jnp.where(mask, s.astype(jnp.float32), -jnp.inf)
    m = jnp.max(s, axis=-1, keepdims=True)
    m = jnp.where(jnp.isfinite(m), m, 0.0)
    e = jnp.exp(s - m)
    return e / jnp.maximum(jnp.sum(e, axis=-1, keepdims=True), 1e-30)


def _nsa(q, kv_full, win_all, gates, q_pos0, cmp_pe, cmp_w):
    bsz, tq = q.shape[0], q.shape[1]
    tk = kv_full.shape[1]
    scale = HEAD_DIM ** -0.5
    n_chunks = tk // CMP_STRIDE
    r = CMP_LEN // CMP_STRIDE
    n_cmp = n_chunks - r + 1
    chunks = kv_full[:, :n_chunks * CMP_STRIDE, :2].reshape(bsz, n_chunks, CMP_STRIDE, 2, N_KV_HEADS, HEAD_DIM)
    blocks = jnp.concatenate([chunks[:, j:j + n_cmp] for j in range(r)], axis=2)
    blocks = blocks + jnp.transpose(cmp_pe, (1, 0, 2))[:, :, None, :]
    kv_c = jnp.einsum('bilcgd,clde->bicge', blocks, cmp_w.reshape(2, CMP_LEN, HEAD_DIM, HEAD_DIM))
    k_c, v_c = kv_c[:, :, 0], kv_c[:, :, 1]
    cmp_start = jnp.arange(n_cmp) * CMP_STRIDE
    cmp_end = cmp_start + CMP_LEN - 1
    n_slc = -(-tk // SLC_BLK)
    pad = n_slc * SLC_BLK - tk
    kv_s = jnp.pad(kv_full[:, :, 2:4], ((0, 0), (0, pad), (0, 0), (0, 0), (0, 0)))
    kv_s = kv_s.reshape(bsz, n_slc, SLC_BLK, 2, N_KV_HEADS, HEAD_DIM).transpose(3, 0, 4, 1, 2, 5)
    k_s, v_s = kv_s[0], kv_s[1]
    slc_start = jnp.arange(n_slc) * SLC_BLK
    overlap = ((cmp_start[:, None] < slc_start[None, :] + SLC_BLK)
               & (cmp_start[:, None] + CMP_LEN > slc_start[None, :])).astype(jnp.float32)
    n_sel = min(N_SEL, n_slc)
    qg = q.reshape(bsz, tq, N_KV_HEADS, GQA_REP, HEAD_DIM)
    gg = gates.reshape(bsz, tq, N_KV_HEADS, GQA_REP, 3)
    qblk = Q_BLOCK if tq % Q_BLOCK == 0 else tq
    bi = jnp.arange(bsz)[:, None, None]
    gi = jnp.arange(N_KV_HEADS)[None, :, None]
    blk_ids = jnp.arange(n_slc)

    def one_block(i):
        s0 = i * qblk
        qb = lax.dynamic_slice_in_dim(qg, s0, qblk, axis=1)
        gb = lax.dynamic_slice_in_dim(gg, s0, qblk, axis=1)
        t = q_pos0 + s0 + jnp.arange(qblk)
        sc = jnp.einsum('bqgrd,bigd->bqgri', qb, k_c) * scale
        pc = _masked_softmax(sc, (cmp_end[None, :] <= t[:, None])[None, :, None, None, :])
        o_cmp = jnp.einsum('bqgri,bigd->bqgrd', pc, v_c)
        imp = jnp.einsum('bqgri,ij->bqgj', pc, overlap)
        cur = t // SLC_BLK
        forced = (blk_ids[None, :] == 0) | (blk_ids[None, :] == cur[:, None]) | (blk_ids[None, :] == cur[:, None] - 1)
        valid = slc_start[None, :] <= t[:, None]
        score = jnp.where(valid[None, :, None, :],
                          imp + SEL_BONUS * forced[None, :, None, :].astype(jnp.float32), -jnp.inf)
        _, idx = lax.top_k(score, n_sel)
        idx_flat = idx.transpose(0, 2, 1, 3).reshape(bsz, N_KV_HEADS, qblk * n_sel)
        kg = k_s[bi, gi, idx_flat].reshape(bsz, N_KV_HEADS, qblk, n_sel, SLC_BLK, HEAD_DIM)
        vg = v_s[bi, gi, idx_flat].reshape(bsz, N_KV_HEADS, qblk, n_sel, SLC_BLK, HEAD_DIM)
        ss = jnp.einsum('bqgrd,bgqnkd->bqgrnk', qb, kg) * scale
        kpos = idx[..., None] * SLC_BLK + jnp.arange(SLC_BLK)
        smask = (kpos <= t[None, :, None, None, None]).reshape(bsz, qblk, N_KV_HEADS, 1, n_sel * SLC_BLK)
        ps = _masked_softmax(ss.reshape(bsz, qblk, N_KV_HEADS, GQA_REP, n_sel * SLC_BLK), smask)
        o_slc = jnp.einsum('bqgrnk,bgqnkd->bqgrd',
                           ps.reshape(bsz, qblk, N_KV_HEADS, GQA_REP, n_sel, SLC_BLK), vg)
        wb = lax.dynamic_slice_in_dim(win_all, s0, WINDOW + qblk, axis=1)
        wpos = q_pos0 - WINDOW + s0 + jnp.arange(WINDOW + qblk)
        dpos = t[:, None] - wpos[None, :]
        wmask = (wpos[None, :] >= 0) & (dpos >= 0) & (dpos < WINDOW)
        sw = jnp.einsum('bqgrd,bkgd->bqgrk', qb, wb[:, :, 0]) * scale
        pw = _masked_softmax(sw, wmask[None, :, None, None, :])
        o_win = jnp.einsum('bqgrk,bkgd->bqgrd', pw, wb[:, :, 1])
        return gb[..., 0:1] * o_cmp + gb[..., 1:2] * o_slc + gb[..., 2:3] * o_win

    out = lax.map(one_block, jnp.arange(tq // qblk))
    return jnp.moveaxis(out, 0, 1).reshape(bsz, tq, N_HEADS * HEAD_DIM).astype(q.dtype)


def _pool_mix(u, prefix, pos0, pool_w, pool_scale):
    bsz, T, C = u.shape
    p = prefix.shape[1]
    up = jnp.concatenate([prefix, u], axis=1)
    up32 = up.astype(jnp.float32)
    cs = jnp.concatenate([jnp.zeros((bsz, 1, C), jnp.float32), jnp.cumsum(up32, axis=1)], axis=1)
    pos = pos0 + jnp.arange(T)
    means = []
    for gidx, w in enumerate(POOL_WINDOWS):
        c0, c1 = gidx * POOL_GROUP, (gidx + 1) * POOL_GROUP
        hi = cs[:, p + 1:p + 1 + T, c0:c1]
        lo = cs[:, p + 1 - w:p + 1 - w + T, c0:c1]
        cnt = jnp.minimum(w, pos + 1).astype(jnp.float32)[None, :, None]
        means.append((hi - lo) / cnt)
    pooled = jnp.concatenate(means, axis=-1) - up32[:, p:]
    mixed = jnp.einsum('btgc,gcd->btgd', pooled.reshape(bsz, T, N_POOL_GROUPS, POOL_GROUP), pool_w)
    mixed = mixed.reshape(bsz, T, C) * pool_scale
    return mixed.astype(u.dtype), up[:, -POOL_STATE:]


def _segsum(a):
    n = a.shape[-1]
    cs = jnp.cumsum(a, axis=-1)
    d = cs[..., :, None] - cs[..., None, :]
    return jnp.where(jnp.tril(jnp.ones((n, n), dtype=bool)), d, -jnp.inf)


def _ssd(x, dt, a_head, bh, ch, init):
    bsz, length = x.shape[0], x.shape[1]
    cl = min(SSD_CHUNK, length)
    pad = (-length) % cl
    if pad:
        padw = lambda v: jnp.pad(v, [(0, 0), (0, pad)] + [(0, 0)] * (v.ndim - 2))
        x, dt, bh, ch = padw(x), padw(dt), padw(bh), padw(ch)
    nc = (length + pad) // cl
    xdt = (x * dt[..., None]).reshape(bsz, nc, cl, M_HEADS, M_HEAD_DIM)
    bc = bh.reshape(bsz, nc, cl, M_HEADS, M_STATE)
    cc = ch.reshape(bsz, nc, cl, M_HEADS, M_STATE)
    la = jnp.transpose((dt * a_head).reshape(bsz, nc, cl, M_HEADS), (0, 3, 1, 2))
    acs = jnp.cumsum(la, axis=-1)
    decay_in = jnp.exp(_segsum(la))
    y_diag = jnp.einsum('bclhn,bcshn,bhcls,bcshp->bclhp', cc, bc, decay_in, xdt)
    decay_to_end = jnp.exp(acs[..., -1:] - acs)
    states = jnp.einsum('bclhn,bhcl,bclhp->bchpn', bc, decay_to_end, xdt)
    states = jnp.concatenate([init[:, None], states], axis=1)
    decay_chunk = jnp.exp(_segsum(jnp.pad(acs[..., -1], ((0, 0), (0, 0), (1, 0)))))
    states = jnp.einsum('bhzc,bchpn->bzhpn', decay_chunk, states)
    prev, final = states[:, :-1], states[:, -1]
    y_off = jnp.einsum('bclhn,bchpn,bhcl->bclhp', cc, prev, jnp.exp(acs))
    y = (y_diag + y_off).reshape(bsz, nc * cl, M_HEADS, M_HEAD_DIM)[:, :length]
    return y, final


def _layer(x, pos0, past, norm_w, w_in, qk_gain, cmp_pe, cmp_w, pool_w, pool_scale, conv_w, conv_b,
           dt_bias, a_log, d_skip, mnorm_w, w_branch, w_out):
    f32 = jnp.float32
    bsz, T, _ = x.shape
    h = _rmsnorm(x, norm_w)
    zin = h @ w_in
    q, kv, nsa_g, nsa_z, pool_u, pool_z, m_z, xbc, m_dt, merge_g = _split(zin, IN_WIDTHS)
    cos, sin = _rope_tables(pos0, T)
    q = _rope(_rmsnorm(q.reshape(bsz, T, N_HEADS, HEAD_DIM), qk_gain[0]), cos[:, None, :], sin[:, None, :])
    kv = kv.reshape(bsz, T, 3, 2, N_KV_HEADS, HEAD_DIM)
    k = _rope(_rmsnorm(kv[:, :, :, 0], qk_gain[1:, None, :]), cos[:, None, None, :], sin[:, None, None, :])
    kv = jnp.stack([k, kv[:, :, :, 1]], axis=3)
    rows = kv[:, :, :2].reshape(bsz, T, 4, N_KV_HEADS, HEAD_DIM)
    win_rows = kv[:, :, 2]
    if past is None:
        kv_full = rows
        win_cat = win_rows
        n_past_win = 0
    else:
        kv_full = jnp.concatenate([past['kv'], rows], axis=1)
        win_cat = jnp.concatenate([past['win'], win_rows], axis=1)
        n_past_win = past['win'].shape[1]
    win_all = jnp.pad(win_cat, ((0, 0), (WINDOW - n_past_win, 0), (0, 0), (0, 0), (0, 0)))
    new_win = win_cat[:, -min(WINDOW, n_past_win + T):]
    gates = jax.nn.sigmoid(nsa_g.astype(f32)).reshape(bsz, T, N_HEADS, 3)
    nsa_out = _nsa(q, kv_full, win_all, gates, pos0, cmp_pe, cmp_w) * jax.nn.silu(nsa_z)
    pool_prefix = jnp.zeros((bsz, POOL_STATE, BRANCH_WIDTH), x.dtype) if past is None else past['pool']
    pool_out, new_pool = _pool_mix(pool_u, pool_prefix, pos0, pool_w, pool_scale)
    pool_out = pool_out * jax.nn.silu(pool_z)
    conv_prefix = jnp.zeros((bsz, CONV_W - 1, CONV_CH), x.dtype) if past is None else past['conv']
    xbc_p = jnp.concatenate([conv_prefix, xbc], axis=1)
    new_conv = xbc_p[:, -(CONV_W - 1):]
    acc = conv_b
    for j in range(CONV_W):
        acc = acc + xbc_p[:, j:j + T] * conv_w[j]
    xbc_c = jax.nn.silu(acc)
    xs, bm, cm = _split(xbc_c, (BRANCH_WIDTH, M_GROUPS * M_STATE, M_GROUPS * M_STATE))
    xs = xs.reshape(bsz, T, M_HEADS, M_HEAD_DIM).astype(f32)
    rep = M_HEADS // M_GROUPS
    bh = jnp.repeat(bm.reshape(bsz, T, M_GROUPS, M_STATE), rep, axis=2).astype(f32)
    ch = jnp.repeat(cm.reshape(bsz, T, M_GROUPS, M_STATE), rep, axis=2).astype(f32)
    dt = jax.nn.softplus(m_dt.astype(f32) + dt_bias.astype(f32))
    a_head = -jnp.exp(a_log.astype(f32))
    init = jnp.zeros((bsz, M_HEADS, M_HEAD_DIM, M_STATE), f32) if past is None else past['ssm'].astype(f32)
    y, ssm_final = _ssd(xs, dt, a_head, bh, ch, init)
    y = (y + d_skip.astype(f32)[:, None] * xs).reshape(bsz, T, BRANCH_WIDTH) * jax.nn.silu(m_z.astype(f32))
    yg = y.reshape(bsz, T, M_GROUPS, BRANCH_WIDTH // M_GROUPS)
    yg = yg * lax.rsqrt(jnp.mean(yg * yg, axis=-1, keepdims=True) + EPS)
    m_out = (yg.reshape(bsz, T, BRANCH_WIDTH) * mnorm_w.astype(f32)).astype(x.dtype)
    br = jnp.stack([nsa_out, pool_out, m_out], axis=2)
    proj = jnp.einsum('btkw,kwd->btkd', br, w_branch)
    g = jax.nn.sigmoid(merge_g.astype(f32)).reshape(bsz, T, 3, D_MODEL)
    merged = jnp.sum(g * proj, axis=2).astype(x.dtype)
    y_out = x + merged @ w_out
    return y_out, rows, new_win, new_pool, new_conv, ssm_final.astype(x.dtype)


def setup_inputs(seed: int = 0) -> dict:
    key = jax.random.key(seed)
    ks = jax.random.split(key, 24)
    f32 = jnp.float32
    n_pages = PAST_LEN // PAGE_SIZE
    n_pool = (DEC_BATCH * n_pages * 5) // 4
    wb = min(WINDOW, PAST_LEN)
    nrm = lambda k, shape, s: s * jax.random.normal(k, shape, f32)
    x_prompt = nrm(ks[0], (BATCH, SEQ, D_MODEL), 1.0)
    x_sample = nrm(ks[1], (DEC_BATCH, DEC_SEQ, D_MODEL), 1.0)
    cache_kv = nrm(ks[2], (DEPTH, n_pool, PAGE_SIZE, 4, N_KV_HEADS, HEAD_DIM), 1.0)
    cache_win = nrm(ks[3], (DEPTH, DEC_BATCH, wb, 2, N_KV_HEADS, HEAD_DIM), 1.0)
    state_pool = nrm(ks[4], (DEPTH, DEC_BATCH, POOL_STATE, BRANCH_WIDTH), 1.0)
    state_conv = nrm(ks[5], (DEPTH, DEC_BATCH, CONV_W - 1, CONV_CH), 1.0)
    state_ssm = nrm(ks[6], (DEPTH, DEC_BATCH, M_HEADS, M_HEAD_DIM, M_STATE), 0.5)
    page_table = jax.random.permutation(ks[7], n_pool)[:DEC_BATCH * n_pages].reshape(DEC_BATCH, n_pages).astype(jnp.int32)
    norm_w = 1.0 + nrm(ks[8], (DEPTH, D_MODEL), 0.05)
    w_in = nrm(ks[9], (DEPTH, D_MODEL, N_IN), D_MODEL ** -0.5)
    qk_gain = 1.0 + nrm(ks[10], (DEPTH, 4, HEAD_DIM), 0.05)
    cmp_pe = nrm(ks[11], (DEPTH, 2, CMP_LEN, HEAD_DIM), 0.1)
    cmp_w = nrm(ks[12], (DEPTH, 2, CMP_LEN * HEAD_DIM, HEAD_DIM), (CMP_LEN * HEAD_DIM) ** -0.5)
    pool_w = nrm(ks[13], (DEPTH, N_POOL_GROUPS, POOL_GROUP, POOL_GROUP), POOL_GROUP ** -0.5)
    pool_scale = 1.0 + nrm(ks[14], (DEPTH, BRANCH_WIDTH), 0.1)
    conv_w = nrm(ks[15], (DEPTH, CONV_W, CONV_CH), CONV_W ** -0.5)
    conv_b = nrm(ks[16], (DEPTH, CONV_CH), 0.02)
    dt0 = jnp.exp(jax.random.uniform(ks[17], (DEPTH, M_HEADS), f32, math.log(1e-3), math.log(1e-1)))
    dt_bias = dt0 + jnp.log(-jnp.expm1(-dt0))
    a_log = jnp.log(jax.random.uniform(ks[18], (DEPTH, M_HEADS), f32, 1.0, 16.0))
    d_skip = 1.0 + nrm(ks[19], (DEPTH, M_HEADS), 0.1)
    mnorm_w = 1.0 + nrm(ks[20], (DEPTH, BRANCH_WIDTH), 0.05)
    w_branch = nrm(ks[21], (DEPTH, 3, BRANCH_WIDTH, D_MODEL), BRANCH_WIDTH ** -0.5)
    w_out = nrm(ks[22], (DEPTH, D_MODEL, D_MODEL), D_MODEL ** -0.5)
    return {'x_prompt': x_prompt, 'x_sample': x_sample, 'cache_kv': cache_kv, 'cache_win': cache_win,
            'state_pool': state_pool, 'state_conv': state_conv, 'state_ssm': state_ssm,
            'page_table': page_table, 'norm_w': norm_w, 'w_in': w_in, 'qk_gain': qk_gain,
            'cmp_pe': cmp_pe, 'cmp_w': cmp_w, 'pool_w': pool_w, 'pool_scale': pool_scale,
            'conv_w': conv_w, 'conv_b': conv_b, 'dt_bias': dt_bias, 'a_log': a_log, 'd_skip': d_skip,
            'mnorm_w': mnorm_w, 'w_branch': w_branch, 'w_out': w_out}


def reference(x_prompt, x_sample, cache_kv, cache_win, state_pool, state_conv, state_ssm, page_table,
              norm_w, w_in, qk_gain, cmp_pe, cmp_w, pool_w, pool_scale, conv_w, conv_b, dt_bias, a_log,
              d_skip, mnorm_w, w_branch, w_out):
    dec_b, n_pages = page_table.shape
    past_len = n_pages * cache_kv.shape[2]
    xp, xs = x_prompt, x_sample
    outs_p = [[] for _ in range(5)]
    outs_s = [[] for _ in range(5)]
    for l in range(DEPTH):
        params = (norm_w[l], w_in[l], qk_gain[l], cmp_pe[l], cmp_w[l], pool_w[l], pool_scale[l], conv_w[l],
                  conv_b[l], dt_bias[l], a_log[l], d_skip[l], mnorm_w[l], w_branch[l], w_out[l])
        xp, *st_p = _layer(xp, 0, None, *params)
        past = {'kv': cache_kv[l][page_table].reshape(dec_b, past_len, 4, N_KV_HEADS, HEAD_DIM),
                'win': cache_win[l], 'pool': state_pool[l], 'conv': state_conv[l], 'ssm': state_ssm[l]}
        xs, *st_s = _layer(xs, past_len, past, *params)
        for lst, v in zip(outs_p, st_p):
            lst.append(v)
        for lst, v in zip(outs_s, st_s):
            lst.append(v)
    kv_p, win_p, pool_p, conv_p, ssm_p = [jnp.stack(v) for v in outs_p]
    kv_s, win_s, pool_s, conv_s, ssm_s = [jnp.stack(v) for v in outs_s]
    return (xp, xs, kv_p, win_p, pool_p, conv_p, ssm_p, kv_s, win_s, pool_s, conv_s, ssm_s)
```

```python
import math
from contextlib import ExitStack

import numpy as np
import concourse.bass as bass
import concourse.mybir as mybir
from concourse.bass_utils import run_bass_kernel_spmd

F32 = mybir.dt.float32
BF16 = mybir.dt.bfloat16
AF = mybir.ActivationFunctionType
ALU = mybir.AluOpType
AX = mybir.AxisListType

ENGS = ("pe", "act", "dve", "pool", "sp")
NDSEM = 12
NEAR = 3

D = 2048
NIN = 14376
BW = 1024
KT = 16
C_Q, C_KV, C_G, C_NZ, C_PU, C_PZ, C_MZ, C_XBC, C_DT, C_MG = 0, 1024, 2560, 2584, 3608, 4632, 5656, 6680, 8216, 8232
EPS = 1e-6
NEG = -1.0e30
PAST = 16384


def I(meth, *a, **kw):
    return lambda e: getattr(e, meth)(*a, **kw)


def AP(t, off, dims):
    return bass.AP(t, off, [list(d) for d in dims])


class Op:
    __slots__ = ("eng", "fn", "dma", "deps", "sig", "sem", "val", "idx", "prev_dma")


class Sched:
    def __init__(self, nc, csem, dsem):
        self.nc = nc
        self.csem = csem
        self.dsem = dsem
        self.ccnt = {e: 0 for e in ENGS}
        self.dcnt = {e: [0] * NDSEM for e in ENGS}
        self.drr = {e: 0 for e in ENGS}
        self.waited = {e: {} for e in ENGS}
        self.reset()

    def reset(self):
        self.ops = {e: [] for e in ENGS}
        self.lastw = {}
        self.readers = {}

    def add(self, eng, fn, reads=(), writes=(), dma=False):
        o = Op()
        o.eng, o.fn, o.dma, o.deps, o.sig = eng, fn, dma, set(), False
        o.sem, o.val, o.prev_dma = None, 0, None
        o.idx = len(self.ops[eng])
        for r in reads:
            w = self.lastw.get(r)
            if w is not None:
                o.deps.add(w)
            self.readers.setdefault(r, []).append(o)
        for r in writes:
            w = self.lastw.get(r)
            if w is not None:
                o.deps.add(w)
            for rd in self.readers.get(r, ()):
                o.deps.add(rd)
            self.lastw[r] = o
            self.readers[r] = []
        o.deps.discard(o)
        self.ops[eng].append(o)
        return o

    def _needs_wait(self, dep, o):
        if dep.dma or dep.eng != o.eng or o.dma:
            return True
        if o.eng == "pe":
            return False
        return (o.idx - dep.idx) <= NEAR

    def emit(self):
        for e in ENGS:
            for o in self.ops[e]:
                if o.dma:
                    o.sig = True
                for d in o.deps:
                    if self._needs_wait(d, o):
                        d.sig = True
        for e in ENGS:
            for o in self.ops[e]:
                if not o.sig:
                    continue
                if o.dma:
                    k = self.drr[e]
                    self.drr[e] = (k + 1) % NDSEM
                    o.prev_dma = self.dcnt[e][k]
                    self.dcnt[e][k] += 16
                    o.sem, o.val = self.dsem[e][k], self.dcnt[e][k]
                else:
                    self.ccnt[e] += 1
                    o.sem, o.val = self.csem[e], self.ccnt[e]
        engobj = {"pe": "tensor", "act": "scalar", "dve": "vector", "pool": "gpsimd", "sp": "sync"}
        with self.nc.Block() as block:
            for e in ENGS:
                ops = self.ops[e]
                if not ops and e != "sp":
                    continue

                def body(eng, e=e, ops=ops):
                    waited = self.waited[e]

                    def wait(sem, val):
                        key = id(sem)
                        if waited.get(key, 0) < val:
                            eng.wait_ge(sem, val)
                            waited[key] = val

                    for o in ops:
                        for d in o.deps:
                            if self._needs_wait(d, o):
                                wait(d.sem, d.val)
                        if o.dma and o.prev_dma:
                            wait(o.sem, o.prev_dma)
                        ins = o.fn(eng)
                        if o.sig:
                            ins.then_inc(o.sem, 16 if o.dma else 1)
                    if e == "sp":
                        for e2 in ENGS:
                            for k in range(NDSEM):
                                if self.dcnt[e2][k]:
                                    wait(self.dsem[e2][k], self.dcnt[e2][k])

                getattr(block, engobj[e])(body)
        self.reset()


def make_consts(T):
    NT = T // 128
    c = {}
    c["ident"] = np.eye(128, dtype=np.float32)
    inv = (np.float32(10000.0) ** (-np.arange(0, 128, 2, dtype=np.float32) / np.float32(128))).astype(np.float32)
    pos = np.concatenate([np.arange(T, dtype=np.float32), PAST + np.arange(128, dtype=np.float32)])
    ang = (pos[:, None] * inv[None, :]).astype(np.float32)
    cs, sn = np.cos(ang).astype(np.float32), np.sin(ang).astype(np.float32)
    rope = np.zeros((T + 128, 2, 128), np.float32)
    rope[:, 0, :64] = cs
    rope[:, 0, 64:] = cs
    rope[:, 1, :64] = -sn
    rope[:, 1, 64:] = sn
    c["rope"] = rope
    k = np.arange(128)
    tri = (k[:, None] <= k[None, :]).astype(np.float32)
    c["tri"] = tri
    c["anti"] = (1.0 - tri).astype(np.float32)
    c["ustrict"] = (k[:, None] > k[None, :]).astype(np.float32)
    c["ones"] = np.ones((128, 128), np.float32)
    n_cmp = T // 16 - 1
    cm = np.zeros((128, NT, 2, 128), np.float32)
    for qt in range(NT):
        for it in range(2):
            i = 128 * it + k
            t = 128 * qt + k
            cm[:, qt, it, :] = ((16 * i[:, None] + 31 <= t[None, :]) & (i[:, None] < n_cmp)).astype(np.float32)
    c["cmpmask"] = cm.reshape(128, NT * 2 * 128)
    bon = np.zeros((128, NT, 64), np.float32)
    j = np.arange(64)
    for qt in range(NT):
        t = 128 * qt + k
        cur = t // 64
        valid = (64 * j[None, :] <= t[:, None])
        forced = (j[None, :] == 0) | (j[None, :] == cur[:, None]) | (j[None, :] == cur[:, None] - 1)
        bon[:, qt, :] = np.where(valid, 1.0e4 * forced, NEG)
    c["bonus"] = bon.reshape(128, NT * 64)
    keys = np.arange(T)
    c["e2"] = (keys[None, :] // 64 == j[:, None]).astype(np.float32)
    ov = np.zeros((128, 2, 64), np.float32)
    for it in range(2):
        i = 128 * it + k
        ov[:, it, :] = ((16 * i[:, None] < 64 * j[None, :] + 64) & (16 * i[:, None] + 32 > 64 * j[None, :])
                        & (i[:, None] < n_cmp)).astype(np.float32)
    c["overlap"] = ov.reshape(128, 128)
    pa = np.zeros((128, 3, 4, 128), np.float32)
    for gi, w in enumerate((2, 4, 8, 16)):
        s = k[:, None]
        t = k[None, :]
        inwin = (s <= t) & (s > t - w)
        pa[:, 0, gi, :] = inwin / w - (s == t)
        cnt = np.minimum(w, t + 1)
        pa[:, 1, gi, :] = inwin / cnt - (s == t)
        sg = k[64:, None] - 128
        pa[64:, 2, gi, :] = ((sg > t - w)).astype(np.float32) / w
    c["poolA"] = pa.reshape(128, 3 * 4 * 128)
    c["riota"] = np.stack([8.0 * k, 2.0 * k], axis=1).astype(np.float32)
    kk = np.arange(8192)
    c["e2big"] = (kk[None, :] // 64 == k[:, None]).astype(np.float32)
    ncs = PAST // 16 - 1
    js = np.arange(257)
    ovs = np.zeros((128, 8, 257), np.float32)
    for it in range(8):
        i = 128 * it + k
        ovs[:, it, :] = ((16 * i[:, None] < 64 * js[None, :] + 64) & (16 * i[:, None] + 32 > 64 * js[None, :])
                         & (i[:, None] < ncs)).astype(np.float32)
    c["ovs"] = ovs.reshape(128, 8 * 257)
    cur = PAST // 64
    c["bons"] = np.tile((1.0e4 * ((js == 0) | (js == cur) | (js == cur - 1))).astype(np.float32)[None, :], (128, 1))
    pas = np.zeros((128, 4, 128), np.float32)
    for gi, w in enumerate((2, 4, 8, 16)):
        for tq in range(4):
            sidx = 15 + tq
            pas[sidx - w + 1:sidx + 1, gi, tq] = 1.0 / w
            pas[sidx, gi, tq] -= 1.0
    c["poolAs"] = pas.reshape(128, 512)
    c["rowm"] = (k < 4).astype(np.float32).reshape(128, 1)
    return c


CONST_SHAPES = lambda T: {"ident": [128, 128], "rope": [T + 128, 2, 128], "tri": [128, 128], "anti": [128, 128],
                          "ustrict": [128, 128], "ones": [128, 128], "cmpmask": [128, (T // 128) * 256],
                          "bonus": [128, (T // 128) * 64], "e2": [64, T], "overlap": [128, 128],
                          "poolA": [128, 1536]}

WEIGHTS = {"norm_w": [D], "w_in": [D, NIN], "qk_gain": [4, 128], "cmp_pe": [2, 32, 128], "cmp_w": [2, 4096, 128],
           "pool_w": [4, 256, 256], "pool_scale": [BW], "conv_w": [4, 1536], "conv_b": [1536], "dt_bias": [16],
           "a_log": [16], "d_skip": [16], "mnorm_w": [BW], "w_branch": [3, BW, D], "w_out": [D, D]}


def build(T, L, debug=False, NPOOL=1280):
    NT = T // 128
    TT = T + 128
    NTT = NT + 1
    n_cmp = T // 16 - 1
    NW = min(4, NT)
    nc = bass.Bass("TRN2", target_bir_lowering=False)
    dt_ = lambda name, shape, kind, dtype=F32: nc.dram_tensor(name, shape, dtype, kind=kind)
    xin = dt_("xp", [TT, D], "ExternalInput")
    Wd = {k: dt_(k, [L] + v, "ExternalInput") for k, v in WEIGHTS.items()}
    Cd = {k: dt_("c_" + k, v, "ExternalInput") for k, v in CONST_SHAPES(T).items()}
    scr = "ExternalOutput" if debug else "Internal"
    Z = dt_("Z", [TT, NIN], scr)
    Z2 = dt_("Z2", [1536, TT], scr)
    QT = dt_("QT", [8, 128, TT], scr, BF16)
    KTs = dt_("KTs", [8, 128, TT], scr, BF16)
    VV = dt_("VV", [TT, 4, 128], scr, BF16)
    BRT = dt_("BRT", [24, 128, TT], scr, BF16)
    MT = dt_("MT", [16, 128, TT], scr, BF16)
    xb = [dt_(f"xb{i}", [TT, D], "Internal") for i in range(2)]
    dbgK = dt_("dbgK", [128, 512], "ExternalOutput", BF16) if debug else None
    dbgV = dt_("dbgV", [128, 772], "ExternalOutput", BF16) if debug else None
    dbgO = dt_("dbgO", [128, 265], "ExternalOutput") if debug else None
    dbgE = dt_("dbgE", [128, 512], "ExternalOutput", BF16) if debug else None
    dbgN = dt_("dbgN", [128, 1024], "ExternalOutput") if debug else None
    y_p = dt_("y_p", [T, D], "ExternalOutput")
    kv_p = dt_("kv_p", [L, T, 1024], "ExternalOutput")
    win_p = dt_("win_p", [L, NW * 128, 512], "ExternalOutput")
    pool_p = dt_("pool_p", [L, 15, BW], "ExternalOutput")
    conv_p = dt_("conv_p", [L, 3, 1536], "ExternalOutput")
    ssm_p = dt_("ssm_p", [L, 1024, 128], "ExternalOutput")
    I32 = mybir.dt.int32
    SM = dict(
        cache=dt_("cache", [L * NPOOL * 128 * 8, 128], "ExternalInput"),
        cwin=dt_("cwin", [L, 512, 512], "ExternalInput"),
        spool=dt_("spool", [L, 15, BW], "ExternalInput"),
        sconv=dt_("sconv", [L, 3, 1536], "ExternalInput"),
        sssm=dt_("sssm", [L, 1024, 128], "ExternalInput"),
        pt=dt_("pt", [1, 128], "ExternalInput", I32),
        riota=dt_("c_riota", [128, 2], "ExternalInput"),
        e2big=dt_("c_e2big", [128, 8192], "ExternalInput"),
        ovs=dt_("c_ovs", [128, 8 * 257], "ExternalInput"),
        bons=dt_("c_bons", [128, 257], "ExternalInput"),
        poolAs=dt_("c_poolAs", [128, 512], "ExternalInput"),
        rowm=dt_("c_rowm", [128, 1], "ExternalInput"),
        y_s=dt_("y_s", [4, D], "ExternalOutput"),
        kv_s=dt_("kv_s", [L, 4, 1024], "ExternalOutput"),
        win_s=dt_("win_s", [L, 512, 512], "ExternalOutput"),
        pool_s=dt_("pool_s", [L, 15, BW], "ExternalOutput"),
        conv_s=dt_("conv_s", [L, 3, 1536], "ExternalOutput"),
        ssm_s=dt_("ssm_s", [L, 1024, 128], "ExternalOutput"),
        NPOOL=NPOOL, I32=I32,
    )

    es = ExitStack()
    csem = {e: es.enter_context(nc.semaphore("c_" + e)) for e in ENGS}
    dsem = {e: [es.enter_context(nc.semaphore(f"d_{e}{k}")) for k in range(NDSEM)] for e in ("sp", "act", "pool")}
    for e in ENGS:
        dsem.setdefault(e, [None] * NDSEM)
    S = Sched(nc, csem, dsem)
    uid = [0]

    def U():
        uid[0] += 1
        return uid[0]

    def dma(eng, out, in_, reads=(), writes=(), **kw):
        return S.add(eng, I("dma_start", out=out, in_=in_, **kw), reads=reads, writes=writes, dma=True)

    def bc_row(dram_t, off, n):
        return AP(dram_t, off, [[0, 128], [1, n]])

    for l in range(L):
        x_src = xin if l == 0 else xb[(l - 1) % 2]
        x_dst = y_p if l == L - 1 else xb[l % 2]
        Wl = {k: (lambda k=k: Wd[k].ap()[l]) for k in Wd}
        woff = {k: l * int(np.prod(v)) for k, v in WEIGHTS.items()}

        with ExitStack() as ph:
            sb = lambda n, s, d: ph.enter_context(nc.sbuf_tensor(f"{n}_{l}", s, d))
            hT = sb("hT", [128, KT, TT], BF16)
            with ExitStack() as p1:
                sb1 = lambda n, s, d: p1.enter_context(nc.sbuf_tensor(f"{n}_{l}", s, d))
                xs = [sb1(f"xs{i}", [128, D], F32) for i in range(2)]
                junk = sb1("junk", [128, D], BF16)
                hb = [sb1(f"hb{i}", [128, D], BF16) for i in range(2)]
                ss = [sb1(f"ss{i}", [128, 1], F32) for i in range(2)]
                rs = [sb1(f"rs{i}", [128, 1], F32) for i in range(2)]
                epsb = sb1("eps", [128, 1], F32)
                nwb = sb1("nwb", [128, D], F32)
                idf = sb1("idf", [128, 128], F32)
                idb = sb1("idb", [128, 128], BF16)
                pst = [p1.enter_context(nc.psum_tensor(f"pt{i}_{l}", [128, 1024], BF16)) for i in range(2)]
                dma("sp", idf[:], Cd["ident"].ap(), writes=["idf"])
                dma("sp", nwb[:], bc_row(Wd["norm_w"], woff["norm_w"], D), writes=["nwb"])
                S.add("dve", I("tensor_copy", out=idb[:], in_=idf[:]), reads=["idf"], writes=["idb"])
                S.add("dve", I("memset", epsb[:], EPS), writes=["eps"])
                dma("sp", xs[0][:], x_src.ap()[0:128, :], writes=[("xs", 0)])
                for t in range(NTT):
                    s = t % 2
                    if t + 1 < NTT:
                        dma("sp", xs[1 - s][:], x_src.ap()[(t + 1) * 128:(t + 2) * 128, :], writes=[("xs", 1 - s)])
                    S.add("act", I("activation", out=junk[:], in_=xs[s][:], func=AF.Square, accum_out=ss[s][:]),
                          reads=[("xs", s)], writes=["junk", ("ss", s)])
                    S.add("act", I("activation", out=rs[s][:], in_=ss[s][:], func=AF.Sqrt, bias=epsb[:], scale=1.0 / D),
                          reads=[("ss", s), "eps"], writes=[("rs", s)])
                    S.add("dve", I("reciprocal", out=rs[s][:], in_=rs[s][:]), reads=[("rs", s)], writes=[("rs", s)])
                    S.add("dve", I("scalar_tensor_tensor", out=hb[s][:], in0=xs[s][:], scalar=rs[s][:], in1=nwb[:],
                                                                         op0=ALU.mult, op1=ALU.mult),
                          reads=[("xs", s), ("rs", s), "nwb"], writes=[("hb", s)])
                    for half in range(2):
                        for j in range(8):
                            kt = half * 8 + j
                            S.add("pe", I("transpose",
                                out=pst[half][:, j * 128:(j + 1) * 128], in_=hb[s][:, kt * 128:(kt + 1) * 128], identity=idb[:]),
                                reads=[("hb", s), "idb"], writes=[("pst", half)])
                        src = AP(pst[half], 0, [[1024, 128], [128, 8], [1, 128]])
                        dst = hT[:, half * 8:(half + 1) * 8, t * 128:(t + 1) * 128]
                        if half == 0:
                            S.add("act", I("copy", out=dst, in_=src), reads=[("pst", half)], writes=[("hT", t)])
                        else:
                            S.add("dve", I("tensor_copy", out=dst, in_=src), reads=[("pst", half)], writes=[("hT", t)])
                S.emit()
            with ExitStack() as p2:
                sb2 = lambda n, s, d: p2.enter_context(nc.sbuf_tensor(f"{n}_{l}", s, d))
                wb = [sb2(f"wb{i}", [128, KT, 512], BF16) for i in range(2)]
                st = [sb2(f"st{i}", [128, 512], F32) for i in range(4)]
                ps = [p2.enter_context(nc.psum_tensor(f"ps{i}_{l}", [128, 512], F32)) for i in range(4)]
                chunks = []
                for (a, b_, fm) in ((0, C_XBC, False), (C_XBC, C_DT, True), (C_DT, NIN, False)):
                    c0 = a
                    while c0 < b_:
                        wd = min(512, b_ - c0)
                        chunks.append((c0, wd, fm))
                        c0 += wd
                n = 0
                TW = min(512, T)
                for ci, (c0, wd, fm) in enumerate(chunks):
                    s = ci % 2
                    dma("pool", wb[s][:, :, 0:wd], AP(Wd["w_in"], woff["w_in"] + c0, [[NIN, 128], [128 * NIN, KT], [1, wd]]),
                        writes=[("wb", s)])
                    if not fm:
                        for t in range(NTT):
                            b = n % 4
                            n += 1
                            for kt in range(KT):
                                S.add("pe", I("matmul",
                                    ps[b][:, 0:wd], lhsT=hT[:, kt, t * 128:(t + 1) * 128], rhs=wb[s][:, kt, 0:wd],
                                    start=(kt == 0), stop=(kt == KT - 1)), reads=[("wb", s)], writes=[("ps", b)])
                            if b % 2 == 0:
                                S.add("act", I("copy", out=st[b][:, 0:wd], in_=ps[b][:, 0:wd]),
                                      reads=[("ps", b)], writes=[("st", b)])
                            else:
                                S.add("dve", I("tensor_copy", out=st[b][:, 0:wd], in_=ps[b][:, 0:wd]),
                                      reads=[("ps", b)], writes=[("st", b)])
                            dma("sp", Z.ap()[t * 128:(t + 1) * 128, c0:c0 + wd], st[b][:, 0:wd], reads=[("st", b)])
                    else:
                        for sub in range(wd // 128):
                            for (q0, qw) in [(tt * TW, TW) for tt in range(T // TW)] + [(T, 128)]:
                                b = n % 4
                                n += 1
                                for kt in range(KT):
                                    S.add("pe", I("matmul",
                                        ps[b][:, 0:qw], lhsT=wb[s][:, kt, sub * 128:(sub + 1) * 128],
                                        rhs=hT[:, kt, q0:q0 + qw], start=(kt == 0), stop=(kt == KT - 1)),
                                        reads=[("wb", s)], writes=[("ps", b)])
                                if b % 2 == 0:
                                    S.add("act", I("copy", out=st[b][:, 0:qw], in_=ps[b][:, 0:qw]),
                                          reads=[("ps", b)], writes=[("st", b)])
                                else:
                                    S.add("dve", I("tensor_copy", out=st[b][:, 0:qw], in_=ps[b][:, 0:qw]),
                                          reads=[("ps", b)], writes=[("st", b)])
                                r0 = c0 - C_XBC + sub * 128
                                dma("sp", Z2.ap()[r0:r0 + 128, q0:q0 + qw], st[b][:, 0:qw], reads=[("st", b)])
                S.emit()

        with ExitStack() as ph:
            sb = lambda n, s, d: ph.enter_context(nc.sbuf_tensor(f"{n}_{l}", s, d))
            zt = [sb(f"zt{i}", [128, 2560], F32) for i in range(2)]
            rp = [sb(f"rp{i}", [128, 2, 128], F32) for i in range(2)]
            sq = sb("sq", [128, 2560], F32)
            tmp = sb("tmp", [128, 1024], F32)
            zb = [sb(f"zb{i}", [128, 2560], BF16) for i in range(2)]
            ssq = sb("ssq", [128, 20], F32)
            rst = sb("rst", [128, 20], F32)
            epsb = sb("eps3", [128, 1], F32)
            gq = sb("gq", [128, 128], F32)
            gk = sb("gk", [128, 3, 128], F32)
            idf = sb("idf3", [128, 128], F32)
            idb = sb("idb3", [128, 128], BF16)
            tst = [sb(f"tst{i}", [128, 16, 128], BF16) for i in range(2)]
            psA = ph.enter_context(nc.psum_tensor(f"p3a_{l}", [128, 1024], BF16))
            psB = ph.enter_context(nc.psum_tensor(f"p3b_{l}", [128, 1024], BF16))
            dma("sp", idf[:], Cd["ident"].ap(), writes=["idf"])
            S.add("dve", I("tensor_copy", out=idb[:], in_=idf[:]), reads=["idf"], writes=["idb"])
            S.add("dve", I("memset", epsb[:], EPS), writes=["eps"])
            dma("sp", gq[:], bc_row(Wd["qk_gain"], woff["qk_gain"], 128), writes=["gq"])
            dma("sp", gk[:], AP(Wd["qk_gain"], woff["qk_gain"] + 128, [[0, 128], [128, 3], [1, 128]]), writes=["gk"])
            S.add("act", I("mul", out=gq[:], in_=gq[:], mul=128.0 ** -0.5), reads=["gq"], writes=["gq"])

            def p3_load(t):
                s = t % 2
                dma("sp", zt[s][:], Z.ap()[t * 128:(t + 1) * 128, 0:2560], writes=[("zt", s)])
                dma("sp", rp[s][:], Cd["rope"].ap()[t * 128:(t + 1) * 128], writes=[("rp", s)])

            p3_load(0)
            kheads = [(8, 2), (12, 2), (16, 2)]
            for t in range(NTT):
                s = t % 2
                if t + 1 < NTT:
                    p3_load(t + 1)
                z = zt[s]
                S.add("act", I("activation", out=sq[:], in_=z[:], func=AF.Square), reads=[("zt", s)], writes=["sq"])
                S.add("dve", I("tensor_reduce", out=ssq[:], in_=AP(sq, 0, [[2560, 128], [128, 20], [1, 128]]),
                                                       axis=AX.X, op=ALU.add), reads=["sq"], writes=["ssq"])
                S.add("act", I("activation", out=rst[:], in_=ssq[:], func=AF.Sqrt, bias=epsb[:], scale=1.0 / 128),
                      reads=["ssq", "eps"], writes=["rst"])
                S.add("dve", I("reciprocal", out=rst[:], in_=rst[:]), reads=["rst"], writes=["rst"])
                groups = [(0, 8, AP(gq, 0, [[128, 128], [0, 8], [1, 128]]))]
                for bi, (h0, nh) in enumerate(kheads):
                    groups.append((h0, nh, AP(gk, bi * 128, [[384, 128], [0, nh], [1, 128]])))
                for (h0, nh, gap) in groups:
                    zv = AP(z, h0 * 128, [[2560, 128], [128, nh], [1, 128]])
                    rv = AP(rst, h0, [[20, 128], [1, nh], [0, 128]])
                    S.add("dve", I("tensor_tensor", out=zv, in0=zv, in1=rv, op=ALU.mult),
                          reads=[("zt", s), "rst"], writes=[("zt", s)])
                    S.add("dve", I("tensor_tensor", out=zv, in0=zv, in1=gap, op=ALU.mult),
                          reads=[("zt", s), "gq", "gk"], writes=[("zt", s)])
                    lo = AP(z, h0 * 128, [[2560, 128], [128, nh], [1, 64]])
                    hi = AP(z, h0 * 128 + 64, [[2560, 128], [128, nh], [1, 64]])
                    tlo = AP(tmp, 0, [[1024, 128], [128, nh], [1, 64]])
                    thi = AP(tmp, 64, [[1024, 128], [128, nh], [1, 64]])
                    tv = AP(tmp, 0, [[1024, 128], [128, nh], [1, 128]])
                    sslo = AP(rp[s], 128, [[256, 128], [0, nh], [1, 64]])
                    sshi = AP(rp[s], 192, [[256, 128], [0, nh], [1, 64]])
                    cc = AP(rp[s], 0, [[256, 128], [0, nh], [1, 128]])
                    S.add("dve", I("tensor_tensor", out=tlo, in0=hi, in1=sslo, op=ALU.mult),
                          reads=[("zt", s), ("rp", s)], writes=["tmp"])
                    S.add("dve", I("tensor_tensor", out=thi, in0=lo, in1=sshi, op=ALU.mult),
                          reads=[("zt", s), ("rp", s)], writes=["tmp"])
                    S.add("dve", I("tensor_tensor", out=zv, in0=zv, in1=cc, op=ALU.mult),
                          reads=[("zt", s), ("rp", s), "tmp"], writes=[("zt", s)])
                    S.add("dve", I("tensor_tensor", out=zv, in0=zv, in1=tv, op=ALU.add),
                          reads=[("zt", s), "tmp"], writes=[("zt", s)])
                S.add("act", I("copy", out=zb[s][:], in_=z[:]), reads=[("zt", s)], writes=[("zb", s)])
                if t < NT:
                    dma("sp", kv_p.ap()[l, t * 128:(t + 1) * 128, :], z[:, 1024:2048], reads=[("zt", s)])
                    if t >= NT - NW:
                        r0 = (t - (NT - NW)) * 128
                        dma("sp", win_p.ap()[l, r0:r0 + 128, :], z[:, 2048:2560], reads=[("zt", s)])
                else:
                    dma("sp", SM["kv_s"].ap()[l], z[0:4, 1024:2048], reads=[("zt", s)])
                    dma("sp", SM["win_s"].ap()[l, 508:512, :], z[0:4, 2048:2560], reads=[("zt", s)])
                    for wq in range(4):
                        dma("sp", sq[0:127, 0:512], SM["cwin"].ap()[l, 4 + wq * 127:4 + (wq + 1) * 127, :], reads=["sq"], writes=["sq"])
                        dma("sp", SM["win_s"].ap()[l, wq * 127:(wq + 1) * 127, :], sq[0:127, 0:512], reads=["sq"], writes=["sq"])
                heads = list(range(8)) + [8, 9, 10, 11, 12, 13, 16, 17]
                for j, h in enumerate(heads):
                    pt = psA if j < 8 else psB
                    jj = j % 8
                    S.add("pe", I("transpose",
                        out=pt[:, jj * 128:(jj + 1) * 128], in_=zb[s][:, h * 128:(h + 1) * 128], identity=idb[:]),
                        reads=[("zb", s), "idb"], writes=[("pt", j // 8)])
                S.add("act", I("copy", out=tst[s][:, 0:8, :], in_=AP(psA, 0, [[1024, 128], [128, 8], [1, 128]])),
                      reads=[("pt", 0)], writes=[("tstq", s)])
                S.add("dve", I("tensor_copy", out=tst[s][:, 8:16, :], in_=AP(psB, 0, [[1024, 128], [128, 8], [1, 128]])),
                      reads=[("pt", 1)], writes=[("tstk", s)])
                dma("sp", AP(QT, t * 128, [[TT, 128], [128 * TT, 8], [1, 128]]), tst[s][:, 0:8, :], reads=[("tstq", s)])
                dma("sp", AP(KTs, t * 128, [[TT, 128], [128 * TT, 8], [1, 128]]), tst[s][:, 8:16, :], reads=[("tstk", s)])
                dma("sp", VV.ap()[t * 128:(t + 1) * 128, 0:2, :], AP(zb[s], 1792, [[2560, 128], [128, 2], [1, 128]]), reads=[("zb", s)])
                dma("sp", VV.ap()[t * 128:(t + 1) * 128, 2:4, :], AP(zb[s], 2304, [[2560, 128], [128, 2], [1, 128]]), reads=[("zb", s)])
            S.emit()

        build_rest(nc, S, dict(l=l, T=T, NT=NT, TT=TT, NTT=NTT, SM=SM, n_cmp=n_cmp, L=L, Wd=Wd, woff=woff, Cd=Cd, Z=Z, Z2=Z2, QT=QT, KTs=KTs, VV=VV,
                               BRT=BRT, MT=MT, x_src=x_src, x_dst=x_dst, pool_p=pool_p, conv_p=conv_p, ssm_p=ssm_p,
                               dma=dma, bc_row=bc_row, dbgK=dbgK, dbgV=dbgV, dbgO=dbgO, dbgE=dbgE, dbgN=dbgN))
    es.close()
    return nc


def build_rest(nc, S, env):
    l, T, NT, n_cmp, L = env["l"], env["T"], env["NT"], env["n_cmp"], env["L"]
    TT, NTT, SM = env["TT"], env["NTT"], env["SM"]
    Wd, woff, Cd = env["Wd"], env["woff"], env["Cd"]
    Z, Z2, QT, KTs, VV, BRT, MT = env["Z"], env["Z2"], env["QT"], env["KTs"], env["VV"], env["BRT"], env["MT"]
    x_src, x_dst = env["x_src"], env["x_dst"]
    pool_p, conv_p, ssm_p = env["pool_p"], env["conv_p"], env["ssm_p"]
    dma, bc_row = env["dma"], env["bc_row"]
    NIT = (n_cmp + 127) // 128

    with ExitStack() as att:
        sba = lambda n, s, d: att.enter_context(nc.sbuf_tensor(f"{n}_{l}", s, d))
        KcT = sba("KcT", [128, 2, 256], BF16)
        Vaug = sba("Vaug", [128, 2, 2, 193], BF16)
        idf = sba("idf4", [128, 128], F32)
        idb = sba("idb4", [128, 128], BF16)
        with ExitStack() as ph:
            sb = lambda n, s, d: ph.enter_context(nc.sbuf_tensor(f"{n}_{l}", s, d))
            XT = sb("XT", [128, 4, T], BF16)
            Wc = sb("Wc", [128, 2, 32, 128], BF16)
            pe_f = sb("pe_f", [64, 128], F32)
            pe_b = sb("pe_b", [64, 128], BF16)
            peT = sb("peT", [128, 64], BF16)
            VcT = sb("VcT", [128, 2, 256], BF16)
            bias = sb("bias", [128, 2], F32)
            pc = [ph.enter_context(nc.psum_tensor(f"pc{i}_{l}", [128, 512], F32)) for i in range(2)]
            pbt = ph.enter_context(nc.psum_tensor(f"pbt_{l}", [128, 1024], BF16))
            dma("sp", idf[:], Cd["ident"].ap(), writes=["idf"])
            S.add("dve", I("tensor_copy", out=idb[:], in_=idf[:]), reads=["idf"], writes=["idb"])
            dma("sp", XT[:], AP(KTs, 0, [[TT, 128], [128 * TT, 4], [1, T]]), writes=["XT"])
            dma("pool", Wc[:], AP(Wd["cmp_w"], woff["cmp_w"], [[128, 128], [4096 * 128, 2], [128 * 128, 32], [1, 128]]), writes=["Wc"])
            dma("sp", pe_f[:], AP(Wd["cmp_pe"], woff["cmp_pe"], [[128, 64], [1, 128]]), writes=["pe_f"])
            S.add("dve", I("tensor_copy", out=pe_b[:], in_=pe_f[:]), reads=["pe_f"], writes=["pe_b"])
            S.add("pe", I("transpose", out=pbt[:, 0:64], in_=pe_b[:], identity=idb[0:64, 0:64]), reads=["pe_b", "idb"], writes=["pbt"])
            S.add("dve", I("tensor_copy", out=peT[:], in_=pbt[:, 0:64]), reads=["pbt"], writes=["peT"])
            S.add("pool", I("memset", Vaug[:], 0.0), writes=["Vaug"])
            S.add("pool", I("memset", KcT[:], 0.0), writes=["KcT"])
            S.add("pool", I("memset", VcT[:], 0.0), writes=["VcT"])
            S.add("pool", I("memset", AP(Vaug, 128, [[772, 128], [193, 4], [1, 1]]), 1.0), writes=["Vaug"])
            for g in range(2):
                dma("pool", AP(Vaug, g * 193 + 129, [[772, 128], [386, 2], [1, 64]]),
                    AP(Cd["overlap"], 0, [[128, 128], [64, 2], [1, 64]]), writes=["Vaug"])
            for c in range(2):
                for ll in range(32):
                    S.add("pe", I("matmul", pc[0][:, c:c + 1], lhsT=Wc[:, c, ll, :], rhs=peT[:, c * 32 + ll:c * 32 + ll + 1],
                                                                start=(ll == 0), stop=(ll == 31)), reads=["Wc", "peT"], writes=[("pc", 0)])
            S.add("dve", I("tensor_copy", out=bias[:], in_=pc[0][:, 0:2]), reads=[("pc", 0)], writes=["bias"])
            n = 1
            for c in range(2):
                for g in range(2):
                    b = n % 2
                    n += 1
                    for ll in range(32):
                        S.add("pe", I("matmul",
                            pc[b][:, 0:n_cmp], lhsT=Wc[:, c, ll, :], rhs=AP(XT, (c * 2 + g) * T + ll, [[4 * T, 128], [16, n_cmp]]),
                            start=(ll == 0), stop=(ll == 31)), reads=["Wc", "XT"], writes=[("pc", b)])
                    dst = KcT if c == 0 else VcT
                    S.add("act", I("activation",
                        out=dst[:, g, 0:n_cmp], in_=pc[b][:, 0:n_cmp], func=AF.Identity, bias=bias[:, c:c + 1]),
                        reads=[("pc", b), "bias"], writes=["KcT" if c == 0 else "VcT"])
            for g in range(2):
                for it in range(NIT):
                    S.add("pe", I("transpose", out=pbt[:, 128:256], in_=VcT[:, g, it * 128:(it + 1) * 128], identity=idb[:]),
                          reads=["VcT", "idb"], writes=["pbt"])
                    S.add("dve", I("tensor_copy", out=Vaug[:, it, g, 0:128], in_=pbt[:, 128:256]),
                          reads=["pbt"], writes=["Vaug"])
            if env.get("dbgK") is not None:
                dma("sp", env["dbgK"].ap(), AP(KcT, 0, [[512, 128], [1, 512]]), reads=["KcT"])
                dma("sp", env["dbgV"].ap(), AP(Vaug, 0, [[772, 128], [1, 772]]), reads=["Vaug"])
            S.emit()

        with ExitStack() as ph:
            sb = lambda n, s, d: ph.enter_context(nc.sbuf_tensor(f"{n}_{l}", s, d))
            KsT = sb("KsT", [128, 2, T], BF16)
            KwT = sb("KwT", [128, 2, T], BF16)
            Vs = sb("Vs", [128, NT, 2, 129], BF16)
            Vw = sb("Vw", [128, NT, 2, 129], BF16)
            E2 = sb("E2", [128, T], BF16)
            cmk = sb("cmk", [128, NT * 256], BF16)
            bon = sb("bon", [128, NT * 64], F32)
            tri = sb("tri", [128, 128], BF16)
            anti = sb("anti", [128, 128], BF16)
            qsb = [sb(f"qsb{i}", [128, 8, 128], BF16) for i in range(2)]
            gt = [sb(f"gt{i}", [128, 24], F32) for i in range(2)]
            nz = [sb(f"nz{i}", [128, 1024], F32) for i in range(2)]
            nsa = sb("nsa", [128, 8, 128], F32)
            ec = [sb(f"ec{i}", [128, 4, 128], BF16) for i in range(2)]
            esb = [sb(f"es{i}", [128, 4, 128], BF16) for i in range(2)]
            em = [sb(f"em{i}", [128, 4, 128], BF16) for i in range(2)]
            ew = [sb(f"ew{i}", [128, 4, 128], BF16) for i in range(2)]
            rc = sb("rc", [128, 4], F32)
            coef = sb("coef", [128, 4], F32)
            imp = sb("imp", [128, 64], F32)
            imp2 = sb("imp2", [128, 64], F32)
            m8a = sb("m8a", [128, 8], F32)
            m8b = sb("m8b", [128, 8], F32)
            sel = sb("sel", [128, 64], F32)
            selT = sb("selT", [64, 128], BF16)
            brs = [sb(f"brs{i}", [128, 8, 128], BF16) for i in range(2)]
            psS = [ph.enter_context(nc.psum_tensor(f"psS{i}_{l}", [128, 512], F32)) for i in range(2)]
            psX = ph.enter_context(nc.psum_tensor(f"psX_{l}", [128, 512], F32))
            psO = [ph.enter_context(nc.psum_tensor(f"psO{i}_{l}", [128, 512], F32)) for i in range(4)]
            psX2 = ph.enter_context(nc.psum_tensor(f"psX2_{l}", [128, 512], F32))
            psXm = [psX, psX2]
            dma("sp", KsT[:], AP(KTs, 4 * 128 * TT, [[TT, 128], [128 * TT, 2], [1, T]]), writes=["KsT"])
            dma("sp", KwT[:], AP(KTs, 6 * 128 * TT, [[TT, 128], [128 * TT, 2], [1, T]]), writes=["KwT"])
            S.add("pool", I("memset", AP(Vs, 128, [[NT * 258, 128], [129, NT * 2], [1, 1]]), 1.0), writes=["Vs1"])
            S.add("pool", I("memset", AP(Vw, 128, [[NT * 258, 128], [129, NT * 2], [1, 1]]), 1.0), writes=["Vw1"])
            for g in range(2):
                dma("sp", AP(Vs, g * 129, [[NT * 258, 128], [258, NT], [1, 128]]), AP(VV, g * 128, [[512, 128], [128 * 512, NT], [1, 128]]), writes=["Vs"])
                dma("sp", AP(Vw, g * 129, [[NT * 258, 128], [258, NT], [1, 128]]), AP(VV, (2 + g) * 128, [[512, 128], [128 * 512, NT], [1, 128]]), writes=["Vw"])
            S.add("pool", I("memset", E2[64:128, :], 0.0), writes=["E2"])
            dma("pool", E2[0:64, :], Cd["e2"].ap(), writes=["E2"])
            dma("pool", cmk[:], Cd["cmpmask"].ap(), writes=["cmk"])
            dma("sp", bon[:], Cd["bonus"].ap(), writes=["bon"])
            dma("pool", tri[:], Cd["tri"].ap(), writes=["tri"])
            dma("pool", anti[:], Cd["anti"].ap(), writes=["anti"])
            VR = ["Vs", "Vs1", "Vw", "Vw1"]

            def p5_load(qt):
                s = qt % 2
                dma("sp", qsb[s][:], AP(QT, qt * 128, [[TT, 128], [128 * TT, 8], [1, 128]]), writes=[("qsb", s)])
                dma("sp", gt[s][:], Z.ap()[qt * 128:(qt + 1) * 128, C_G:C_G + 24], writes=[("gt", s)])
                dma("sp", nz[s][:], Z.ap()[qt * 128:(qt + 1) * 128, C_NZ:C_NZ + 1024], writes=[("nz", s)])

            def scores(kT, g, kt, s, b, nrow=128, stop=True):
                S.add("pe", I("matmul", psS[b][0:nrow, :], lhsT=kT[:, g, kt * 128:kt * 128 + nrow],
                                               rhs=AP(qsb[s], 4 * g * 128, [[1024, 128], [1, 512]]), start=True, stop=stop),
                      reads=[("qsb", s), "KsT", "KwT", "KcT"], writes=[("psS", b)])

            selTb = sb("selTb", [128, 4, 128], BF16)
            S.add("pool", I("memset", selTb[:], 0.0), writes=["selT"])
            antiB = sb("antiB", [128, 4, 128], BF16)
            S.add("act", I("activation", out=antiB[:], in_=AP(anti, 0, [[128, 128], [0, 4], [1, 128]]), func=AF.Identity, scale=-30000.0),
                  reads=["anti"], writes=["antiB"])

            def accsel(branch, h):
                if branch == "slc":
                    return psO[h // 2], (h % 2) * 129, ("psO", h // 2)
                if branch == "win":
                    return psO[2 + h // 2], (h % 2) * 129, ("psO", 2 + h // 2)
                return psO[2 + h // 2], (h % 2) * 193, ("psO", 2 + h // 2)

            def pv(src, Vt, kt, g, first, last, width, branch, nrow=128, vap=None):
                for h in range(4):
                    rhs = vap if vap is not None else Vt[:, kt, g, :]
                    bank, off, res = accsel(branch, h)
                    S.add("pe", I("matmul", bank[:, off:off + width], lhsT=src[0:nrow, h, :], rhs=rhs,
                                  start=(first and h % 2 == 0), stop=last, skip_group_check=True),
                          reads=[src.name] + VR + ["Vaug"], writes=[res])

            def finish(gcol, g, s, first, branch):
                for h in range(4):
                    bank, off, res = accsel(branch, h)
                    S.add("dve", I("tensor_scalar", out=rc2[branch][:, h:h + 1], in0=bank[:, off + 128:off + 129], scalar1=1e-30, scalar2=None,
                                   op0=ALU.max), reads=[res], writes=[("rc", branch)])
                S.add("dve", I("reciprocal", out=rc2[branch][:], in_=rc2[branch][:]), reads=[("rc", branch)], writes=[("rc", branch)])
                S.add("dve", I("tensor_tensor", out=cf2[branch][:], in0=rc2[branch][:], in1=AP(gt[s], 12 * g + gcol, [[24, 128], [3, 4]]), op=ALU.mult),
                      reads=[("rc", branch), ("gt", s)], writes=[("coef", branch)])
                for h in range(4):
                    hh = 4 * g + h
                    bank, off, res = accsel(branch, h)
                    S.add("dve", I("scalar_tensor_tensor", out=nsa[:, hh, :], in0=bank[:, off:off + 128], scalar=cf2[branch][:, h:h + 1],
                                   in1=nsa[:, hh, :], op0=ALU.mult, op1=ALU.add),
                          reads=[res, ("coef", branch), ("nsa", hh)], writes=[("nsa", hh)])

            rc2 = {"slc": sb("rc_s", [128, 4], F32), "win": sb("rc_w", [128, 4], F32)}
            cf2 = {"slc": sb("cf_s", [128, 4], F32), "win": sb("cf_w", [128, 4], F32)}

            p5_load(0)
            for qt in range(NT):
                s = qt % 2
                if qt + 1 < NT:
                    p5_load(qt + 1)
                S.add("act", I("activation", out=gt[s][:], in_=gt[s][:], func=AF.Sigmoid), reads=[("gt", s)], writes=[("gt", s)])
                S.add("act", I("activation", out=nz[s][:], in_=nz[s][:], func=AF.Silu), reads=[("nz", s)], writes=[("nz", s)])
                for g in range(2):
                    its = [it for it in range(NIT) if 128 * it <= 8 * qt + 6]
                    for it in its:
                        ni = 128
                        scores(KcT, g, it, s, it, nrow=ni)
                        S.add("act", I("activation", out=ec[it][0:ni], in_=psS[it][0:ni, :], func=AF.Exp),
                              reads=[("psS", it)], writes=[ec[it].name])
                        S.add("dve", I("tensor_tensor",
                            out=ec[it][0:ni], in0=ec[it][0:ni], in1=AP(cmk, (qt * 2 + it) * 128, [[NT * 256, ni], [0, 4], [1, 128]]), op=ALU.mult),
                            reads=[ec[it].name, "cmk"], writes=[ec[it].name])
                    for h in range(4):
                        for k_, it in enumerate(its):
                            ni = 128
                            bank, off, res = accsel("cmp", h)
                            S.add("pe", I("matmul", bank[:, off:off + 193], lhsT=ec[it][0:ni, h, :], rhs=Vaug[0:ni, it, g, :],
                                          start=(k_ == 0 and h % 2 == 0), stop=(k_ == len(its) - 1), skip_group_check=True),
                                  reads=[ec[it].name, "Vaug"], writes=[res])
                    for h in range(4):
                        bank, off, res = accsel("cmp", h)
                        S.add("dve", I("tensor_scalar", out=rc[:, h:h + 1], in0=bank[:, off + 128:off + 129], scalar1=1e-30, scalar2=None,
                                       op0=ALU.max), reads=[res], writes=["rc"])
                    S.add("dve", I("reciprocal", out=rc[:], in_=rc[:]), reads=["rc"], writes=["rc"])
                    for h in range(4):
                        bank, off, res = accsel("cmp", h)
                        if h == 0:
                            S.add("dve", I("tensor_scalar", out=imp[:], in0=bank[:, off + 129:off + 193], scalar1=rc[:, 0:1], scalar2=None, op0=ALU.mult),
                                  reads=[res, "rc"], writes=["imp"])
                        else:
                            S.add("dve", I("scalar_tensor_tensor", out=imp[:], in0=bank[:, off + 129:off + 193], scalar=rc[:, h:h + 1], in1=imp[:],
                                           op0=ALU.mult, op1=ALU.add), reads=[res, "rc", "imp"], writes=["imp"])
                    S.add("dve", I("tensor_tensor", out=imp[:], in0=imp[:], in1=bon[:, qt * 64:(qt + 1) * 64], op=ALU.add),
                          reads=["imp", "bon"], writes=["imp"])
                    S.add("dve", I("tensor_tensor", out=coef[:], in0=rc[:], in1=AP(gt[s], 12 * g + 0, [[24, 128], [3, 4]]), op=ALU.mult),
                          reads=["rc", ("gt", s)], writes=["coef"])
                    for h in range(4):
                        hh = 4 * g + h
                        bank, off, res = accsel("cmp", h)
                        S.add("dve", I("tensor_scalar", out=nsa[:, hh, :], in0=bank[:, off:off + 128], scalar1=coef[:, h:h + 1],
                                       scalar2=None, op0=ALU.mult), reads=[res, "coef"], writes=[("nsa", hh)])
                    if False and env.get("dbgK") is not None and (qt, g) == env.get("dbgsel", (1, 0)):
                        dbo = sb("dbo", [128, 193 + 8 + 64], F32)
                        S.add("dve", I("tensor_copy", out=dbo[:, 0:193], in_=psO[0][:, 0:193]), reads=[("psO", 0)], writes=["dbo"])
                        S.add("dve", I("tensor_copy", out=dbo[:, 193:197], in_=rc[:]), reads=["rc"], writes=["dbo"])
                        S.add("dve", I("tensor_copy", out=dbo[:, 197:201], in_=coef[:]), reads=["coef"], writes=["dbo"])
                        S.add("dve", I("tensor_copy", out=dbo[:, 201:265], in_=imp[:]), reads=["imp"], writes=["dbo"])
                        dma("sp", env["dbgO"].ap(), dbo[:], reads=["dbo"])
                        dma("sp", env["dbgE"].ap(), AP(ec[0], 0, [[512, 128], [1, 512]]), reads=[ec[0].name])
                        dma("sp", env["dbgN"].ap(), AP(nsa, 0, [[1024, 128], [1, 1024]]), reads=[("nsa", h) for h in range(8)])
                    S.add("dve", I("max", out=m8a[:], in_=imp[:]), reads=["imp"], writes=["m8a"])
                    S.add("dve", I("match_replace", out=imp2[:], in_to_replace=m8a[:], in_values=imp[:], imm_value=NEG),
                          reads=["imp", "m8a"], writes=["imp2"])
                    S.add("dve", I("max", out=m8b[:], in_=imp2[:]), reads=["imp2"], writes=["m8b"])
                    S.add("dve", I("tensor_scalar", out=sel[:], in0=imp[:], scalar1=m8b[:, 7:8], scalar2=None, op0=ALU.is_ge),
                          reads=["imp", "m8b"], writes=["sel"])
                    S.add("pe", I("transpose", out=psX[0:64, 128:256], in_=sel[:], identity=idf[:]), reads=["sel", "idf"], writes=[("psX", 0)])
                    S.add("act", I("activation", out=selTb[0:64], in_=AP(psX, 128, [[512, 64], [0, 4], [1, 128]]), func=AF.Identity,
                                   scale=30000.0, bias=-30000.0), reads=[("psX", 0)], writes=["selT"])
                    def slc_front(kt):
                        b = kt % 2
                        scores(KsT, g, kt, s, b, stop=False)
                        S.add("pe", I("matmul", psS[b][:], lhsT=E2[:, kt * 128:(kt + 1) * 128], rhs=AP(selTb, 0, [[512, 128], [1, 512]]),
                                      start=False, stop=(kt != qt)), reads=["E2", "selT"], writes=[("psS", b)])
                        if kt == qt:
                            S.add("pe", I("matmul", psS[b][:], lhsT=idb[:], rhs=AP(antiB, 0, [[512, 128], [1, 512]]), start=False, stop=True),
                                  reads=["idb", "antiB"], writes=[("psS", b)])
                        S.add("act", I("activation", out=esb[b][:], in_=psS[b][:], func=AF.Exp), reads=[("psS", b)], writes=[esb[b].name])

                    def slc_back(kt):
                        pv(esb[kt % 2], Vs, kt, g, kt == 0, kt == qt, 129, "slc")

                    k0 = max(0, qt - 4)

                    def win_front(kt):
                        b = kt % 2
                        scores(KwT, g, kt, s, b)
                        S.add("act", I("activation", out=ew[b][:], in_=psS[b][:], func=AF.Exp), reads=[("psS", b)], writes=[ew[b].name])
                        if kt == qt or kt == qt - 4:
                            mk = tri if kt == qt else anti
                            S.add("pool", I("tensor_tensor", out=ew[b][:], in0=ew[b][:], in1=AP(mk, 0, [[128, 128], [0, 4], [1, 128]]), op=ALU.mult),
                                  reads=[ew[b].name, "tri", "anti"], writes=[ew[b].name])

                    slc_front(0)
                    for kt in range(qt + 1):
                        if kt + 1 <= qt:
                            slc_front(kt + 1)
                        slc_back(kt)
                    win_front(k0)
                    finish(1, g, s, False, "slc")
                    for kt in range(k0, qt + 1):
                        if kt + 1 <= qt:
                            win_front(kt + 1)
                        pv(ew[kt % 2], Vw, kt, g, kt == k0, kt == qt, 129, "win")
                    finish(2, g, s, False, "win")
                S.add("dve", I("tensor_tensor", out=nsa[:], in0=nsa[:], in1=AP(nz[s], 0, [[1024, 128], [128, 8], [1, 128]]), op=ALU.mult),
                      reads=[("nsa", h) for h in range(8)] + [("nz", s)], writes=[("nsa", h) for h in range(8)])
                for half in range(2):
                    for j in range(4):
                        hh = half * 4 + j
                        S.add("pe", I("transpose", out=psS[half][:, j * 128:(j + 1) * 128], in_=nsa[:, hh, :], identity=idf[:]),
                              reads=[("nsa", hh), "idf"], writes=[("psS", half)])
                    if half == 0:
                        S.add("act", I("copy", out=brs[s][:, 0:4, :], in_=AP(psS[0], 0, [[512, 128], [128, 4], [1, 128]])),
                              reads=[("psS", 0)], writes=[("brs", s, 0)])
                    else:
                        S.add("dve", I("tensor_copy", out=brs[s][:, 4:8, :], in_=AP(psS[1], 0, [[512, 128], [128, 4], [1, 128]])),
                              reads=[("psS", 1)], writes=[("brs", s, 1)])
                dma("sp", AP(BRT, qt * 393216, [[3072, 128], [128, 8], [1, 128]]), brs[s][:], reads=[("brs", s, 0), ("brs", s, 1)])
            S.emit()

    with ExitStack() as ph:
        sb = lambda n, s, d: ph.enter_context(nc.sbuf_tensor(f"{n}_{l}", s, d))
        I32, NPOOL, cache = SM["I32"], SM["NPOOL"], SM["cache"]
        idf = sb("idfS", [128, 128], F32)
        idb = sb("idbS", [128, 128], BF16)
        ptb = sb("ptb", [128, 128], I32)
        rio = sb("rio", [128, 2], F32)
        idx2 = sb("idx2", [128, 2, 128], I32)
        Wc = sb("WcS", [128, 2, 32, 128], BF16)
        pe_f = sb("pe_fS", [64, 128], F32)
        pe_b = sb("pe_bS", [64, 128], BF16)
        peT = sb("peTS", [128, 64], BF16)
        bias = sb("biasS", [128, 2], F32)
        XT = sb("XTS", [128, 4, 4112], BF16)
        pg = [sb(f"pg{i}", [128, 4, 128], F32) for i in range(2)]
        pgb = [sb(f"pgb{i}", [128, 4, 128], BF16) for i in range(2)]
        KcT = sb("KcTS", [128, 2, 1024], BF16)
        VcT = sb("VcTS", [128, 2, 1024], BF16)
        Vaug = sb("VaugS", [128, 8, 2, 386], BF16)
        E2b = sb("E2b", [128, 8192], BF16)
        bons = sb("bons", [128, 257], F32)
        tri = sb("triS", [128, 128], BF16)
        anti = sb("antiS", [128, 128], BF16)
        qs = sb("qs", [128, 8, 128], BF16)
        gt = sb("gtS", [128, 24], F32)
        nz = sb("nzS", [128, 1024], F32)
        nsa = sb("nsaS", [128, 8, 128], F32)
        ecs = [sb(f"ecs{i}", [128, 4, 128], BF16) for i in range(2)]
        esb = [sb(f"esS{i}", [128, 4, 128], BF16) for i in range(2)]
        em = [sb(f"emS{i}", [128, 4, 128], BF16) for i in range(2)]
        kg = [sb(f"kg{i}", [128, 4, 128], F32) for i in range(4)]
        kgb = [sb(f"kgb{i}", [128, 4, 128], BF16) for i in range(2)]
        kTt = [sb(f"kTt{i}", [128, 128], BF16) for i in range(2)]
        vat = [sb(f"vat{i}", [128, 129], BF16) for i in range(2)]
        rc = sb("rcS", [128, 4], F32)
        coef = sb("coefS", [128, 4], F32)
        imp = sb("impS", [128, 257], F32)
        imp2 = sb("imp2S", [128, 257], F32)
        m8a = sb("m8aS", [128, 8], F32)
        m8b = sb("m8bS", [128, 8], F32)
        sel = sb("selS", [128, 384], F32)
        selT = sb("selTS", [128, 3, 128], BF16)
        brs = sb("brsS", [128, 8, 128], BF16)
        psS = [ph.enter_context(nc.psum_tensor(f"psSs{i}_{l}", [128, 512], F32)) for i in range(2)]
        psX = ph.enter_context(nc.psum_tensor(f"psXs_{l}", [128, 512], F32))
        psO = [ph.enter_context(nc.psum_tensor(f"psOs{i}_{l}", [128, 512], F32)) for i in range(4)]
        psT = ph.enter_context(nc.psum_tensor(f"psTs_{l}", [128, 1024], BF16))
        dma("sp", idf[:], Cd["ident"].ap(), writes=["idf"])
        S.add("dve", I("tensor_copy", out=idb[:], in_=idf[:]), reads=["idf"], writes=["idb"])
        dma("sp", ptb[:], AP(SM["pt"], 0, [[0, 128], [1, 128]]), writes=["ptb"])
        dma("sp", rio[:], SM["riota"].ap(), writes=["rio"])
        for half in range(2):
            S.add("dve", I("tensor_scalar", out=idx2[:, half, :], in0=ptb[:], scalar1=256.0, scalar2=rio[:, 1:2], op0=ALU.mult, op1=ALU.add),
                  reads=["ptb", "rio"], writes=["idx"])
            S.add("dve", I("tensor_scalar", out=idx2[:, half, :], in0=idx2[:, half, :], scalar1=float(2 * l * NPOOL * 128 + half), scalar2=None, op0=ALU.add),
                  reads=["idx"], writes=["idx"])
        dma("pool", Wc[:], AP(Wd["cmp_w"], woff["cmp_w"], [[128, 128], [4096 * 128, 2], [128 * 128, 32], [1, 128]]), writes=["Wc"])
        dma("sp", pe_f[:], AP(Wd["cmp_pe"], woff["cmp_pe"], [[128, 64], [1, 128]]), writes=["pe_f"])
        S.add("dve", I("tensor_copy", out=pe_b[:], in_=pe_f[:]), reads=["pe_f"], writes=["pe_b"])
        S.add("pe", I("transpose", out=psT[:, 0:64], in_=pe_b[:], identity=idb[0:64, 0:64]), reads=["pe_b", "idb"], writes=["psT"])
        S.add("dve", I("tensor_copy", out=peT[:], in_=psT[:, 0:64]), reads=["psT"], writes=["peT"])
        for c in range(2):
            for ll in range(32):
                S.add("pe", I("matmul", psX[:, c:c + 1], lhsT=Wc[:, c, ll, :], rhs=peT[:, c * 32 + ll:c * 32 + ll + 1],
                              start=(ll == 0), stop=(ll == 31)), reads=["Wc", "peT"], writes=["psX"])
        S.add("dve", I("tensor_copy", out=bias[:], in_=psX[:, 0:2]), reads=["psX"], writes=["bias"])
        S.add("pool", I("memset", XT[:], 0.0), writes=["XT"])
        S.add("pool", I("memset", KcT[:], 0.0), writes=["KcT"])
        S.add("pool", I("memset", VcT[:], 0.0), writes=["VcT"])
        S.add("pool", I("memset", Vaug[:], 0.0), writes=["Vaug"])
        S.add("pool", I("memset", AP(Vaug, 128, [[6176, 128], [386, 16], [1, 1]]), 1.0), writes=["Vaug"])
        S.add("pool", I("memset", sel[:], 0.0), writes=["sel"])
        for i in range(2):
            S.add("pool", I("memset", vat[i][:, 128:129], 1.0), writes=[("vat1", i)])
        for g in range(2):
            dma("pool", AP(Vaug, g * 386 + 129, [[6176, 128], [772, 8], [1, 257]]), AP(SM["ovs"], 0, [[2056, 128], [257, 8], [1, 257]]), writes=["Vaug"])
        dma("pool", E2b[:], SM["e2big"].ap(), writes=["E2b"])
        dma("sp", bons[:], SM["bons"].ap(), writes=["bons"])
        dma("pool", tri[:], Cd["tri"].ap(), writes=["tri"])
        dma("pool", anti[:], Cd["anti"].ap(), writes=["anti"])
        dma("sp", qs[:], AP(QT, T, [[TT, 128], [128 * TT, 8], [1, 128]]), writes=["qs"])
        dma("sp", gt[:], Z.ap()[T:T + 128, C_G:C_G + 24], writes=["gt"])
        dma("sp", nz[:], Z.ap()[T:T + 128, C_NZ:C_NZ + 1024], writes=["nz"])
        S.add("act", I("activation", out=gt[:], in_=gt[:], func=AF.Sigmoid), reads=["gt"], writes=["gt"])
        S.add("act", I("activation", out=nz[:], in_=nz[:], func=AF.Silu), reads=["nz"], writes=["nz"])

        cache2 = AP(cache, 0, [[512, L * NPOOL * 128 * 2], [1, 512]])

        def gather(dst, half, page, wr):
            S.add("pool", I("indirect_dma_start", out=dst, out_offset=None, in_=cache2,
                            in_offset=bass.IndirectOffsetOnAxis(ap=idx2[:, half, page:page + 1], axis=0)),
                  reads=["idx"], writes=[wr], dma=True)

        nb = 0
        for qd in range(4):
            if qd > 0:
                S.add("dve", I("tensor_copy", out=XT[:, :, 0:16], in_=XT[:, :, 4096:4112]), reads=["XT"], writes=["XT"])
            for pp in range(32):
                page = qd * 32 + pp
                s = page % 2
                gather(AP(pg[s], 0, [[512, 128], [1, 512]]), 0, page, ("pg", s))
                S.add("act", I("copy", out=pgb[s][:], in_=pg[s][:]), reads=[("pg", s)], writes=[("pgb", s)])
                for slot in range(4):
                    S.add("pe", I("transpose", out=psT[:, slot * 128:(slot + 1) * 128], in_=pgb[s][:, slot, :], identity=idb[:]),
                          reads=[("pgb", s), "idb"], writes=["psT"])
                S.add("dve", I("tensor_copy", out=XT[:, :, 16 + pp * 128:16 + (pp + 1) * 128], in_=AP(psT, 0, [[1024, 128], [128, 4], [1, 128]])),
                      reads=["psT"], writes=["XT"])
            i0 = 1 if qd == 0 else 0
            for c in range(2):
                for g in range(2):
                    b = nb % 2
                    nb += 1
                    for ll in range(32):
                        S.add("pe", I("matmul", psS[b][:, 0:256], lhsT=Wc[:, c, ll, :], rhs=AP(XT, (c * 2 + g) * 4112 + ll, [[4 * 4112, 128], [16, 256]]),
                                      start=(ll == 0), stop=(ll == 31)), reads=["Wc", "XT"], writes=[("psS", b)])
                    dst = KcT if c == 0 else VcT
                    S.add("act", I("activation", out=dst[:, g, 256 * qd - 1 + i0:256 * qd + 255], in_=psS[b][:, i0:256], func=AF.Identity,
                                   bias=bias[:, c:c + 1]), reads=[("psS", b), "bias"], writes=["KcT" if c == 0 else "VcT"])
        for g in range(2):
            for it in range(8):
                S.add("pe", I("transpose", out=psT[:, 512:640], in_=VcT[:, g, it * 128:(it + 1) * 128], identity=idb[:]),
                      reads=["VcT", "idb"], writes=["psT"])
                S.add("dve", I("tensor_copy", out=Vaug[:, it, g, 0:128], in_=psT[:, 512:640]), reads=["psT"], writes=["Vaug"])

        def scores(lhsT, g, b, nrow=128, stop=True):
            S.add("pe", I("matmul", psS[b][0:nrow, :], lhsT=lhsT, rhs=AP(qs, 4 * g * 128, [[1024, 128], [1, 512]]), start=True, stop=stop),
                  reads=["qs", "KcT", ("kTt", 0), ("kTt", 1)], writes=[("psS", b)])

        selTb = sb("selTbS", [128, 3, 4, 128], BF16)
        antiB = sb("antiBS", [128, 4, 128], BF16)
        S.add("act", I("activation", out=antiB[:], in_=AP(anti, 0, [[128, 128], [0, 4], [1, 128]]), func=AF.Identity, scale=-30000.0),
              reads=["anti"], writes=["antiB"])

        def pv(src, rhs, first, last, width, nrow=128):
            for h in range(4):
                S.add("pe", I("matmul", psO[h][:, 0:width], lhsT=src[0:nrow, h, :], rhs=rhs, start=first, stop=last),
                      reads=[src.name, "Vaug", ("vat", 0), ("vat", 1), ("vat1", 0), ("vat1", 1)], writes=[("psO", h)])

        def rc_coef(gcol, g):
            for h in range(4):
                S.add("dve", I("tensor_scalar", out=rc[:, h:h + 1], in0=psO[h][:, 128:129], scalar1=1e-30, scalar2=None, op0=ALU.max),
                      reads=[("psO", h)], writes=["rc"])
            S.add("dve", I("reciprocal", out=rc[:], in_=rc[:]), reads=["rc"], writes=["rc"])
            S.add("dve", I("tensor_tensor", out=coef[:], in0=rc[:], in1=AP(gt, 12 * g + gcol, [[24, 128], [3, 4]]), op=ALU.mult),
                  reads=["rc", "gt"], writes=["coef"])

        def finish(gcol, g):
            rc_coef(gcol, g)
            for h in range(4):
                hh = 4 * g + h
                S.add("dve", I("scalar_tensor_tensor", out=nsa[:, hh, :], in0=psO[h][:, 0:128], scalar=coef[:, h:h + 1], in1=nsa[:, hh, :],
                               op0=ALU.mult, op1=ALU.add), reads=[("psO", h), "coef", ("nsa", hh)], writes=[("nsa", hh)])

        def load_new_tile(b, kslot, vslot):
            dma("sp", kTt[b][:], AP(KTs, kslot * 128 * TT + T, [[TT, 128], [1, 128]]), writes=[("kTt", b)])
            dma("sp", vat[b][:, 0:128], AP(VV, T * 512 + vslot * 128, [[512, 128], [1, 128]]), writes=[("vat", b)])

        for g in range(2):
            for it in range(8):
                b = it % 2
                ni = 128 if it < 7 else 127
                scores(KcT[:, g, it * 128:it * 128 + ni], g, b, nrow=ni)
                S.add("act", I("activation", out=ecs[b][0:ni], in_=psS[b][0:ni, :], func=AF.Exp), reads=[("psS", b)], writes=[ecs[b].name])
                pv(ecs[b], Vaug[0:ni, it, g, :], it == 0, it == 7, 386, nrow=ni)
            rc_coef(0, g)
            S.add("dve", I("tensor_scalar", out=imp[:], in0=psO[0][:, 129:386], scalar1=rc[:, 0:1], scalar2=None, op0=ALU.mult),
                  reads=[("psO", 0), "rc"], writes=["imp"])
            for h in range(1, 4):
                S.add("dve", I("scalar_tensor_tensor", out=imp[:], in0=psO[h][:, 129:386], scalar=rc[:, h:h + 1], in1=imp[:], op0=ALU.mult, op1=ALU.add),
                      reads=[("psO", h), "rc", "imp"], writes=["imp"])
            S.add("dve", I("tensor_tensor", out=imp[:], in0=imp[:], in1=bons[:], op=ALU.add), reads=["imp", "bons"], writes=["imp"])
            for h in range(4):
                hh = 4 * g + h
                S.add("dve", I("tensor_scalar", out=nsa[:, hh, :], in0=psO[h][:, 0:128], scalar1=coef[:, h:h + 1], scalar2=None, op0=ALU.mult),
                      reads=[("psO", h), "coef"], writes=[("nsa", hh)])
            S.add("dve", I("max", out=m8a[:], in_=imp[:]), reads=["imp"], writes=["m8a"])
            S.add("dve", I("match_replace", out=imp2[:], in_to_replace=m8a[:], in_values=imp[:], imm_value=NEG), reads=["imp", "m8a"], writes=["imp2"])
            S.add("dve", I("max", out=m8b[:], in_=imp2[:]), reads=["imp2"], writes=["m8b"])
            S.add("dve", I("tensor_scalar", out=sel[:, 0:257], in0=imp[:], scalar1=m8b[:, 7:8], scalar2=None, op0=ALU.is_ge),
                  reads=["imp", "m8b"], writes=["sel"])
            for jt in range(3):
                S.add("pe", I("transpose", out=psX[:, 128 + jt * 128:256 + jt * 128], in_=sel[:, jt * 128:(jt + 1) * 128], identity=idf[:]),
                      reads=["sel", "idf"], writes=["psX"])
            for jt in range(3):
                S.add("act", I("activation", out=selTb[:, jt, :, :], in_=AP(psX, 128 + jt * 128, [[512, 128], [0, 4], [1, 128]]), func=AF.Identity,
                               scale=30000.0, bias=-30000.0), reads=["psX"], writes=["selT"])
            def s_gather(kt):
                if kt < 128:
                    gather(AP(kg[kt % 4], 0, [[512, 128], [1, 512]]), 1, kt, ("kg", kt % 4))

            def s_front(kt):
                b = kt % 2
                if kt < 128:
                    S.add("act", I("copy", out=kgb[b][:], in_=kg[kt % 4][:]), reads=[("kg", kt % 4)], writes=[("kgb", b)])
                    S.add("pe", I("transpose", out=psT[:, 0:128], in_=kgb[b][:, g, :], identity=idb[:]), reads=[("kgb", b), "idb"], writes=["psT"])
                    S.add("dve", I("tensor_copy", out=kTt[b][:], in_=psT[:, 0:128]), reads=["psT"], writes=[("kTt", b)])
                    S.add("dve", I("tensor_copy", out=vat[b][:, 0:128], in_=kgb[b][:, 2 + g, :]), reads=[("kgb", b)], writes=[("vat", b)])
                else:
                    load_new_tile(b, 4 + g, g)
                scores(kTt[b][:], g, b, stop=False)
                S.add("pe", I("matmul", psS[b][:], lhsT=E2b[:, (kt % 64) * 128:(kt % 64 + 1) * 128], rhs=AP(selTb, (kt // 64) * 512, [[1536, 128], [1, 512]]),
                              start=False, stop=(kt != 128)), reads=["E2b", "selT"], writes=[("psS", b)])
                if kt == 128:
                    S.add("pe", I("matmul", psS[b][:], lhsT=idb[:], rhs=AP(antiB, 0, [[512, 128], [1, 512]]), start=False, stop=True),
                          reads=["idb", "antiB"], writes=[("psS", b)])
                S.add("act", I("activation", out=esb[b][:], in_=psS[b][:], func=AF.Exp), reads=[("psS", b)], writes=[esb[b].name])

            def s_mid(kt):
                pass

            for kk in range(3):
                s_gather(kk)
            s_front(0)
            for kt in range(129):
                s_gather(kt + 3)
                s_mid(kt)
                if kt + 1 <= 128:
                    s_front(kt + 1)
                pv(esb[kt % 2], vat[kt % 2][:], kt == 0, kt == 128, 129)
            finish(1, g)
            for w in range(5):
                b = w % 2
                if w < 4:
                    dma("sp", AP(kg[b], 0, [[512, 128], [1, 512]]), SM["cwin"].ap()[l, w * 128:(w + 1) * 128, :], writes=[("kg", b)])
                    S.add("act", I("copy", out=kgb[b][:], in_=kg[b][:]), reads=[("kg", b)], writes=[("kgb", b)])
                    S.add("pe", I("transpose", out=psT[:, 0:128], in_=kgb[b][:, g, :], identity=idb[:]), reads=[("kgb", b), "idb"], writes=["psT"])
                    S.add("dve", I("tensor_copy", out=kTt[b][:], in_=psT[:, 0:128]), reads=["psT"], writes=[("kTt", b)])
                    S.add("dve", I("tensor_copy", out=vat[b][:, 0:128], in_=kgb[b][:, 2 + g, :]), reads=[("kgb", b)], writes=[("vat", b)])
                else:
                    load_new_tile(b, 6 + g, 2 + g)
                scores(kTt[b][:], g, b)
                S.add("act", I("activation", out=esb[b][:], in_=psS[b][:], func=AF.Exp), reads=[("psS", b)], writes=[esb[b].name])
                mk = anti if w == 0 else (tri if w == 4 else None)
                if mk is not None:
                    S.add("pool", I("tensor_tensor", out=esb[b][:], in0=esb[b][:], in1=AP(mk, 0, [[128, 128], [0, 4], [1, 128]]), op=ALU.mult),
                          reads=[esb[b].name, "tri", "anti"], writes=[esb[b].name])
                pv(esb[b], vat[b][:], w == 0, w == 4, 129)
            finish(2, g)
        S.add("dve", I("tensor_tensor", out=nsa[:], in0=nsa[:], in1=AP(nz, 0, [[1024, 128], [128, 8], [1, 128]]), op=ALU.mult),
              reads=[("nsa", h) for h in range(8)] + ["nz"], writes=[("nsa", h) for h in range(8)])
        for half in range(2):
            for j in range(4):
                hh = half * 4 + j
                S.add("pe", I("transpose", out=psS[half][:, j * 128:(j + 1) * 128], in_=nsa[:, hh, :], identity=idf[:]),
                      reads=[("nsa", hh), "idf"], writes=[("psS", half)])
            S.add("act" if half == 0 else "dve", I("copy" if half == 0 else "tensor_copy", out=brs[:, half * 4:(half + 1) * 4, :],
                                                    in_=AP(psS[half], 0, [[512, 128], [128, 4], [1, 128]])),
                  reads=[("psS", half)], writes=[("brs", half)])
        dma("sp", AP(BRT, NT * 393216, [[3072, 128], [128, 8], [1, 128]]), brs[:], reads=[("brs", 0), ("brs", 1)])
        S.emit()

    with ExitStack() as ph:
        sb = lambda n, s, d: ph.enter_context(nc.sbuf_tensor(f"{n}_{l}", s, d))
        pu = [sb(f"pu{i}", [128, 1024], F32) for i in range(2)]
        pub = [sb(f"pub{i}", [128, 1024], BF16) for i in range(2)]
        pz = [sb(f"pz{i}", [128, 1024], F32) for i in range(2)]
        PA = sb("PA", [128, 3, 4, 128], BF16)
        pw = sb("pw", [128, 4, 2, 256], BF16)
        psc = sb("psc", [128, 1024], F32)
        pT = sb("pT", [128, 8, 128], BF16)
        po = sb("po", [128, 1024], F32)
        pob = sb("pob", [128, 1024], BF16)
        brs = [sb(f"brs6{i}", [128, 8, 128], BF16) for i in range(2)]
        idf = sb("idf6", [128, 128], F32)
        idb = sb("idb6", [128, 128], BF16)
        psP = [ph.enter_context(nc.psum_tensor(f"psP{i}_{l}", [128, 512], F32)) for i in range(2)]
        psM = [ph.enter_context(nc.psum_tensor(f"psM{i}_{l}", [128, 512], F32)) for i in range(2)]
        psT = ph.enter_context(nc.psum_tensor(f"psT6_{l}", [128, 1024], BF16))
        dma("sp", idf[:], Cd["ident"].ap(), writes=["idf"])
        S.add("dve", I("tensor_copy", out=idb[:], in_=idf[:]), reads=["idf"], writes=["idb"])
        dma("pool", PA[:], Cd["poolA"].ap(), writes=["PA"])
        dma("pool", pw[:], AP(Wd["pool_w"], woff["pool_w"], [[256, 128], [256 * 256, 4], [128 * 256, 2], [1, 256]]), writes=["pw"])
        dma("sp", psc[:], bc_row(Wd["pool_scale"], woff["pool_scale"], 1024), writes=["psc"])

        PAs = sb("PAs", [128, 4, 128], BF16)
        dma("pool", PAs[:], SM["poolAs"].ap(), writes=["PAs"])

        def p6_load(t):
            s = t % 2
            if t < NT:
                dma("sp", pu[s][:], Z.ap()[t * 128:(t + 1) * 128, C_PU:C_PU + 1024], writes=[("pu", s)])
            else:
                S.add("pool", I("memset", pu[s][:], 0.0), writes=[("pu", s)])
                dma("sp", pu[s][0:15, :], SM["spool"].ap()[l], writes=[("pu", s)])
                dma("sp", pu[s][15:19, :], Z.ap()[T:T + 4, C_PU:C_PU + 1024], writes=[("pu", s)])
            dma("sp", pz[s][:], Z.ap()[t * 128:(t + 1) * 128, C_PZ:C_PZ + 1024], writes=[("pz", s)])

        p6_load(0)
        for t in range(NTT):
            s = t % 2
            if t + 1 < NTT:
                p6_load(t + 1)
            S.add("act", I("copy", out=pub[s][:], in_=pu[s][:]), reads=[("pu", s)], writes=[("pub", s)])
            S.add("act", I("activation", out=pz[s][:], in_=pz[s][:], func=AF.Silu), reads=[("pz", s)], writes=[("pz", s)])
            kind = 1 if t == 0 else 0
            for cc in range(8):
                gi = cc // 2
                outp = psP[cc // 4][:, (cc % 4) * 128:(cc % 4 + 1) * 128]
                S.add("pe", I("matmul",
                    outp, lhsT=pub[s][:, cc * 128:(cc + 1) * 128], rhs=(PA[:, kind, gi, :] if t < NT else PAs[:, gi, :]),
                    start=True, stop=(t == 0 or t == NT)),
                    reads=[("pub", s), "PA", "PAs"], writes=[("psP", cc // 4)])
                if 0 < t < NT:
                    S.add("pe", I("matmul",
                        outp, lhsT=pub[1 - s][:, cc * 128:(cc + 1) * 128], rhs=PA[:, 2, gi, :], start=False, stop=True),
                        reads=[("pub", 1 - s), "PA"], writes=[("psP", cc // 4)])
            S.add("act", I("copy", out=pT[:, 0:4, :], in_=AP(psP[0], 0, [[512, 128], [128, 4], [1, 128]])), reads=[("psP", 0)], writes=["pT0"])
            S.add("dve", I("tensor_copy", out=pT[:, 4:8, :], in_=AP(psP[1], 0, [[512, 128], [128, 4], [1, 128]])), reads=[("psP", 1)], writes=["pT1"])
            for gi in range(4):
                for kk in range(2):
                    S.add("pe", I("matmul", psM[gi // 2][:, (gi % 2) * 256:(gi % 2 + 1) * 256], lhsT=pT[:, 2 * gi + kk, :],
                                                                 rhs=pw[:, gi, kk, :], start=(kk == 0), stop=(kk == 1)),
                          reads=["pT0", "pT1", "pw"], writes=[("psM", gi // 2)])
            for hf in range(2):
                S.add("dve", I("tensor_tensor", out=po[:, hf * 512:(hf + 1) * 512], in0=psM[hf][:], in1=psc[:, hf * 512:(hf + 1) * 512], op=ALU.mult),
                      reads=[("psM", hf), "psc"], writes=[("po", hf)])
            S.add("pool", I("tensor_tensor", out=pob[:], in0=po[:], in1=pz[s][:], op=ALU.mult),
                  reads=[("po", 0), ("po", 1), ("pz", s)], writes=["pob"])
            for j in range(8):
                S.add("pe", I("transpose", out=psT[:, j * 128:(j + 1) * 128], in_=pob[:, j * 128:(j + 1) * 128], identity=idb[:]),
                      reads=["pob", "idb"], writes=["psT"])
            S.add("act", I("copy", out=brs[s][:], in_=AP(psT, 0, [[1024, 128], [128, 8], [1, 128]])), reads=["psT"], writes=[("brs", s)])
            dma("sp", AP(BRT, t * 393216 + 1024, [[3072, 128], [128, 8], [1, 128]]), brs[s][:], reads=[("brs", s)])
            if t == NT - 1:
                dma("sp", pool_p.ap()[l], pu[s][113:128, :], reads=[("pu", s)])
            if t == NT:
                dma("sp", SM["pool_s"].ap()[l], pu[s][4:19, :], reads=[("pu", s)])
        S.emit()

    build_ssd(nc, S, env)

    with ExitStack() as ph:
        sb = lambda n, s, d: ph.enter_context(nc.sbuf_tensor(f"{n}_{l}", s, d))
        wbr = [sb(f"wbr{i}", [128, 3, 8, 512], BF16) for i in range(2)]
        brt = [sb(f"brt{i}", [128, 24, 128], BF16) for i in range(2)]
        gz = [sb(f"gz{i}", [128, 3, 512], F32) for i in range(2)]
        mg = sb("mg", [128, 512], F32)
        tm = sb("tm", [128, 512], F32)
        mgb = sb("mgb", [128, 512], BF16)
        mts = [sb(f"mts{i}", [128, 4, 128], BF16) for i in range(2)]
        idf = sb("idf8", [128, 128], F32)
        idb = sb("idb8", [128, 128], BF16)
        psP = [ph.enter_context(nc.psum_tensor(f"psQ{i}_{l}", [128, 512], F32)) for i in range(6)]
        psT = ph.enter_context(nc.psum_tensor(f"psT8_{l}", [128, 1024], BF16))
        dma("sp", idf[:], Cd["ident"].ap(), writes=["idf"])
        S.add("dve", I("tensor_copy", out=idb[:], in_=idf[:]), reads=["idf"], writes=["idb"])
        n = 0
        stg = sb("wstg", [128, 3, 8, 512], F32)

        def wbr_fetch(cc):
            for k in range(3):
                dma("sp", stg[:, k, :, :], AP(Wd["w_branch"], woff["w_branch"] + k * 1024 * 2048 + cc * 512, [[2048, 128], [128 * 2048, 8], [1, 512]]),
                    writes=[("wstg", k)])

        def wbr_cast(cc):
            for k in range(3):
                S.add("act", I("copy", out=wbr[cc % 2][:, k, :, :], in_=stg[:, k, :, :]), reads=[("wstg", k)], writes=[("wbr", cc % 2)])

        wbr_fetch(0)
        wbr_cast(0)
        for cc in range(4):
            ws = cc % 2
            if cc + 1 < 4:
                wbr_fetch(cc + 1)
            def m_front(t, s):
                dma("sp", brt[s][:], AP(BRT, t * 393216, [[3072, 128], [128, 24], [1, 128]]), writes=[("brt", s)])
                dma("sp", gz[s][:], AP(Z, t * 128 * NIN + C_MG + cc * 512, [[NIN, 128], [2048, 3], [1, 512]]), writes=[("gz", s)])
                S.add("act", I("activation", out=gz[s][:], in_=gz[s][:], func=AF.Sigmoid), reads=[("gz", s)], writes=[("gz", s)])
                for k in range(3):
                    pb = psP[3 * s + k]
                    for kt in range(8):
                        S.add("pe", I("matmul", pb[:], lhsT=brt[s][:, 8 * k + kt, :], rhs=wbr[ws][:, k, kt, :],
                                                                                   start=(kt == 0), stop=(kt == 7)),
                              reads=[("brt", s), ("wbr", ws)], writes=[("psQ", 3 * s + k)])

            m_front(0, n % 2)
            for t in range(NTT):
                s = n % 2
                n += 1
                if t == NTT // 2 and cc + 1 < 4:
                    wbr_cast(cc + 1)
                if t + 1 < NTT:
                    m_front(t + 1, n % 2)
                S.add("dve", I("tensor_tensor", out=mg[:], in0=psP[3 * s][:], in1=gz[s][:, 0, :], op=ALU.mult),
                      reads=[("psQ", 3 * s), ("gz", s)], writes=["mg"])
                S.add("dve", I("tensor_tensor", out=tm[:], in0=psP[3 * s + 1][:], in1=gz[s][:, 1, :], op=ALU.mult),
                      reads=[("psQ", 3 * s + 1), ("gz", s)], writes=["tm"])
                S.add("pool", I("tensor_tensor", out=mg[:], in0=mg[:], in1=tm[:], op=ALU.add), reads=["mg", "tm"], writes=["mg"])
                S.add("dve", I("tensor_tensor", out=tm[:], in0=psP[3 * s + 2][:], in1=gz[s][:, 2, :], op=ALU.mult),
                      reads=[("psQ", 3 * s + 2), ("gz", s)], writes=["tm"])
                S.add("pool", I("tensor_tensor", out=mgb[:], in0=mg[:], in1=tm[:], op=ALU.add), reads=["mg", "tm"], writes=["mgb"])
                for j in range(4):
                    S.add("pe", I("transpose", out=psT[:, j * 128:(j + 1) * 128], in_=mgb[:, j * 128:(j + 1) * 128], identity=idb[:]),
                          reads=["mgb", "idb"], writes=["psT"])
                S.add("act", I("copy", out=mts[s][:], in_=AP(psT, 0, [[1024, 128], [128, 4], [1, 128]])), reads=["psT"], writes=[("mts", s)])
                dma("sp", AP(MT, cc * 4 * 128 * TT + t * 128, [[TT, 128], [128 * TT, 4], [1, 128]]), mts[s][:], reads=[("mts", s)])
        S.emit()

    with ExitStack() as ph:
        sb = lambda n, s, d: ph.enter_context(nc.sbuf_tensor(f"{n}_{l}", s, d))
        mT = sb("mT", [128, 16, TT], BF16)
        wo = [sb(f"wo{i}", [128, 16, 512], BF16) for i in range(2)]
        xr = [sb(f"xr{i}", [128, 512], F32) for i in range(4)]
        ps = [ph.enter_context(nc.psum_tensor(f"ps9{i}_{l}", [128, 512], F32)) for i in range(4)]
        dma("sp", mT[:], AP(MT, 0, [[TT, 128], [128 * TT, 16], [1, TT]]), writes=["mT"])
        n = 0
        for cc in range(4):
            ws = cc % 2
            dma("pool", wo[ws][:], AP(Wd["w_out"], woff["w_out"] + cc * 512, [[2048, 128], [128 * 2048, 16], [1, 512]]), writes=[("wo", ws)])
            for t in range(NTT):
                b = n % 4
                n += 1
                dma("sp", xr[b][:], x_src.ap()[t * 128:(t + 1) * 128, cc * 512:(cc + 1) * 512], writes=[("xr", b)])
                for kt in range(16):
                    S.add("pe", I("matmul", ps[b][:], lhsT=mT[:, kt, t * 128:(t + 1) * 128], rhs=wo[ws][:, kt, :],
                                                                           start=(kt == 0), stop=(kt == 15)),
                          reads=["mT", ("wo", ws)], writes=[("ps", b)])
                S.add("dve", I("tensor_tensor", out=xr[b][:], in0=ps[b][:], in1=xr[b][:], op=ALU.add),
                      reads=[("ps", b), ("xr", b)], writes=[("xr", b)])
                if t < NT or l < L - 1:
                    dma("sp", x_dst.ap()[t * 128:(t + 1) * 128, cc * 512:(cc + 1) * 512], xr[b][:], reads=[("xr", b)])
                else:
                    dma("sp", SM["y_s"].ap()[:, cc * 512:(cc + 1) * 512], xr[b][0:4, :], reads=[("xr", b)])
        S.emit()


def build_ssd(nc, S, env):
    l, T, NT = env["l"], env["T"], env["NT"]
    TT, NTT, SM = env["TT"], env["NTT"], env["SM"]
    Wd, woff, Cd = env["Wd"], env["woff"], env["Cd"]
    Z, Z2, BRT = env["Z"], env["Z2"], env["BRT"]
    conv_p, ssm_p = env["conv_p"], env["ssm_p"]
    dma, bc_row = env["dma"], env["bc_row"]
    with ExitStack() as ssd:
        sbo = lambda n, s, d: ssd.enter_context(nc.sbuf_tensor(f"{n}_{l}", s, d))
        xcT = sbo("xcT", [128, 12, TT], BF16)
        idf = sbo("idf7", [128, 128], F32)
        idb = sbo("idb7", [128, 128], BF16)
        with ExitStack() as ph:
            sb = lambda n, s, d: ph.enter_context(nc.sbuf_tensor(f"{n}_{l}", s, d))
            xin = [sb(f"xin{i}", [128, T + 4], F32) for i in range(2)]
            acc = sb("acc", [128, T], F32)
            cw_in = sb("cw_in", [60, 128], F32)
            cwb = sb("cwb", [128, 60], F32)
            pcw = ph.enter_context(nc.psum_tensor(f"pcw_{l}", [128, 512], F32))
            dma("sp", idf[:], Cd["ident"].ap(), writes=["idf"])
            S.add("dve", I("tensor_copy", out=idb[:], in_=idf[:]), reads=["idf"], writes=["idb"])
            dma("sp", cw_in[0:48, :], AP(Wd["conv_w"], woff["conv_w"], [[128, 48], [1, 128]]), writes=["cw_in"])
            dma("sp", cw_in[48:60, :], AP(Wd["conv_b"], woff["conv_b"], [[128, 12], [1, 128]]), writes=["cw_in"])
            S.add("pe", I("transpose", out=pcw[:, 0:60], in_=cw_in[:], identity=idf[0:60, 0:60]), reads=["cw_in", "idf"], writes=["pcw"])
            S.add("dve", I("tensor_copy", out=cwb[:], in_=pcw[:, 0:60]), reads=["pcw"], writes=["cwb"])
            for i in range(2):
                S.add("pool", I("memset", xin[i][:, 0:3], 0.0), writes=[("xin", i)])
            xsi = [sb(f"xsi{i}", [128, 8], F32) for i in range(2)]
            accs = sb("accs", [128, 4], F32)
            S.add("pool", I("memset", xcT[:, :, T:TT], 0.0), writes=[("xcT", k) for k in range(12)])
            for kc in range(12):
                s = kc % 2
                dma("sp", xsi[s][:, 0:3], AP(SM["sconv"], l * 3 * 1536 + kc * 128, [[1, 128], [1536, 3]]), writes=[("xsi", s)],
                    allow_slow_non_contiguous=True)
                dma("sp", xsi[s][:, 3:7], Z2.ap()[kc * 128:(kc + 1) * 128, T:T + 4], writes=[("xsi", s)])
                S.add("dve", I("tensor_scalar", out=accs[:], in0=xsi[s][:, 0:4], scalar1=cwb[:, kc:kc + 1],
                               scalar2=cwb[:, 48 + kc:49 + kc], op0=ALU.mult, op1=ALU.add), reads=[("xsi", s), "cwb"], writes=["accs"])
                for j in range(1, 4):
                    S.add("dve", I("scalar_tensor_tensor", out=accs[:], in0=xsi[s][:, j:j + 4], scalar=cwb[:, j * 12 + kc:j * 12 + kc + 1],
                                   in1=accs[:], op0=ALU.mult, op1=ALU.add), reads=[("xsi", s), "cwb", "accs"], writes=["accs"])
                S.add("act", I("activation", out=xcT[:, kc, T:T + 4], in_=accs[:], func=AF.Silu), reads=["accs"], writes=[("xcT", kc)])
                dma("sp", AP(SM["conv_s"], l * 3 * 1536 + kc * 128, [[1, 128], [1536, 3]]), xsi[s][:, 4:7], reads=[("xsi", s)],
                    allow_slow_non_contiguous=True)
                dma("sp", xin[s][:, 3:3 + T], Z2.ap()[kc * 128:(kc + 1) * 128, 0:T], writes=[("xin", s)])
                S.add("dve", I("tensor_scalar", out=acc[:], in0=xin[s][:, 0:T], scalar1=cwb[:, kc:kc + 1],
                                                                    scalar2=cwb[:, 48 + kc:49 + kc], op0=ALU.mult, op1=ALU.add),
                      reads=[("xin", s), "cwb"], writes=["acc"])
                for j in range(1, 4):
                    S.add("dve", I("scalar_tensor_tensor", out=acc[:], in0=xin[s][:, j:j + T], scalar=cwb[:, j * 12 + kc:j * 12 + kc + 1],
                                                                                in1=acc[:], op0=ALU.mult, op1=ALU.add),
                          reads=[("xin", s), "cwb", "acc"], writes=["acc"])
                S.add("act", I("activation", out=xcT[:, kc, 0:T], in_=acc[:], func=AF.Silu), reads=["acc"], writes=[("xcT", kc)])
                dma("sp", AP(conv_p, l * 3 * 1536 + kc * 128, [[1, 128], [1536, 3]]), xin[s][:, T:T + 3], reads=[("xin", s)],
                    allow_slow_non_contiguous=True)
            S.emit()
        with ExitStack() as ph:
            sb = lambda n, s, d: ph.enter_context(nc.sbuf_tensor(f"{n}_{l}", s, d))
            xtok = sb("xtok", [128, 1024], BF16)
            Btok = sb("Btok", [128, 2, 128], BF16)
            dtr = [sb(f"dtr{i}", [128, 16], F32) for i in range(2)]
            mz = [sb(f"mz{i}", [128, 1024], F32) for i in range(2)]
            dtb = sb("dtb", [128, 16], F32)
            abc = sb("abc", [128, 16], F32)
            dsk = sb("dsk", [128, 16], F32)
            mnw = sb("mnw", [128, 1024], F32)
            TRIf = sb("TRIf", [128, 128], F32)
            Uf = sb("Uf", [128, 128], F32)
            ONESf = sb("ONESf", [128, 128], F32)
            TRIb = sb("TRIb", [128, 128], BF16)
            epsb = sb("eps7", [128, 1], F32)
            dt = sb("dt", [128, 16], F32)
            la = sb("la", [128, 16], F32)
            acs = sb("acs", [128, 16], F32)
            eacs = sb("eacs", [128, 16], F32)
            dte = sb("dte", [128, 16], F32)
            dch = sb("dch", [128, 16], F32)
            Y = sb("Yk", [128, 16, 128], F32)
            expD = sb("expD", [128, 16, 128], BF16)
            GM = sb("GM", [128, 2, 128], BF16)
            MTt = sb("MTt", [128, 16, 128], BF16)
            xdt = sb("xdt", [128, 1024], BF16)
            xdtd = sb("xdtd", [128, 1024], BF16)
            Sf = sb("Sf", [128, 16, 64], F32)
            Sb = sb("Sb", [128, 16, 64], BF16)
            ytmp = sb("ytmp", [128, 1024], F32)
            yy = sb("yy", [128, 1024], F32)
            junk = sb("junk7", [128, 512], BF16)
            ssg = sb("ssg", [128, 2], F32)
            rsg = sb("rsg", [128, 2], F32)
            mo = sb("mo", [128, 1024], BF16)
            brs = [sb(f"brs7{i}", [128, 8, 128], BF16) for i in range(2)]
            fin = sb("fin", [128, 8, 128], F32)
            psXT = ph.enter_context(nc.psum_tensor(f"psXT_{l}", [128, 1024], BF16))
            psD = [ph.enter_context(nc.psum_tensor(f"psD{i}_{l}", [128, 512], F32)) for i in range(2)]
            psA = ph.enter_context(nc.psum_tensor(f"psA_{l}", [128, 512], F32))
            psY = [ph.enter_context(nc.psum_tensor(f"psY{i}_{l}", [128, 512], F32)) for i in range(2)]
            psF = [ph.enter_context(nc.psum_tensor(f"psF{i}_{l}", [128, 512], F32)) for i in range(2)]
            dma("sp", dtb[:], bc_row(Wd["dt_bias"], woff["dt_bias"], 16), writes=["dtb"])
            dma("sp", abc[:], bc_row(Wd["a_log"], woff["a_log"], 16), writes=["abc"])
            dma("sp", dsk[:], bc_row(Wd["d_skip"], woff["d_skip"], 16), writes=["dsk"])
            dma("sp", mnw[:], bc_row(Wd["mnorm_w"], woff["mnorm_w"], 1024), writes=["mnw"])
            dma("sp", TRIf[:], Cd["tri"].ap(), writes=["TRIf"])
            dma("sp", Uf[:], Cd["ustrict"].ap(), writes=["Uf"])
            dma("sp", ONESf[:], Cd["ones"].ap(), writes=["ONESf"])
            S.add("dve", I("tensor_copy", out=TRIb[:], in_=TRIf[:]), reads=["TRIf"], writes=["TRIb"])
            S.add("dve", I("memset", epsb[:], EPS), writes=["eps"])
            S.add("act", I("activation", out=abc[:], in_=abc[:], func=AF.Exp), reads=["abc"], writes=["abc"])
            S.add("act", I("mul", out=abc[:], in_=abc[:], mul=-1.0), reads=["abc"], writes=["abc"])
            S.add("pool", I("memset", Sf[:], 0.0), writes=["Sf"])
            S.add("pool", I("memset", Sb[:], 0.0), writes=["Sb"])

            def p7_load(c):
                s = c % 2
                dma("sp", dtr[s][:], Z.ap()[c * 128:(c + 1) * 128, C_DT:C_DT + 16], writes=[("dtr", s)])
                dma("sp", mz[s][:], Z.ap()[c * 128:(c + 1) * 128, C_MZ:C_MZ + 1024], writes=[("mz", s)])

            p7_load(0)
            XC = [("xcT", k) for k in range(12)]
            rowm = sb("rowm", [128, 1], F32)
            dma("sp", rowm[:], SM["rowm"].ap(), writes=["rowm"])

            def dump_state(dst, off):
                for j in range(8):
                    S.add("pe", I("transpose", out=psY[j // 4][:, (j % 4) * 128:(j % 4 + 1) * 128], in_=AP(Sf, j * 128, [[1024, 128], [1, 128]]),
                                  identity=idf[:]), reads=["Sf", "idf"], writes=[("psY", j // 4)])
                for hf in range(2):
                    S.add("dve", I("tensor_copy", out=fin[:, hf * 4:(hf + 1) * 4, :], in_=AP(psY[hf], 0, [[512, 128], [128, 4], [1, 128]])),
                          reads=[("psY", hf)], writes=[("fin", hf)])
                dma("sp", AP(dst, off, [[128, 128], [128 * 128, 8], [1, 128]]), fin[:], reads=[("fin", 0), ("fin", 1)])

            for c in range(NTT):
                s = c % 2
                if c + 1 < NTT:
                    p7_load(c + 1)
                if c == NT:
                    dump_state(ssm_p, l * 1024 * 128)
                    dma("sp", fin[:], AP(SM["sssm"], l * 1024 * 128, [[128, 128], [128 * 128, 8], [1, 128]]), writes=[("fin", 0), ("fin", 1)])
                    for j in range(8):
                        S.add("pe", I("transpose", out=psY[j // 4][:, (j % 4) * 128:(j % 4 + 1) * 128], in_=fin[:, j, :], identity=idf[:]),
                              reads=[("fin", 0), ("fin", 1), "idf"], writes=[("psY", j // 4)])
                    for hf in range(2):
                        S.add("dve", I("tensor_copy", out=AP(Sf, hf * 512, [[1024, 128], [1, 512]]), in_=psY[hf][:]),
                              reads=[("psY", hf)], writes=["Sf"])
                    S.add("act", I("copy", out=Sb[:], in_=Sf[:]), reads=["Sf"], writes=["Sb"])
                tok = slice(c * 128, (c + 1) * 128)
                for kc in range(8):
                    S.add("pe", I("transpose", out=psXT[:, kc * 128:(kc + 1) * 128], in_=xcT[:, kc, tok], identity=idb[:]),
                          reads=XC + ["idb"], writes=["psXT"])
                S.add("act", I("copy", out=xtok[:], in_=psXT[:]), reads=["psXT"], writes=["xtok"])
                for g in range(2):
                    S.add("pe", I("transpose", out=psXT[:, g * 128:(g + 1) * 128], in_=xcT[:, 8 + g, tok], identity=idb[:]),
                          reads=XC + ["idb"], writes=["psXT"])
                S.add("dve", I("tensor_copy", out=Btok[:], in_=AP(psXT, 0, [[1024, 128], [128, 2], [1, 128]])), reads=["psXT"], writes=["Btok"])
                S.add("dve", I("tensor_tensor", out=dt[:], in0=dtr[s][:], in1=dtb[:], op=ALU.add), reads=[("dtr", s), "dtb"], writes=["dt"])
                S.add("act", I("activation", out=dt[:], in_=dt[:], func=AF.Exp), reads=["dt"], writes=["dt"])
                S.add("act", I("activation", out=dt[:], in_=dt[:], func=AF.Ln, bias=1.0), reads=["dt"], writes=["dt"])
                if c == NT:
                    S.add("dve", I("tensor_scalar", out=dt[:], in0=dt[:], scalar1=rowm[:, 0:1], scalar2=None, op0=ALU.mult),
                          reads=["dt", "rowm"], writes=["dt"])
                S.add("dve", I("tensor_tensor", out=la[:], in0=dt[:], in1=abc[:], op=ALU.mult), reads=["dt", "abc"], writes=["la"])
                S.add("dve", I("tensor_tensor", out=Y[:], in0=AP(TRIf, 0, [[128, 128], [0, 16], [1, 128]]),
                                                       in1=AP(la, 0, [[16, 128], [1, 16], [0, 128]]), op=ALU.mult),
                      reads=["TRIf", "la"], writes=["Y"])
                for qd in range(4):
                    S.add("pe", I("matmul", psD[qd % 2][:], lhsT=Uf[:], rhs=AP(Y, qd * 512, [[2048, 128], [1, 512]]), start=True, stop=True),
                          reads=["Uf", "Y"], writes=[("psD", qd % 2)])
                    S.add("act", I("activation", out=AP(expD, qd * 512, [[2048, 128], [1, 512]]), in_=psD[qd % 2][:], func=AF.Exp),
                          reads=[("psD", qd % 2)], writes=[("expD", qd)])
                S.add("pe", I("matmul", psA[:, 0:16], lhsT=TRIf[:], rhs=la[:], start=True, stop=True), reads=["TRIf", "la"], writes=["psA0"])
                S.add("pe", I("matmul", psA[:, 16:32], lhsT=ONESf[:], rhs=la[:], start=True, stop=True), reads=["ONESf", "la"], writes=["psA1"])
                S.add("act", I("copy", out=acs[:], in_=psA[:, 0:16]), reads=["psA0"], writes=["acs"])
                S.add("act", I("activation", out=eacs[:], in_=psA[:, 0:16], func=AF.Exp), reads=["psA0"], writes=["eacs"])
                S.add("act", I("activation", out=dch[:], in_=psA[:, 16:32], func=AF.Exp), reads=["psA1"], writes=["dch"])
                S.add("dve", I("tensor_tensor", out=dte[:], in0=psA[:, 16:32], in1=acs[:], op=ALU.subtract), reads=["psA1", "acs"], writes=["dte"])
                S.add("act", I("activation", out=dte[:], in_=dte[:], func=AF.Exp), reads=["dte"], writes=["dte"])
                for g in range(2):
                    S.add("pe", I("matmul", psA[:, 32 + g * 128:160 + g * 128], lhsT=xcT[:, 8 + g, tok], rhs=xcT[:, 10 + g, tok],
                                                                 start=True, stop=True), reads=XC, writes=["psAG"])
                S.add("dve", I("tensor_tensor", out=GM[:], in0=AP(psA, 32, [[512, 128], [128, 2], [1, 128]]),
                                                       in1=AP(TRIb, 0, [[128, 128], [0, 2], [1, 128]]), op=ALU.mult),
                      reads=["psAG", "TRIb"], writes=["GM"])
                for g in range(2):
                    S.add("dve" if g == 0 else "pool", I("tensor_tensor",
                        out=MTt[:, 8 * g:8 * g + 8, :], in0=expD[:, 8 * g:8 * g + 8, :], in1=AP(GM, g * 128, [[256, 128], [0, 8], [1, 128]]), op=ALU.mult),
                        reads=[("expD", q) for q in range(4)] + ["GM"], writes=[("MTt", g)])
                S.add("dve", I("tensor_tensor", out=AP(xdt, 0, [[1024, 128], [64, 16], [1, 64]]), in0=AP(xtok, 0, [[1024, 128], [64, 16], [1, 64]]),
                                                       in1=AP(dt, 0, [[16, 128], [1, 16], [0, 64]]), op=ALU.mult), reads=["xtok", "dt"], writes=["xdt"])
                S.add("pool", I("tensor_tensor", out=AP(xdtd, 0, [[1024, 128], [64, 16], [1, 64]]), in0=AP(xdt, 0, [[1024, 128], [64, 16], [1, 64]]),
                                                        in1=AP(dte, 0, [[16, 128], [1, 16], [0, 64]]), op=ALU.mult), reads=["xdt", "dte"], writes=["xdtd"])
                for h in range(16):
                    S.add("pe", I("matmul", psY[h // 8][:, (h % 8) * 64:(h % 8 + 1) * 64], lhsT=MTt[:, h, :], rhs=xdt[:, h * 64:(h + 1) * 64],
                                                        start=True, stop=True), reads=[("MTt", h // 8), "xdt"], writes=[("psY", h // 8)])
                for h in range(16):
                    S.add("pe", I("matmul", psF[h // 8][:, (h % 8) * 64:(h % 8 + 1) * 64], lhsT=xcT[:, 10 + h // 8, tok], rhs=Sb[:, h, :],
                                                                 start=True, stop=True), reads=XC + ["Sb"], writes=[("psF", h // 8)])
                for hf in range(2):
                    S.add("dve", I("tensor_tensor", out=AP(ytmp, hf * 512, [[1024, 128], [64, 8], [1, 64]]),
                                                                  in0=AP(psF[hf], 0, [[512, 128], [64, 8], [1, 64]]),
                                                                  in1=AP(eacs, hf * 8, [[16, 128], [1, 8], [0, 64]]), op=ALU.mult),
                          reads=[("psF", hf), "eacs"], writes=[("ytmp", hf)])
                    S.add("dve", I("tensor_tensor", out=yy[:, hf * 512:(hf + 1) * 512], in0=psY[hf][:], in1=ytmp[:, hf * 512:(hf + 1) * 512], op=ALU.add),
                          reads=[("psY", hf), ("ytmp", hf)], writes=[("yy", hf)])
                S.add("pool", I("tensor_tensor", out=AP(ytmp, 0, [[1024, 128], [64, 16], [1, 64]]), in0=AP(xtok, 0, [[1024, 128], [64, 16], [1, 64]]),
                                                        in1=AP(dsk, 0, [[16, 128], [1, 16], [0, 64]]), op=ALU.mult),
                      reads=["xtok", "dsk", ("yy", 0), ("yy", 1)], writes=[("ytmp", 0), ("ytmp", 1)])
                S.add("pool", I("tensor_tensor", out=yy[:], in0=yy[:], in1=ytmp[:], op=ALU.add),
                      reads=[("ytmp", 0), ("ytmp", 1), ("yy", 0), ("yy", 1)], writes=[("yy", 0), ("yy", 1)])
                S.add("act", I("activation", out=mz[s][:], in_=mz[s][:], func=AF.Silu), reads=[("mz", s)], writes=[("mz", s)])
                S.add("dve", I("tensor_tensor", out=yy[:], in0=yy[:], in1=mz[s][:], op=ALU.mult),
                      reads=[("mz", s), ("yy", 0), ("yy", 1)], writes=[("yy", 0), ("yy", 1)])
                for g in range(2):
                    S.add("act", I("activation", out=junk[:], in_=yy[:, g * 512:(g + 1) * 512], func=AF.Square, accum_out=ssg[:, g:g + 1]),
                          reads=[("yy", 0), ("yy", 1)], writes=["junk", "ssg"])
                S.add("act", I("activation", out=rsg[:], in_=ssg[:], func=AF.Sqrt, bias=epsb[:], scale=1.0 / 512), reads=["ssg", "eps"], writes=["rsg"])
                S.add("dve", I("reciprocal", out=rsg[:], in_=rsg[:]), reads=["rsg"], writes=["rsg"])
                S.add("dve", I("tensor_tensor", out=AP(yy, 0, [[1024, 128], [512, 2], [1, 512]]), in0=AP(yy, 0, [[1024, 128], [512, 2], [1, 512]]),
                                                       in1=AP(rsg, 0, [[2, 128], [1, 2], [0, 512]]), op=ALU.mult),
                      reads=["rsg", ("yy", 0), ("yy", 1)], writes=[("yy", 0), ("yy", 1)])
                S.add("pool", I("tensor_tensor", out=mo[:], in0=yy[:], in1=mnw[:], op=ALU.mult), reads=["mnw", ("yy", 0), ("yy", 1)], writes=["mo"])
                for j in range(8):
                    S.add("pe", I("transpose", out=psXT[:, j * 128:(j + 1) * 128], in_=mo[:, j * 128:(j + 1) * 128], identity=idb[:]),
                          reads=["mo", "idb"], writes=["psXT"])
                S.add("act", I("copy", out=brs[s][:], in_=AP(psXT, 0, [[1024, 128], [128, 8], [1, 128]])), reads=["psXT"], writes=[("brs", s)])
                dma("sp", AP(BRT, c * 393216 + 2048, [[3072, 128], [128, 8], [1, 128]]), brs[s][:], reads=[("brs", s)])
                for h in range(16):
                    S.add("pe", I("matmul", psD[h // 8][:, (h % 8) * 64:(h % 8 + 1) * 64], lhsT=Btok[:, h // 8, :], rhs=xdtd[:, h * 64:(h + 1) * 64],
                                                        start=True, stop=True), reads=["Btok", "xdtd"], writes=[("psD", h // 8)])
                S.add("dve", I("tensor_tensor", out=Sf[:], in0=Sf[:], in1=AP(dch, 0, [[16, 128], [1, 16], [0, 64]]), op=ALU.mult),
                      reads=["Sf", "dch"], writes=["Sf"])
                for hf in range(2):
                    S.add("dve", I("tensor_tensor", out=AP(Sf, hf * 512, [[1024, 128], [1, 512]]), in0=psD[hf][:],
                                                                  in1=AP(Sf, hf * 512, [[1024, 128], [1, 512]]), op=ALU.add),
                          reads=[("psD", hf), "Sf"], writes=["Sf"])
                S.add("act", I("copy", out=Sb[:], in_=Sf[:]), reads=["Sf"], writes=["Sb"])
            dump_state(SM["ssm_s"], l * 1024 * 128)
            S.emit()


_CACHE = {}


def make_in_map(T, L, x_prompt_seq, x_sample_seq, weights, cache_flat, cwin, spool, sconv, sssm, pt, consts):
    xp = np.zeros((T + 128, D), np.float32)
    xp[:T] = x_prompt_seq
    xp[T:T + 4] = x_sample_seq
    m = {"xp": xp, "cache": cache_flat, "cwin": np.ascontiguousarray(cwin).reshape(L, 512, 512),
         "spool": np.ascontiguousarray(spool), "sconv": np.ascontiguousarray(sconv),
         "sssm": np.ascontiguousarray(sssm).reshape(L, 1024, 128), "pt": np.ascontiguousarray(pt).reshape(1, 128).astype(np.int32)}
    for k in WEIGHTS:
        m[k] = weights[k]
    for k, v in consts.items():
        m["c_" + k] = v
    return m


def kernel(**inputs):
    T, L = 4096, 4
    npool = inputs["cache_kv"].shape[1]
    key = (T, L, npool)
    if key not in _CACHE:
        _CACHE[key] = build(T, L, NPOOL=npool)
    nc = _CACHE[key]
    consts = make_consts(T)
    weights = {k: np.ascontiguousarray(inputs[k]) for k in WEIGHTS}
    cache_flat = np.ascontiguousarray(inputs["cache_kv"]).reshape(L * npool * 128 * 8, 128)
    in_maps = []
    for c in range(8):
        in_maps.append(make_in_map(T, L, inputs["x_prompt"][c % 2], inputs["x_sample"][c], weights, cache_flat,
                                   inputs["cache_win"][:, c], inputs["state_pool"][:, c], inputs["state_conv"][:, c],
                                   inputs["state_ssm"][:, c], inputs["page_table"][c], consts))
    res = run_bass_kernel_spmd(nc, in_maps, core_ids=list(range(8))).results
    f = lambda name: np.stack([res[b][name] for b in range(2)])
    y_prompt = f("y_p")
    kv_p = np.transpose(f("kv_p"), (1, 0, 2, 3)).reshape(L, 2, T, 4, 2, 128)
    win_p = np.transpose(f("win_p"), (1, 0, 2, 3)).reshape(L, 2, 512, 2, 2, 128)
    pool_p = np.transpose(f("pool_p"), (1, 0, 2, 3))
    conv_p = np.transpose(f("conv_p"), (1, 0, 2, 3))
    ssm_p = np.transpose(f("ssm_p"), (1, 0, 2, 3)).reshape(L, 2, 16, 64, 128)
    s8 = lambda name: np.stack([res[b][name] for b in range(8)])
    y_sample = s8("y_s")
    kv_s = np.transpose(s8("kv_s"), (1, 0, 2, 3)).reshape(L, 8, 4, 4, 2, 128)
    win_s = np.transpose(s8("win_s"), (1, 0, 2, 3)).reshape(L, 8, 512, 2, 2, 128)
    pool_s = np.transpose(s8("pool_s"), (1, 0, 2, 3))
    conv_s = np.transpose(s8("conv_s"), (1, 0, 2, 3))
    ssm_s = np.transpose(s8("ssm_s"), (1, 0, 2, 3)).reshape(L, 8, 16, 64, 128)
    return (y_prompt, y_sample, kv_p, win_p, pool_p, conv_p, ssm_p, kv_s, win_s, pool_s, conv_s, ssm_s)
```

```python
import math
from contextlib import ExitStack

import numpy as np
import concourse.bass as bass
import concourse.mybir as mybir
from concourse.bass_utils import run_bass_kernel_spmd

F32 = mybir.dt.float32
BF16 = mybir.dt.bfloat16
AF = mybir.ActivationFunctionType
ALU = mybir.AluOpType
AX = mybir.AxisListType

ENGS = ("pe", "act", "dve", "pool", "sp")
NDSEM = 12
NEAR = 3

D = 2048
NIN = 14376
BW = 1024
KT = 16
C_Q, C_KV, C_G, C_NZ, C_PU, C_PZ, C_MZ, C_XBC, C_DT, C_MG = 0, 1024, 2560, 2584, 3608, 4632, 5656, 6680, 8216, 8232
EPS = 1e-6
NEG = -1.0e30
PAST = 16384


def I(meth, *a, **kw):
    return lambda e: getattr(e, meth)(*a, **kw)


def AP(t, off, dims):
    return bass.AP(t, off, [list(d) for d in dims])


class Op:
    __slots__ = ("eng", "fn", "dma", "deps", "sig", "sem", "val", "idx", "prev_dma")


class Sched:
    def __init__(self, nc, csem, dsem):
        self.nc = nc
        self.csem = csem
        self.dsem = dsem
        self.ccnt = {e: 0 for e in ENGS}
        self.dcnt = {e: [0] * NDSEM for e in ENGS}
        self.drr = {e: 0 for e in ENGS}
        self.waited = {e: {} for e in ENGS}
        self.reset()

    def reset(self):
        self.ops = {e: [] for e in ENGS}
        self.lastw = {}
        self.readers = {}

    def add(self, eng, fn, reads=(), writes=(), dma=False):
        o = Op()
        o.eng, o.fn, o.dma, o.deps, o.sig = eng, fn, dma, set(), False
        o.sem, o.val, o.prev_dma = None, 0, None
        o.idx = len(self.ops[eng])
        for r in reads:
            w = self.lastw.get(r)
            if w is not None:
                o.deps.add(w)
            self.readers.setdefault(r, []).append(o)
        for r in writes:
            w = self.lastw.get(r)
            if w is not None:
                o.deps.add(w)
            for rd in self.readers.get(r, ()):
                o.deps.add(rd)
            self.lastw[r] = o
            self.readers[r] = []
        o.deps.discard(o)
        self.ops[eng].append(o)
        return o

    def _needs_wait(self, dep, o):
        if dep.dma or dep.eng != o.eng or o.dma:
            return True
        if o.eng == "pe":
            return False
        return (o.idx - dep.idx) <= NEAR

    def emit(self):
        for e in ENGS:
            for o in self.ops[e]:
                if o.dma:
                    o.sig = True
                for d in o.deps:
                    if self._needs_wait(d, o):
                        d.sig = True
        for e in ENGS:
            for o in self.ops[e]:
                if not o.sig:
                    continue
                if o.dma:
                    k = self.drr[e]
                    self.drr[e] = (k + 1) % NDSEM
                    o.prev_dma = self.dcnt[e][k]
                    self.dcnt[e][k] += 16
                    o.sem, o.val = self.dsem[e][k], self.dcnt[e][k]
                else:
                    self.ccnt[e] += 1
                    o.sem, o.val = self.csem[e], self.ccnt[e]
        engobj = {"pe": "tensor", "act": "scalar", "dve": "vector", "pool": "gpsimd", "sp": "sync"}
        with self.nc.Block() as block:
            for e in ENGS:
                ops = self.ops[e]
                if not ops and e != "sp":
                    continue

                def body(eng, e=e, ops=ops):
                    waited = self.waited[e]

                    def wait(sem, val):
                        key = id(sem)
                        if waited.get(key, 0) < val:
                            eng.wait_ge(sem, val)
                            waited[key] = val

                    for o in ops:
                        for d in o.deps:
                            if self._needs_wait(d, o):
                                wait(d.sem, d.val)
                        if o.dma and o.prev_dma:
                            wait(o.sem, o.prev_dma)
                        ins = o.fn(eng)
                        if o.sig:
                            ins.then_inc(o.sem, 16 if o.dma else 1)
                    if e == "sp":
                        for e2 in ENGS:
                            for k in range(NDSEM):
                                if self.dcnt[e2][k]:
                                    wait(self.dsem[e2][k], self.dcnt[e2][k])

                getattr(block, engobj[e])(body)
        self.reset()


def make_consts(T):
    NT = T // 128
    c = {}
    c["ident"] = np.eye(128, dtype=np.float32)
    inv = (np.float32(10000.0) ** (-np.arange(0, 128, 2, dtype=np.float32) / np.float32(128))).astype(np.float32)
    pos = np.concatenate([np.arange(T, dtype=np.float32), PAST + np.arange(128, dtype=np.float32)])
    ang = (pos[:, None] * inv[None, :]).astype(np.float32)
    cs, sn = np.cos(ang).astype(np.float32), np.sin(ang).astype(np.float32)
    rope = np.zeros((T + 128, 2, 128), np.float32)
    rope[:, 0, :64] = cs
    rope[:, 0, 64:] = cs
    rope[:, 1, :64] = -sn
    rope[:, 1, 64:] = sn
    c["rope"] = rope
    k = np.arange(128)
    tri = (k[:, None] <= k[None, :]).astype(np.float32)
    c["tri"] = tri
    c["anti"] = (1.0 - tri).astype(np.float32)
    c["ustrict"] = (k[:, None] > k[None, :]).astype(np.float32)
    c["ones"] = np.ones((128, 128), np.float32)
    n_cmp = T // 16 - 1
    cm = np.zeros((128, NT, 2, 128), np.float32)
    for qt in range(NT):
        for it in range(2):
            i = 128 * it + k
            t = 128 * qt + k
            cm[:, qt, it, :] = ((16 * i[:, None] + 31 <= t[None, :]) & (i[:, None] < n_cmp)).astype(np.float32)
    c["cmpmask"] = cm.reshape(128, NT * 2 * 128)
    bon = np.zeros((128, NT, 64), np.float32)
    j = np.arange(64)
    for qt in range(NT):
        t = 128 * qt + k
        cur = t // 64
        valid = (64 * j[None, :] <= t[:, None])
        forced = (j[None, :] == 0) | (j[None, :] == cur[:, None]) | (j[None, :] == cur[:, None] - 1)
        bon[:, qt, :] = np.where(valid, 1.0e4 * forced, NEG)
    c["bonus"] = bon.reshape(128, NT * 64)
    keys = np.arange(T)
    c["e2"] = (keys[None, :] // 64 == j[:, None]).astype(np.float32)
    ov = np.zeros((128, 2, 64), np.float32)
    for it in range(2):
        i = 128 * it + k
        ov[:, it, :] = ((16 * i[:, None] < 64 * j[None, :] + 64) & (16 * i[:, None] + 32 > 64 * j[None, :])
                        & (i[:, None] < n_cmp)).astype(np.float32)
    c["overlap"] = ov.reshape(128, 128)
    pa = np.zeros((128, 3, 4, 128), np.float32)
    for gi, w in enumerate((2, 4, 8, 16)):
        s = k[:, None]
        t = k[None, :]
        inwin = (s <= t) & (s > t - w)
        pa[:, 0, gi, :] = inwin / w - (s == t)
        cnt = np.minimum(w, t + 1)
        pa[:, 1, gi, :] = inwin / cnt - (s == t)
        sg = k[64:, None] - 128
        pa[64:, 2, gi, :] = ((sg > t - w)).astype(np.float32) / w
    c["poolA"] = pa.reshape(128, 3 * 4 * 128)
    c["riota"] = np.stack([8.0 * k, 2.0 * k], axis=1).astype(np.float32)
    kk = np.arange(8192)
    c["e2big"] = (kk[None, :] // 64 == k[:, None]).astype(np.float32)
    ncs = PAST // 16 - 1
    js = np.arange(257)
    ovs = np.zeros((128, 8, 257), np.float32)
    for it in range(8):
        i = 128 * it + k
        ovs[:, it, :] = ((16 * i[:, None] < 64 * js[None, :] + 64) & (16 * i[:, None] + 32 > 64 * js[None, :])
                         & (i[:, None] < ncs)).astype(np.float32)
    c["ovs"] = ovs.reshape(128, 8 * 257)
    cur = PAST // 64
    c["bons"] = np.tile((1.0e4 * ((js == 0) | (js == cur) | (js == cur - 1))).astype(np.float32)[None, :], (128, 1))
    pas = np.zeros((128, 4, 128), np.float32)
    for gi, w in enumerate((2, 4, 8, 16)):
        for tq in range(4):
            sidx = 15 + tq
            pas[sidx - w + 1:sidx + 1, gi, tq] = 1.0 / w
            pas[sidx, gi, tq] -= 1.0
    c["poolAs"] = pas.reshape(128, 512)
    c["rowm"] = (k < 4).astype(np.float32).reshape(128, 1)
    return c


CONST_SHAPES = lambda T: {"ident": [128, 128], "rope": [T + 128, 2, 128], "tri": [128, 128], "anti": [128, 128],
                          "ustrict": [128, 128], "ones": [128, 128], "cmpmask": [128, (T // 128) * 256],
                          "bonus": [128, (T // 128) * 64], "e2": [64, T], "overlap": [128, 128],
                          "poolA": [128, 1536]}

WEIGHTS = {"norm_w": [D], "w_in": [D, NIN], "qk_gain": [4, 128], "cmp_pe": [2, 32, 128], "cmp_w": [2, 4096, 128],
           "pool_w": [4, 256, 256], "pool_scale": [BW], "conv_w": [4, 1536], "conv_b": [1536], "dt_bias": [16],
           "a_log": [16], "d_skip": [16], "mnorm_w": [BW], "w_branch": [3, BW, D], "w_out": [D, D]}


def build(T, L, debug=False, NPOOL=1280):
    NT = T // 128
    TT = T + 128
    NTT = NT + 1
    n_cmp = T // 16 - 1
    NW = min(4, NT)
    nc = bass.Bass("TRN2", target_bir_lowering=False)
    dt_ = lambda name, shape, kind, dtype=F32: nc.dram_tensor(name, shape, dtype, kind=kind)
    xin = dt_("xp", [TT, D], "ExternalInput")
    Wd = {k: dt_(k, [L] + v, "ExternalInput") for k, v in WEIGHTS.items()}
    Cd = {k: dt_("c_" + k, v, "ExternalInput") for k, v in CONST_SHAPES(T).items()}
    scr = "ExternalOutput" if debug else "Internal"
    Z = dt_("Z", [TT, NIN], scr)
    Z2 = dt_("Z2", [1536, TT], scr)
    QT = dt_("QT", [8, 128, TT], scr, BF16)
    KTs = dt_("KTs", [8, 128, TT], scr, BF16)
    VV = dt_("VV", [TT, 4, 128], scr, BF16)
    BRT = dt_("BRT", [24, 128, TT], scr, BF16)
    MT = dt_("MT", [16, 128, TT], scr, BF16)
    xb = [dt_(f"xb{i}", [TT, D], "Internal") for i in range(2)]
    dbgK = dt_("dbgK", [128, 512], "ExternalOutput", BF16) if debug else None
    dbgV = dt_("dbgV", [128, 772], "ExternalOutput", BF16) if debug else None
    dbgO = dt_("dbgO", [128, 265], "ExternalOutput") if debug else None
    dbgE = dt_("dbgE", [128, 512], "ExternalOutput", BF16) if debug else None
    dbgN = dt_("dbgN", [128, 1024], "ExternalOutput") if debug else None
    y_p = dt_("y_p", [T, D], "ExternalOutput")
    kv_p = dt_("kv_p", [L, T, 1024], "ExternalOutput")
    win_p = dt_("win_p", [L, NW * 128, 512], "ExternalOutput")
    pool_p = dt_("pool_p", [L, 15, BW], "ExternalOutput")
    conv_p = dt_("conv_p", [L, 3, 1536], "ExternalOutput")
    ssm_p = dt_("ssm_p", [L, 1024, 128], "ExternalOutput")
    I32 = mybir.dt.int32
    SM = dict(
        cache=dt_("cache", [L * NPOOL * 128 * 8, 128], "ExternalInput"),
        cwin=dt_("cwin", [L, 512, 512], "ExternalInput"),
        spool=dt_("spool", [L, 15, BW], "ExternalInput"),
        sconv=dt_("sconv", [L, 3, 1536], "ExternalInput"),
        sssm=dt_("sssm", [L, 1024, 128], "ExternalInput"),
        pt=dt_("pt", [1, 128], "ExternalInput", I32),
        riota=dt_("c_riota", [128, 2], "ExternalInput"),
        e2big=dt_("c_e2big", [128, 8192], "ExternalInput"),
        ovs=dt_("c_ovs", [128, 8 * 257], "ExternalInput"),
        bons=dt_("c_bons", [128, 257], "ExternalInput"),
        poolAs=dt_("c_poolAs", [128, 512], "ExternalInput"),
        rowm=dt_("c_rowm", [128, 1], "ExternalInput"),
        y_s=dt_("y_s", [4, D], "ExternalOutput"),
        kv_s=dt_("kv_s", [L, 4, 1024], "ExternalOutput"),
        win_s=dt_("win_s", [L, 512, 512], "ExternalOutput"),
        pool_s=dt_("pool_s", [L, 15, BW], "ExternalOutput"),
        conv_s=dt_("conv_s", [L, 3, 1536], "ExternalOutput"),
        ssm_s=dt_("ssm_s", [L, 1024, 128], "ExternalOutput"),
        NPOOL=NPOOL, I32=I32,
    )

    es = ExitStack()
    csem = {e: es.enter_context(nc.semaphore("c_" + e)) for e in ENGS}
    dsem = {e: [es.enter_context(nc.semaphore(f"d_{e}{k}")) for k in range(NDSEM)] for e in ("sp", "act", "pool")}
    for e in ENGS:
        dsem.setdefault(e, [None] * NDSEM)
    S = Sched(nc, csem, dsem)
    uid = [0]

    def U():
        uid[0] += 1
        return uid[0]

    def dma(eng, out, in_, reads=(), writes=(), **kw):
        return S.add(eng, I("dma_start", out=out, in_=in_, **kw), reads=reads, writes=writes, dma=True)

    def bc_row(dram_t, off, n):
        return AP(dram_t, off, [[0, 128], [1, n]])

    for l in range(L):
        x_src = xin if l == 0 else xb[(l - 1) % 2]
        x_dst = y_p if l == L - 1 else xb[l % 2]
        Wl = {k: (lambda k=k: Wd[k].ap()[l]) for k in Wd}
        woff = {k: l * int(np.prod(v)) for k, v in WEIGHTS.items()}

        with ExitStack() as ph:
            sb = lambda n, s, d: ph.enter_context(nc.sbuf_tensor(f"{n}_{l}", s, d))
            hT = sb("hT", [128, KT, TT], BF16)
            with ExitStack() as p1:
                sb1 = lambda n, s, d: p1.enter_context(nc.sbuf_tensor(f"{n}_{l}", s, d))
                xs = [sb1(f"xs{i}", [128, D], F32) for i in range(2)]
                junk = sb1("junk", [128, D], BF16)
                hb = [sb1(f"hb{i}", [128, D], BF16) for i in range(2)]
                ss = [sb1(f"ss{i}", [128, 1], F32) for i in range(2)]
                rs = [sb1(f"rs{i}", [128, 1], F32) for i in range(2)]
                epsb = sb1("eps", [128, 1], F32)
                nwb = sb1("nwb", [128, D], F32)
                idf = sb1("idf", [128, 128], F32)
                idb = sb1("idb", [128, 128], BF16)
                pst = [p1.enter_context(nc.psum_tensor(f"pt{i}_{l}", [128, 1024], BF16)) for i in range(2)]
                dma("sp", idf[:], Cd["ident"].ap(), writes=["idf"])
                dma("sp", nwb[:], bc_row(Wd["norm_w"], woff["norm_w"], D), writes=["nwb"])
                S.add("dve", I("tensor_copy", out=idb[:], in_=idf[:]), reads=["idf"], writes=["idb"])
                S.add("dve", I("memset", epsb[:], EPS), writes=["eps"])
                dma("sp", xs[0][:], x_src.ap()[0:128, :], writes=[("xs", 0)])
                for t in range(NTT):
                    s = t % 2
                    if t + 1 < NTT:
                        dma("sp", xs[1 - s][:], x_src.ap()[(t + 1) * 128:(t + 2) * 128, :], writes=[("xs", 1 - s)])
                    S.add("act", I("activation", out=junk[:], in_=xs[s][:], func=AF.Square, accum_out=ss[s][:]),
                          reads=[("xs", s)], writes=["junk", ("ss", s)])
                    S.add("act", I("activation", out=rs[s][:], in_=ss[s][:], func=AF.Sqrt, bias=epsb[:], scale=1.0 / D),
                          reads=[("ss", s), "eps"], writes=[("rs", s)])
                    S.add("dve", I("reciprocal", out=rs[s][:], in_=rs[s][:]), reads=[("rs", s)], writes=[("rs", s)])
                    S.add("dve", I("scalar_tensor_tensor", out=hb[s][:], in0=xs[s][:], scalar=rs[s][:], in1=nwb[:],
                                                                         op0=ALU.mult, op1=ALU.mult),
                          reads=[("xs", s), ("rs", s), "nwb"], writes=[("hb", s)])
                    for half in range(2):
                        for j in range(8):
                            kt = half * 8 + j
                            S.add("pe", I("transpose",
                                out=pst[half][:, j * 128:(j + 1) * 128], in_=hb[s][:, kt * 128:(kt + 1) * 128], identity=idb[:]),
                                reads=[("hb", s), "idb"], writes=[("pst", half)])
                        src = AP(pst[half], 0, [[1024, 128], [128, 8], [1, 128]])
                        dst = hT[:, half * 8:(half + 1) * 8, t * 128:(t + 1) * 128]
                        if half == 0:
                            S.add("act", I("copy", out=dst, in_=src), reads=[("pst", half)], writes=[("hT", t)])
                        else:
                            S.add("dve", I("tensor_copy", out=dst, in_=src), reads=[("pst", half)], writes=[("hT", t)])
                S.emit()
            with ExitStack() as p2:
                sb2 = lambda n, s, d: p2.enter_context(nc.sbuf_tensor(f"{n}_{l}", s, d))
                wb = [sb2(f"wb{i}", [128, KT, 512], BF16) for i in range(2)]
                st = [sb2(f"st{i}", [128, 512], F32) for i in range(4)]
                ps = [p2.enter_context(nc.psum_tensor(f"ps{i}_{l}", [128, 512], F32)) for i in range(4)]
                chunks = []
                for (a, b_, fm) in ((0, C_XBC, False), (C_XBC, C_DT, True), (C_DT, NIN, False)):
                    c0 = a
                    while c0 < b_:
                        wd = min(512, b_ - c0)
                        chunks.append((c0, wd, fm))
                        c0 += wd
                n = 0
                TW = min(512, T)
                for ci, (c0, wd, fm) in enumerate(chunks):
                    s = ci % 2
                    dma("pool", wb[s][:, :, 0:wd], AP(Wd["w_in"], woff["w_in"] + c0, [[NIN, 128], [128 * NIN, KT], [1, wd]]),
                        writes=[("wb", s)])
                    if not fm:
                        for t in range(NTT):
                            b = n % 4
                            n += 1
                            for kt in range(KT):
                                S.add("pe", I("matmul",
                                    ps[b][:, 0:wd], lhsT=hT[:, kt, t * 128:(t + 1) * 128], rhs=wb[s][:, kt, 0:wd],
                                    start=(kt == 0), stop=(kt == KT - 1)), reads=[("wb", s)], writes=[("ps", b)])
                            if b % 2 == 0:
                                S.add("act", I("copy", out=st[b][:, 0:wd], in_=ps[b][:, 0:wd]),
                                      reads=[("ps", b)], writes=[("st", b)])
                            else:
                                S.add("dve", I("tensor_copy", out=st[b][:, 0:wd], in_=ps[b][:, 0:wd]),
                                      reads=[("ps", b)], writes=[("st", b)])
                            dma("sp", Z.ap()[t * 128:(t + 1) * 128, c0:c0 + wd], st[b][:, 0:wd], reads=[("st", b)])
                    else:
                        for sub in range(wd // 128):
                            for (q0, qw) in [(tt * TW, TW) for tt in range(T // TW)] + [(T, 128)]:
                                b = n % 4
                                n += 1
                                for kt in range(KT):
                                    S.add("pe", I("matmul",
                                        ps[b][:, 0:qw], lhsT=wb[s][:, kt, sub * 128:(sub + 1) * 128],
                                        rhs=hT[:, kt, q0:q0 + qw], start=(kt == 0), stop=(kt == KT - 1)),
                                        reads=[("wb", s)], writes=[("ps", b)])
                                if b % 2 == 0:
                                    S.add("act", I("copy", out=st[b][:, 0:qw], in_=ps[b][:, 0:qw]),
                                          reads=[("ps", b)], writes=[("st", b)])
                                else:
                                    S.add("dve", I("tensor_copy", out=st[b][:, 0:qw], in_=ps[b][:, 0:qw]),
                                          reads=[("ps", b)], writes=[("st", b)])
                                r0 = c0 - C_XBC + sub * 128
                                dma("sp", Z2.ap()[r0:r0 + 128, q0:q0 + qw], st[b][:, 0:qw], reads=[("st", b)])
                S.emit()

        with ExitStack() as ph:
            sb = lambda n, s, d: ph.enter_context(nc.sbuf_tensor(f"{n}_{l}", s, d))
            zt = [sb(f"zt{i}", [128, 2560], F32) for i in range(2)]
            rp = [sb(f"rp{i}", [128, 2, 128], F32) for i in range(2)]
            sq = sb("sq", [128, 2560], F32)
            tmp = sb("tmp", [128, 1024], F32)
            zb = [sb(f"zb{i}", [128, 2560], BF16) for i in range(2)]
            ssq = sb("ssq", [128, 20], F32)
            rst = sb("rst", [128, 20], F32)
            epsb = sb("eps3", [128, 1], F32)
            gq = sb("gq", [128, 128], F32)
            gk = sb("gk", [128, 3, 128], F32)
            idf = sb("idf3", [128, 128], F32)
            idb = sb("idb3", [128, 128], BF16)
            tst = [sb(f"tst{i}", [128, 16, 128], BF16) for i in range(2)]
            psA = ph.enter_context(nc.psum_tensor(f"p3a_{l}", [128, 1024], BF16))
            psB = ph.enter_context(nc.psum_tensor(f"p3b_{l}", [128, 1024], BF16))
            dma("sp", idf[:], Cd["ident"].ap(), writes=["idf"])
            S.add("dve", I("tensor_copy", out=idb[:], in_=idf[:]), reads=["idf"], writes=["idb"])
            S.add("dve", I("memset", epsb[:], EPS), writes=["eps"])
            dma("sp", gq[:], bc_row(Wd["qk_gain"], woff["qk_gain"], 128), writes=["gq"])
            dma("sp", gk[:], AP(Wd["qk_gain"], woff["qk_gain"] + 128, [[0, 128], [128, 3], [1, 128]]), writes=["gk"])
            S.add("act", I("mul", out=gq[:], in_=gq[:], mul=128.0 ** -0.5), reads=["gq"], writes=["gq"])

            def p3_load(t):
                s = t % 2
                dma("sp", zt[s][:], Z.ap()[t * 128:(t + 1) * 128, 0:2560], writes=[("zt", s)])
                dma("sp", rp[s][:], Cd["rope"].ap()[t * 128:(t + 1) * 128], writes=[("rp", s)])

            p3_load(0)
            kheads = [(8, 2), (12, 2), (16, 2)]
            for t in range(NTT):
                s = t % 2
                if t + 1 < NTT:
                    p3_load(t + 1)
                z = zt[s]
                S.add("act", I("activation", out=sq[:], in_=z[:], func=AF.Square), reads=[("zt", s)], writes=["sq"])
                S.add("dve", I("tensor_reduce", out=ssq[:], in_=AP(sq, 0, [[2560, 128], [128, 20], [1, 128]]),
                                                       axis=AX.X, op=ALU.add), reads=["sq"], writes=["ssq"])
                S.add("act", I("activation", out=rst[:], in_=ssq[:], func=AF.Sqrt, bias=epsb[:], scale=1.0 / 128),
                      reads=["ssq", "eps"], writes=["rst"])
                S.add("dve", I("reciprocal", out=rst[:], in_=rst[:]), reads=["rst"], writes=["rst"])
                grp = [
                    dict(zoff=0, zmid=[[128, 8]], roff=0, rmid=[[1, 8]], g=gq, gpitch=128, gmid=[[0, 8]], tmid=[[128, 8]], bmid=[[0, 8]]),
                    dict(zoff=1024, zmid=[[512, 3], [128, 2]], roff=8, rmid=[[4, 3], [1, 2]], g=gk, gpitch=384, gmid=[[128, 3], [0, 2]],
                         tmid=[[256, 3], [128, 2]], bmid=[[0, 3], [0, 2]]),
                ]
                for G in grp:
                    zv = AP(z, G["zoff"], [[2560, 128]] + G["zmid"] + [[1, 128]])
                    rv = AP(rst, G["roff"], [[20, 128]] + G["rmid"] + [[0, 128]])
                    gap = AP(G["g"], 0, [[G["gpitch"], 128]] + G["gmid"] + [[1, 128]])
                    S.add("dve", I("tensor_tensor", out=zv, in0=zv, in1=rv, op=ALU.mult),
                          reads=[("zt", s), "rst"], writes=[("zt", s)])
                    S.add("dve", I("tensor_tensor", out=zv, in0=zv, in1=gap, op=ALU.mult),
                          reads=[("zt", s), "gq", "gk"], writes=[("zt", s)])
                    lo = AP(z, G["zoff"], [[2560, 128]] + G["zmid"] + [[1, 64]])
                    hi = AP(z, G["zoff"] + 64, [[2560, 128]] + G["zmid"] + [[1, 64]])
                    tlo = AP(tmp, 0, [[1024, 128]] + G["tmid"] + [[1, 64]])
                    thi = AP(tmp, 64, [[1024, 128]] + G["tmid"] + [[1, 64]])
                    tv = AP(tmp, 0, [[1024, 128]] + G["tmid"] + [[1, 128]])
                    sslo = AP(rp[s], 128, [[256, 128]] + G["bmid"] + [[1, 64]])
                    sshi = AP(rp[s], 192, [[256, 128]] + G["bmid"] + [[1, 64]])
                    cc = AP(rp[s], 0, [[256, 128]] + G["bmid"] + [[1, 128]])
                    S.add("dve", I("tensor_tensor", out=tlo, in0=hi, in1=sslo, op=ALU.mult),
                          reads=[("zt", s), ("rp", s)], writes=["tmp"])
                    S.add("dve", I("tensor_tensor", out=thi, in0=lo, in1=sshi, op=ALU.mult),
                          reads=[("zt", s), ("rp", s)], writes=["tmp"])
                    S.add("dve", I("tensor_tensor", out=zv, in0=zv, in1=cc, op=ALU.mult),
                          reads=[("zt", s), ("rp", s), "tmp"], writes=[("zt", s)])
                    S.add("dve", I("tensor_tensor", out=zv, in0=zv, in1=tv, op=ALU.add),
                          reads=[("zt", s), "tmp"], writes=[("zt", s)])
                S.add("act", I("copy", out=zb[s][:], in_=z[:]), reads=[("zt", s)], writes=[("zb", s)])
                if t < NT:
                    dma("sp", kv_p.ap()[l, t * 128:(t + 1) * 128, :], z[:, 1024:2048], reads=[("zt", s)])
                    if t >= NT - NW:
                        r0 = (t - (NT - NW)) * 128
                        dma("sp", win_p.ap()[l, r0:r0 + 128, :], z[:, 2048:2560], reads=[("zt", s)])
                else:
                    dma("sp", SM["kv_s"].ap()[l], z[0:4, 1024:2048], reads=[("zt", s)])
                    dma("sp", SM["win_s"].ap()[l, 508:512, :], z[0:4, 2048:2560], reads=[("zt", s)])
                    for wq in range(4):
                        dma("sp", sq[0:127, 0:512], SM["cwin"].ap()[l, 4 + wq * 127:4 + (wq + 1) * 127, :], reads=["sq"], writes=["sq"])
                        dma("sp", SM["win_s"].ap()[l, wq * 127:(wq + 1) * 127, :], sq[0:127, 0:512], reads=["sq"], writes=["sq"])
                heads = list(range(8)) + [8, 9, 10, 11, 12, 13, 16, 17]
                for j, h in enumerate(heads):
                    pt = psA if j < 8 else psB
                    jj = j % 8
                    S.add("pe", I("transpose",
                        out=pt[:, jj * 128:(jj + 1) * 128], in_=zb[s][:, h * 128:(h + 1) * 128], identity=idb[:]),
                        reads=[("zb", s), "idb"], writes=[("pt", j // 8)])
                S.add("act", I("copy", out=tst[s][:, 0:8, :], in_=AP(psA, 0, [[1024, 128], [128, 8], [1, 128]])),
                      reads=[("pt", 0)], writes=[("tstq", s)])
                S.add("dve", I("tensor_copy", out=tst[s][:, 8:16, :], in_=AP(psB, 0, [[1024, 128], [128, 8], [1, 128]])),
                      reads=[("pt", 1)], writes=[("tstk", s)])
                dma("sp", AP(QT, t * 128, [[TT, 128], [128 * TT, 8], [1, 128]]), tst[s][:, 0:8, :], reads=[("tstq", s)])
                dma("sp", AP(KTs, t * 128, [[TT, 128], [128 * TT, 8], [1, 128]]), tst[s][:, 8:16, :], reads=[("tstk", s)])
                dma("sp", VV.ap()[t * 128:(t + 1) * 128, 0:2, :], AP(zb[s], 1792, [[2560, 128], [128, 2], [1, 128]]), reads=[("zb", s)])
                dma("sp", VV.ap()[t * 128:(t + 1) * 128, 2:4, :], AP(zb[s], 2304, [[2560, 128], [128, 2], [1, 128]]), reads=[("zb", s)])
            S.emit()

        build_rest(nc, S, dict(l=l, T=T, NT=NT, TT=TT, NTT=NTT, SM=SM, n_cmp=n_cmp, L=L, Wd=Wd, woff=woff, Cd=Cd, Z=Z, Z2=Z2, QT=QT, KTs=KTs, VV=VV,
                               BRT=BRT, MT=MT, x_src=x_src, x_dst=x_dst, pool_p=pool_p, conv_p=conv_p, ssm_p=ssm_p,
                               dma=dma, bc_row=bc_row, dbgK=dbgK, dbgV=dbgV, dbgO=dbgO, dbgE=dbgE, dbgN=dbgN))
    es.close()
    return nc


def build_rest(nc, S, env):
    l, T, NT, n_cmp, L = env["l"], env["T"], env["NT"], env["n_cmp"], env["L"]
    TT, NTT, SM = env["TT"], env["NTT"], env["SM"]
    Wd, woff, Cd = env["Wd"], env["woff"], env["Cd"]
    Z, Z2, QT, KTs, VV, BRT, MT = env["Z"], env["Z2"], env["QT"], env["KTs"], env["VV"], env["BRT"], env["MT"]
    x_src, x_dst = env["x_src"], env["x_dst"]
    pool_p, conv_p, ssm_p = env["pool_p"], env["conv_p"], env["ssm_p"]
    dma, bc_row = env["dma"], env["bc_row"]
    NIT = (n_cmp + 127) // 128

    with ExitStack() as att:
        sba = lambda n, s, d: att.enter_context(nc.sbuf_tensor(f"{n}_{l}", s, d))
        KcT = sba("KcT", [128, 2, 256], BF16)
        Vaug = sba("Vaug", [128, 2, 2, 193], BF16)
        idf = sba("idf4", [128, 128], F32)
        idb = sba("idb4", [128, 128], BF16)
        with ExitStack() as ph:
            sb = lambda n, s, d: ph.enter_context(nc.sbuf_tensor(f"{n}_{l}", s, d))
            XT = sb("XT", [128, 4, T], BF16)
            Wc = sb("Wc", [128, 2, 32, 128], BF16)
            pe_f = sb("pe_f", [64, 128], F32)
            pe_b = sb("pe_b", [64, 128], BF16)
            peT = sb("peT", [128, 64], BF16)
            VcT = sb("VcT", [128, 2, 256], BF16)
            bias = sb("bias", [128, 2], F32)
            pc = [ph.enter_context(nc.psum_tensor(f"pc{i}_{l}", [128, 512], F32)) for i in range(2)]
            pbt = ph.enter_context(nc.psum_tensor(f"pbt_{l}", [128, 1024], BF16))
            dma("sp", idf[:], Cd["ident"].ap(), writes=["idf"])
            S.add("dve", I("tensor_copy", out=idb[:], in_=idf[:]), reads=["idf"], writes=["idb"])
            dma("sp", XT[:], AP(KTs, 0, [[TT, 128], [128 * TT, 4], [1, T]]), writes=["XT"])
            dma("pool", Wc[:], AP(Wd["cmp_w"], woff["cmp_w"], [[128, 128], [4096 * 128, 2], [128 * 128, 32], [1, 128]]), writes=["Wc"])
            dma("sp", pe_f[:], AP(Wd["cmp_pe"], woff["cmp_pe"], [[128, 64], [1, 128]]), writes=["pe_f"])
            S.add("dve", I("tensor_copy", out=pe_b[:], in_=pe_f[:]), reads=["pe_f"], writes=["pe_b"])
            S.add("pe", I("transpose", out=pbt[:, 0:64], in_=pe_b[:], identity=idb[0:64, 0:64]), reads=["pe_b", "idb"], writes=["pbt"])
            S.add("dve", I("tensor_copy", out=peT[:], in_=pbt[:, 0:64]), reads=["pbt"], writes=["peT"])
            S.add("pool", I("memset", Vaug[:], 0.0), writes=["Vaug"])
            S.add("pool", I("memset", KcT[:], 0.0), writes=["KcT"])
            S.add("pool", I("memset", VcT[:], 0.0), writes=["VcT"])
            S.add("pool", I("memset", AP(Vaug, 128, [[772, 128], [193, 4], [1, 1]]), 1.0), writes=["Vaug"])
            for g in range(2):
                dma("pool", AP(Vaug, g * 193 + 129, [[772, 128], [386, 2], [1, 64]]),
                    AP(Cd["overlap"], 0, [[128, 128], [64, 2], [1, 64]]), writes=["Vaug"])
            for c in range(2):
                for ll in range(32):
                    S.add("pe", I("matmul", pc[0][:, c:c + 1], lhsT=Wc[:, c, ll, :], rhs=peT[:, c * 32 + ll:c * 32 + ll + 1],
                                                                start=(ll == 0), stop=(ll == 31)), reads=["Wc", "peT"], writes=[("pc", 0)])
            S.add("dve", I("tensor_copy", out=bias[:], in_=pc[0][:, 0:2]), reads=[("pc", 0)], writes=["bias"])
            n = 1
            for c in range(2):
                for g in range(2):
                    b = n % 2
                    n += 1
                    for ll in range(32):
                        S.add("pe", I("matmul",
                            pc[b][:, 0:n_cmp], lhsT=Wc[:, c, ll, :], rhs=AP(XT, (c * 2 + g) * T + ll, [[4 * T, 128], [16, n_cmp]]),
                            start=(ll == 0), stop=(ll == 31)), reads=["Wc", "XT"], writes=[("pc", b)])
                    dst = KcT if c == 0 else VcT
                    S.add("act", I("activation",
                        out=dst[:, g, 0:n_cmp], in_=pc[b][:, 0:n_cmp], func=AF.Identity, bias=bias[:, c:c + 1]),
                        reads=[("pc", b), "bias"], writes=["KcT" if c == 0 else "VcT"])
            for g in range(2):
                for it in range(NIT):
                    S.add("pe", I("transpose", out=pbt[:, 128:256], in_=VcT[:, g, it * 128:(it + 1) * 128], identity=idb[:]),
                          reads=["VcT", "idb"], writes=["pbt"])
                    S.add("dve", I("tensor_copy", out=Vaug[:, it, g, 0:128], in_=pbt[:, 128:256]),
                          reads=["pbt"], writes=["Vaug"])
            if env.get("dbgK") is not None:
                dma("sp", env["dbgK"].ap(), AP(KcT, 0, [[512, 128], [1, 512]]), reads=["KcT"])
                dma("sp", env["dbgV"].ap(), AP(Vaug, 0, [[772, 128], [1, 772]]), reads=["Vaug"])
            S.emit()

        with ExitStack() as ph:
            sb = lambda n, s, d: ph.enter_context(nc.sbuf_tensor(f"{n}_{l}", s, d))
            KsT = sb("KsT", [128, 2, T], BF16)
            KwT = sb("KwT", [128, 2, T], BF16)
            Vs = sb("Vs", [128, NT, 2, 129], BF16)
            Vw = sb("Vw", [128, NT, 2, 129], BF16)
            E2 = sb("E2", [128, T], BF16)
            cmk = sb("cmk", [128, NT * 256], BF16)
            bon = sb("bon", [128, NT * 64], F32)
            tri = sb("tri", [128, 128], BF16)
            anti = sb("anti", [128, 128], BF16)
            qsb = [sb(f"qsb{i}", [128, 8, 128], BF16) for i in range(2)]
            gt = [sb(f"gt{i}", [128, 24], F32) for i in range(2)]
            nz = [sb(f"nz{i}", [128, 1024], F32) for i in range(2)]
            nsa = sb("nsa", [128, 8, 128], F32)
            ec = [sb(f"ec{i}", [128, 4, 128], BF16) for i in range(2)]
            esb = [sb(f"es{i}", [128, 4, 128], BF16) for i in range(2)]
            em = [sb(f"em{i}", [128, 4, 128], BF16) for i in range(2)]
            ew = [sb(f"ew{i}", [128, 4, 128], BF16) for i in range(2)]
            rc = sb("rc", [128, 4], F32)
            coef = sb("coef", [128, 4], F32)
            imp = sb("imp", [128, 64], F32)
            imp2 = sb("imp2", [128, 64], F32)
            m8a = sb("m8a", [128, 8], F32)
            m8b = sb("m8b", [128, 8], F32)
            sel = sb("sel", [128, 64], F32)
            selT = sb("selT", [64, 128], BF16)
            brs = [sb(f"brs{i}", [128, 8, 128], BF16) for i in range(2)]
            psS = [ph.enter_context(nc.psum_tensor(f"psS{i}_{l}", [128, 512], F32)) for i in range(2)]
            psX = ph.enter_context(nc.psum_tensor(f"psX_{l}", [128, 512], F32))
            psO = [ph.enter_context(nc.psum_tensor(f"psO{i}_{l}", [128, 512], F32)) for i in range(4)]
            psX2 = ph.enter_context(nc.psum_tensor(f"psX2_{l}", [128, 512], F32))
            psXm = [psX, psX2]
            dma("sp", KsT[:], AP(KTs, 4 * 128 * TT, [[TT, 128], [128 * TT, 2], [1, T]]), writes=["KsT"])
            dma("sp", KwT[:], AP(KTs, 6 * 128 * TT, [[TT, 128], [128 * TT, 2], [1, T]]), writes=["KwT"])
            S.add("pool", I("memset", AP(Vs, 128, [[NT * 258, 128], [129, NT * 2], [1, 1]]), 1.0), writes=["Vs1"])
            S.add("pool", I("memset", AP(Vw, 128, [[NT * 258, 128], [129, NT * 2], [1, 1]]), 1.0), writes=["Vw1"])
            for g in range(2):
                dma("sp", AP(Vs, g * 129, [[NT * 258, 128], [258, NT], [1, 128]]), AP(VV, g * 128, [[512, 128], [128 * 512, NT], [1, 128]]), writes=["Vs"])
                dma("sp", AP(Vw, g * 129, [[NT * 258, 128], [258, NT], [1, 128]]), AP(VV, (2 + g) * 128, [[512, 128], [128 * 512, NT], [1, 128]]), writes=["Vw"])
            S.add("pool", I("memset", E2[64:128, :], 0.0), writes=["E2"])
            dma("pool", E2[0:64, :], Cd["e2"].ap(), writes=["E2"])
            dma("pool", cmk[:], Cd["cmpmask"].ap(), writes=["cmk"])
            dma("sp", bon[:], Cd["bonus"].ap(), writes=["bon"])
            dma("pool", tri[:], Cd["tri"].ap(), writes=["tri"])
            dma("pool", anti[:], Cd["anti"].ap(), writes=["anti"])
            VR = ["Vs", "Vs1", "Vw", "Vw1"]

            def p5_load(qt):
                s = qt % 2
                dma("sp", qsb[s][:], AP(QT, qt * 128, [[TT, 128], [128 * TT, 8], [1, 128]]), writes=[("qsb", s)])
                dma("sp", gt[s][:], Z.ap()[qt * 128:(qt + 1) * 128, C_G:C_G + 24], writes=[("gt", s)])
                dma("sp", nz[s][:], Z.ap()[qt * 128:(qt + 1) * 128, C_NZ:C_NZ + 1024], writes=[("nz", s)])

            def scores(kT, g, kt, s, b, nrow=128, stop=True):
                S.add("pe", I("matmul", psS[b][0:nrow, :], lhsT=kT[:, g, kt * 128:kt * 128 + nrow],
                                               rhs=AP(qsb[s], 4 * g * 128, [[1024, 128], [1, 512]]), start=True, stop=stop),
                      reads=[("qsb", s), "KsT", "KwT", "KcT"], writes=[("psS", b)])

            selTb = sb("selTb", [128, 4, 128], BF16)
            S.add("pool", I("memset", selTb[:], 0.0), writes=["selT"])
            antiB = sb("antiB", [128, 4, 128], BF16)
            S.add("act", I("activation", out=antiB[:], in_=AP(anti, 0, [[128, 128], [0, 4], [1, 128]]), func=AF.Identity, scale=-30000.0),
                  reads=["anti"], writes=["antiB"])

            def accsel(branch, h):
                if branch == "slc":
                    return psO[h // 2], (h % 2) * 129, ("psO", h // 2)
                if branch == "win":
                    return psO[2 + h // 2], (h % 2) * 129, ("psO", 2 + h // 2)
                return psO[2 + h // 2], (h % 2) * 193, ("psO", 2 + h // 2)

            def pv(src, Vt, kt, g, first, last, width, branch, nrow=128, vap=None):
                for h in range(4):
                    rhs = vap if vap is not None else Vt[:, kt, g, :]
                    bank, off, res = accsel(branch, h)
                    S.add("pe", I("matmul", bank[:, off:off + width], lhsT=src[0:nrow, h, :], rhs=rhs,
                                  start=(first and h % 2 == 0), stop=last, skip_group_check=True),
                          reads=[src.name] + VR + ["Vaug"], writes=[res])

            def finish(gcol, g, s, first, branch):
                for h in range(4):
                    bank, off, res = accsel(branch, h)
                    S.add("dve", I("tensor_scalar", out=rc2[branch][:, h:h + 1], in0=bank[:, off + 128:off + 129], scalar1=1e-30, scalar2=None,
                                   op0=ALU.max), reads=[res], writes=[("rc", branch)])
                S.add("dve", I("reciprocal", out=rc2[branch][:], in_=rc2[branch][:]), reads=[("rc", branch)], writes=[("rc", branch)])
                S.add("dve", I("tensor_tensor", out=cf2[branch][:], in0=rc2[branch][:], in1=AP(gt[s], 12 * g + gcol, [[24, 128], [3, 4]]), op=ALU.mult),
                      reads=[("rc", branch), ("gt", s)], writes=[("coef", branch)])
                for h in range(4):
                    hh = 4 * g + h
                    bank, off, res = accsel(branch, h)
                    S.add("dve", I("scalar_tensor_tensor", out=nsa[:, hh, :], in0=bank[:, off:off + 128], scalar=cf2[branch][:, h:h + 1],
                                   in1=nsa[:, hh, :], op0=ALU.mult, op1=ALU.add),
                          reads=[res, ("coef", branch), ("nsa", hh)], writes=[("nsa", hh)])

            rc2 = {"slc": sb("rc_s", [128, 4], F32), "win": sb("rc_w", [128, 4], F32)}
            cf2 = {"slc": sb("cf_s", [128, 4], F32), "win": sb("cf_w", [128, 4], F32)}

            p5_load(0)
            for qt in range(NT):
                s = qt % 2
                if qt + 1 < NT:
                    p5_load(qt + 1)
                S.add("act", I("activation", out=gt[s][:], in_=gt[s][:], func=AF.Sigmoid), reads=[("gt", s)], writes=[("gt", s)])
                S.add("act", I("activation", out=nz[s][:], in_=nz[s][:], func=AF.Silu), reads=[("nz", s)], writes=[("nz", s)])
                for g in range(2):
                    its = [it for it in range(NIT) if 128 * it <= 8 * qt + 6]
                    for it in its:
                        ni = 128
                        scores(KcT, g, it, s, it, nrow=ni)
                        S.add("act", I("activation", out=ec[it][0:ni], in_=psS[it][0:ni, :], func=AF.Exp),
                              reads=[("psS", it)], writes=[ec[it].name])
                        S.add("dve", I("tensor_tensor",
                            out=ec[it][0:ni], in0=ec[it][0:ni], in1=AP(cmk, (qt * 2 + it) * 128, [[NT * 256, ni], [0, 4], [1, 128]]), op=ALU.mult),
                            reads=[ec[it].name, "cmk"], writes=[ec[it].name])
                    for h in range(4):
                        for k_, it in enumerate(its):
                            ni = 128
                            bank, off, res = accsel("cmp", h)
                            S.add("pe", I("matmul", bank[:, off:off + 193], lhsT=ec[it][0:ni, h, :], rhs=Vaug[0:ni, it, g, :],
                                          start=(k_ == 0 and h % 2 == 0), stop=(k_ == len(its) - 1), skip_group_check=True),
                                  reads=[ec[it].name, "Vaug"], writes=[res])
                    for h in range(4):
                        bank, off, res = accsel("cmp", h)
                        S.add("dve", I("tensor_scalar", out=rc[:, h:h + 1], in0=bank[:, off + 128:off + 129], scalar1=1e-30, scalar2=None,
                                       op0=ALU.max), reads=[res], writes=["rc"])
                    S.add("dve", I("reciprocal", out=rc[:], in_=rc[:]), reads=["rc"], writes=["rc"])
                    for h in range(4):
                        bank, off, res = accsel("cmp", h)
                        if h == 0:
                            S.add("dve", I("tensor_scalar", out=imp[:], in0=bank[:, off + 129:off + 193], scalar1=rc[:, 0:1], scalar2=None, op0=ALU.mult),
                                  reads=[res, "rc"], writes=["imp"])
                        else:
                            S.add("dve", I("scalar_tensor_tensor", out=imp[:], in0=bank[:, off + 129:off + 193], scalar=rc[:, h:h + 1], in1=imp[:],
                                           op0=ALU.mult, op1=ALU.add), reads=[res, "rc", "imp"], writes=["imp"])
                    S.add("dve", I("tensor_tensor", out=imp[:], in0=imp[:], in1=bon[:, qt * 64:(qt + 1) * 64], op=ALU.add),
                          reads=["imp", "bon"], writes=["imp"])
                    S.add("dve", I("tensor_tensor", out=coef[:], in0=rc[:], in1=AP(gt[s], 12 * g + 0, [[24, 128], [3, 4]]), op=ALU.mult),
                          reads=["rc", ("gt", s)], writes=["coef"])
                    for h in range(4):
                        hh = 4 * g + h
                        bank, off, res = accsel("cmp", h)
                        S.add("dve", I("tensor_scalar", out=nsa[:, hh, :], in0=bank[:, off:off + 128], scalar1=coef[:, h:h + 1],
                                       scalar2=None, op0=ALU.mult), reads=[res, "coef"], writes=[("nsa", hh)])
                    if False and env.get("dbgK") is not None and (qt, g) == env.get("dbgsel", (1, 0)):
                        dbo = sb("dbo", [128, 193 + 8 + 64], F32)
                        S.add("dve", I("tensor_copy", out=dbo[:, 0:193], in_=psO[0][:, 0:193]), reads=[("psO", 0)], writes=["dbo"])
                        S.add("dve", I("tensor_copy", out=dbo[:, 193:197], in_=rc[:]), reads=["rc"], writes=["dbo"])
                        S.add("dve", I("tensor_copy", out=dbo[:, 197:201], in_=coef[:]), reads=["coef"], writes=["dbo"])
                        S.add("dve", I("tensor_copy", out=dbo[:, 201:265], in_=imp[:]), reads=["imp"], writes=["dbo"])
                        dma("sp", env["dbgO"].ap(), dbo[:], reads=["dbo"])
                        dma("sp", env["dbgE"].ap(), AP(ec[0], 0, [[512, 128], [1, 512]]), reads=[ec[0].name])
                        dma("sp", env["dbgN"].ap(), AP(nsa, 0, [[1024, 128], [1, 1024]]), reads=[("nsa", h) for h in range(8)])
                    S.add("dve", I("max", out=m8a[:], in_=imp[:]), reads=["imp"], writes=["m8a"])
                    S.add("dve", I("match_replace", out=imp2[:], in_to_replace=m8a[:], in_values=imp[:], imm_value=NEG),
                          reads=["imp", "m8a"], writes=["imp2"])
                    S.add("dve", I("max", out=m8b[:], in_=imp2[:]), reads=["imp2"], writes=["m8b"])
                    S.add("dve", I("tensor_scalar", out=sel[:], in0=imp[:], scalar1=m8b[:, 7:8], scalar2=None, op0=ALU.is_ge),
                          reads=["imp", "m8b"], writes=["sel"])
                    S.add("pe", I("transpose", out=psX[0:64, 128:256], in_=sel[:], identity=idf[:]), reads=["sel", "idf"], writes=[("psX", 0)])
                    S.add("act", I("activation", out=selTb[0:64], in_=AP(psX, 128, [[512, 64], [0, 4], [1, 128]]), func=AF.Identity,
                                   scale=30000.0, bias=-30000.0), reads=[("psX", 0)], writes=["selT"])
                    def slc_front(kt):
                        b = kt % 2
                        scores(KsT, g, kt, s, b, stop=False)
                        S.add("pe", I("matmul", psS[b][:], lhsT=E2[:, kt * 128:(kt + 1) * 128], rhs=AP(selTb, 0, [[512, 128], [1, 512]]),
                                      start=False, stop=(kt != qt)), reads=["E2", "selT"], writes=[("psS", b)])
                        if kt == qt:
                            S.add("pe", I("matmul", psS[b][:], lhsT=idb[:], rhs=AP(antiB, 0, [[512, 128], [1, 512]]), start=False, stop=True),
                                  reads=["idb", "antiB"], writes=[("psS", b)])
                        S.add("act", I("activation", out=esb[b][:], in_=psS[b][:], func=AF.Exp), reads=[("psS", b)], writes=[esb[b].name])

                    def slc_back(kt):
                        pv(esb[kt % 2], Vs, kt, g, kt == 0, kt == qt, 129, "slc")

                    k0 = max(0, qt - 4)

                    def win_front(kt):
                        b = kt % 2
                        scores(KwT, g, kt, s, b)
                        S.add("act", I("activation", out=ew[b][:], in_=psS[b][:], func=AF.Exp), reads=[("psS", b)], writes=[ew[b].name])
                        if kt == qt or kt == qt - 4:
                            mk = tri if kt == qt else anti
                            S.add("pool", I("tensor_tensor", out=ew[b][:], in0=ew[b][:], in1=AP(mk, 0, [[128, 128], [0, 4], [1, 128]]), op=ALU.mult),
                                  reads=[ew[b].name, "tri", "anti"], writes=[ew[b].name])

                    slc_front(0)
                    for kt in range(qt + 1):
                        if kt + 1 <= qt:
                            slc_front(kt + 1)
                        slc_back(kt)
                    win_front(k0)
                    finish(1, g, s, False, "slc")
                    for kt in range(k0, qt + 1):
                        if kt + 1 <= qt:
                            win_front(kt + 1)
                        pv(ew[kt % 2], Vw, kt, g, kt == k0, kt == qt, 129, "win")
                    finish(2, g, s, False, "win")
                S.add("dve", I("tensor_tensor", out=nsa[:], in0=nsa[:], in1=AP(nz[s], 0, [[1024, 128], [128, 8], [1, 128]]), op=ALU.mult),
                      reads=[("nsa", h) for h in range(8)] + [("nz", s)], writes=[("nsa", h) for h in range(8)])
                for half in range(2):
                    for j in range(4):
                        hh = half * 4 + j
                        S.add("pe", I("transpose", out=psS[half][:, j * 128:(j + 1) * 128], in_=nsa[:, hh, :], identity=idf[:]),
                              reads=[("nsa", hh), "idf"], writes=[("psS", half)])
                    if half == 0:
                        S.add("act", I("copy", out=brs[s][:, 0:4, :], in_=AP(psS[0], 0, [[512, 128], [128, 4], [1, 128]])),
                              reads=[("psS", 0)], writes=[("brs", s, 0)])
                    else:
                        S.add("dve", I("tensor_copy", out=brs[s][:, 4:8, :], in_=AP(psS[1], 0, [[512, 128], [128, 4], [1, 128]])),
                              reads=[("psS", 1)], writes=[("brs", s, 1)])
                dma("sp", AP(BRT, qt * 393216, [[3072, 128], [128, 8], [1, 128]]), brs[s][:], reads=[("brs", s, 0), ("brs", s, 1)])
            S.emit()

    with ExitStack() as ph:
        sb = lambda n, s, d: ph.enter_context(nc.sbuf_tensor(f"{n}_{l}", s, d))
        I32, NPOOL, cache = SM["I32"], SM["NPOOL"], SM["cache"]
        idf = sb("idfS", [128, 128], F32)
        idb = sb("idbS", [128, 128], BF16)
        ptb = sb("ptb", [128, 128], I32)
        rio = sb("rio", [128, 2], F32)
        idx2 = sb("idx2", [128, 2, 128], I32)
        Wc = sb("WcS", [128, 2, 32, 128], BF16)
        pe_f = sb("pe_fS", [64, 128], F32)
        pe_b = sb("pe_bS", [64, 128], BF16)
        peT = sb("peTS", [128, 64], BF16)
        bias = sb("biasS", [128, 2], F32)
        XT = sb("XTS", [128, 4, 4112], BF16)
        pg = [sb(f"pg{i}", [128, 4, 128], F32) for i in range(2)]
        pgb = [sb(f"pgb{i}", [128, 4, 128], BF16) for i in range(2)]
        KcT = sb("KcTS", [128, 2, 1024], BF16)
        VcT = sb("VcTS", [128, 2, 1024], BF16)
        Vaug = sb("VaugS", [128, 8, 2, 386], BF16)
        E2b = sb("E2b", [128, 8192], BF16)
        bons = sb("bons", [128, 257], F32)
        tri = sb("triS", [128, 128], BF16)
        anti = sb("antiS", [128, 128], BF16)
        qs = sb("qs", [128, 8, 128], BF16)
        gt = sb("gtS", [128, 24], F32)
        nz = sb("nzS", [128, 1024], F32)
        nsa = sb("nsaS", [128, 8, 128], F32)
        ecs = [sb(f"ecs{i}", [128, 4, 128], BF16) for i in range(2)]
        esb = [sb(f"esS{i}", [128, 4, 128], BF16) for i in range(2)]
        em = [sb(f"emS{i}", [128, 4, 128], BF16) for i in range(2)]
        kg = [sb(f"kg{i}", [128, 4, 128], F32) for i in range(4)]
        kgb = [sb(f"kgb{i}", [128, 4, 128], BF16) for i in range(2)]
        kTt = [sb(f"kTt{i}", [128, 128], BF16) for i in range(2)]
        vat = [sb(f"vat{i}", [128, 129], BF16) for i in range(2)]
        rc = sb("rcS", [128, 4], F32)
        coef = sb("coefS", [128, 4], F32)
        imp = sb("impS", [128, 257], F32)
        imp2 = sb("imp2S", [128, 257], F32)
        m8a = sb("m8aS", [128, 8], F32)
        m8b = sb("m8bS", [128, 8], F32)
        sel = sb("selS", [128, 384], F32)
        selT = sb("selTS", [128, 3, 128], BF16)
        brs = sb("brsS", [128, 8, 128], BF16)
        psS = [ph.enter_context(nc.psum_tensor(f"psSs{i}_{l}", [128, 512], F32)) for i in range(2)]
        psX = ph.enter_context(nc.psum_tensor(f"psXs_{l}", [128, 512], F32))
        psO = [ph.enter_context(nc.psum_tensor(f"psOs{i}_{l}", [128, 512], F32)) for i in range(4)]
        psT = ph.enter_context(nc.psum_tensor(f"psTs_{l}", [128, 1024], BF16))
        dma("sp", idf[:], Cd["ident"].ap(), writes=["idf"])
        S.add("dve", I("tensor_copy", out=idb[:], in_=idf[:]), reads=["idf"], writes=["idb"])
        dma("sp", ptb[:], AP(SM["pt"], 0, [[0, 128], [1, 128]]), writes=["ptb"])
        dma("sp", rio[:], SM["riota"].ap(), writes=["rio"])
        for half in range(2):
            S.add("dve", I("tensor_scalar", out=idx2[:, half, :], in0=ptb[:], scalar1=256.0, scalar2=rio[:, 1:2], op0=ALU.mult, op1=ALU.add),
                  reads=["ptb", "rio"], writes=["idx"])
            S.add("dve", I("tensor_scalar", out=idx2[:, half, :], in0=idx2[:, half, :], scalar1=float(2 * l * NPOOL * 128 + half), scalar2=None, op0=ALU.add),
                  reads=["idx"], writes=["idx"])
        dma("pool", Wc[:], AP(Wd["cmp_w"], woff["cmp_w"], [[128, 128], [4096 * 128, 2], [128 * 128, 32], [1, 128]]), writes=["Wc"])
        dma("sp", pe_f[:], AP(Wd["cmp_pe"], woff["cmp_pe"], [[128, 64], [1, 128]]), writes=["pe_f"])
        S.add("dve", I("tensor_copy", out=pe_b[:], in_=pe_f[:]), reads=["pe_f"], writes=["pe_b"])
        S.add("pe", I("transpose", out=psT[:, 0:64], in_=pe_b[:], identity=idb[0:64, 0:64]), reads=["pe_b", "idb"], writes=["psT"])
        S.add("dve", I("tensor_copy", out=peT[:], in_=psT[:, 0:64]), reads=["psT"], writes=["peT"])
        for c in range(2):
            for ll in range(32):
                S.add("pe", I("matmul", psX[:, c:c + 1], lhsT=Wc[:, c, ll, :], rhs=peT[:, c * 32 + ll:c * 32 + ll + 1],
                              start=(ll == 0), stop=(ll == 31)), reads=["Wc", "peT"], writes=["psX"])
        S.add("dve", I("tensor_copy", out=bias[:], in_=psX[:, 0:2]), reads=["psX"], writes=["bias"])
        S.add("pool", I("memset", XT[:], 0.0), writes=["XT"])
        S.add("pool", I("memset", KcT[:], 0.0), writes=["KcT"])
        S.add("pool", I("memset", VcT[:], 0.0), writes=["VcT"])
        S.add("pool", I("memset", Vaug[:], 0.0), writes=["Vaug"])
        S.add("pool", I("memset", AP(Vaug, 128, [[6176, 128], [386, 16], [1, 1]]), 1.0), writes=["Vaug"])
        S.add("pool", I("memset", sel[:], 0.0), writes=["sel"])
        for i in range(2):
            S.add("pool", I("memset", vat[i][:, 128:129], 1.0), writes=[("vat1", i)])
        for g in range(2):
            dma("pool", AP(Vaug, g * 386 + 129, [[6176, 128], [772, 8], [1, 257]]), AP(SM["ovs"], 0, [[2056, 128], [257, 8], [1, 257]]), writes=["Vaug"])
        dma("pool", E2b[:], SM["e2big"].ap(), writes=["E2b"])
        dma("sp", bons[:], SM["bons"].ap(), writes=["bons"])
        dma("pool", tri[:], Cd["tri"].ap(), writes=["tri"])
        dma("pool", anti[:], Cd["anti"].ap(), writes=["anti"])
        dma("sp", qs[:], AP(QT, T, [[TT, 128], [128 * TT, 8], [1, 128]]), writes=["qs"])
        dma("sp", gt[:], Z.ap()[T:T + 128, C_G:C_G + 24], writes=["gt"])
        dma("sp", nz[:], Z.ap()[T:T + 128, C_NZ:C_NZ + 1024], writes=["nz"])
        S.add("act", I("activation", out=gt[:], in_=gt[:], func=AF.Sigmoid), reads=["gt"], writes=["gt"])
        S.add("act", I("activation", out=nz[:], in_=nz[:], func=AF.Silu), reads=["nz"], writes=["nz"])

        cache2 = AP(cache, 0, [[512, L * NPOOL * 128 * 2], [1, 512]])

        def gather(dst, half, page, wr):
            S.add("pool", I("indirect_dma_start", out=dst, out_offset=None, in_=cache2,
                            in_offset=bass.IndirectOffsetOnAxis(ap=idx2[:, half, page:page + 1], axis=0)),
                  reads=["idx"], writes=[wr], dma=True)

        nb = 0
        for qd in range(4):
            if qd > 0:
                S.add("dve", I("tensor_copy", out=XT[:, :, 0:16], in_=XT[:, :, 4096:4112]), reads=["XT"], writes=["XT"])
            for pp in range(32):
                page = qd * 32 + pp
                s = page % 2
                gather(AP(pg[s], 0, [[512, 128], [1, 512]]), 0, page, ("pg", s))
                S.add("act", I("copy", out=pgb[s][:], in_=pg[s][:]), reads=[("pg", s)], writes=[("pgb", s)])
                for slot in range(4):
                    S.add("pe", I("transpose", out=psT[:, slot * 128:(slot + 1) * 128], in_=pgb[s][:, slot, :], identity=idb[:]),
                          reads=[("pgb", s), "idb"], writes=["psT"])
                S.add("dve", I("tensor_copy", out=XT[:, :, 16 + pp * 128:16 + (pp + 1) * 128], in_=AP(psT, 0, [[1024, 128], [128, 4], [1, 128]])),
                      reads=["psT"], writes=["XT"])
            i0 = 1 if qd == 0 else 0
            for c in range(2):
                for g in range(2):
                    b = nb % 2
                    nb += 1
                    for ll in range(32):
                        S.add("pe", I("matmul", psS[b][:, 0:256], lhsT=Wc[:, c, ll, :], rhs=AP(XT, (c * 2 + g) * 4112 + ll, [[4 * 4112, 128], [16, 256]]),
                                      start=(ll == 0), stop=(ll == 31)), reads=["Wc", "XT"], writes=[("psS", b)])
                    dst = KcT if c == 0 else VcT
                    S.add("act", I("activation", out=dst[:, g, 256 * qd - 1 + i0:256 * qd + 255], in_=psS[b][:, i0:256], func=AF.Identity,
                                   bias=bias[:, c:c + 1]), reads=[("psS", b), "bias"], writes=["KcT" if c == 0 else "VcT"])
        for g in range(2):
            for it in range(8):
                S.add("pe", I("transpose", out=psT[:, 512:640], in_=VcT[:, g, it * 128:(it + 1) * 128], identity=idb[:]),
                      reads=["VcT", "idb"], writes=["psT"])
                S.add("dve", I("tensor_copy", out=Vaug[:, it, g, 0:128], in_=psT[:, 512:640]), reads=["psT"], writes=["Vaug"])

        def scores(lhsT, g, b, nrow=128, stop=True):
            S.add("pe", I("matmul", psS[b][0:nrow, :], lhsT=lhsT, rhs=AP(qs, 4 * g * 128, [[1024, 128], [1, 512]]), start=True, stop=stop),
                  reads=["qs", "KcT", ("kTt", 0), ("kTt", 1)], writes=[("psS", b)])

        selTb = sb("selTbS", [128, 3, 4, 128], BF16)
        antiB = sb("antiBS", [128, 4, 128], BF16)
        S.add("act", I("activation", out=antiB[:], in_=AP(anti, 0, [[128, 128], [0, 4], [1, 128]]), func=AF.Identity, scale=-30000.0),
              reads=["anti"], writes=["antiB"])

        def pv(src, rhs, first, last, width, nrow=128):
            for h in range(4):
                S.add("pe", I("matmul", psO[h][:, 0:width], lhsT=src[0:nrow, h, :], rhs=rhs, start=first, stop=last),
                      reads=[src.name, "Vaug", ("vat", 0), ("vat", 1), ("vat1", 0), ("vat1", 1)], writes=[("psO", h)])

        def rc_coef(gcol, g):
            for h in range(4):
                S.add("dve", I("tensor_scalar", out=rc[:, h:h + 1], in0=psO[h][:, 128:129], scalar1=1e-30, scalar2=None, op0=ALU.max),
                      reads=[("psO", h)], writes=["rc"])
            S.add("dve", I("reciprocal", out=rc[:], in_=rc[:]), reads=["rc"], writes=["rc"])
            S.add("dve", I("tensor_tensor", out=coef[:], in0=rc[:], in1=AP(gt, 12 * g + gcol, [[24, 128], [3, 4]]), op=ALU.mult),
                  reads=["rc", "gt"], writes=["coef"])

        def finish(gcol, g):
            rc_coef(gcol, g)
            for h in range(4):
                hh = 4 * g + h
                S.add("dve", I("scalar_tensor_tensor", out=nsa[:, hh, :], in0=psO[h][:, 0:128], scalar=coef[:, h:h + 1], in1=nsa[:, hh, :],
                               op0=ALU.mult, op1=ALU.add), reads=[("psO", h), "coef", ("nsa", hh)], writes=[("nsa", hh)])

        def load_new_tile(b, kslot, vslot):
            dma("sp", kTt[b][:], AP(KTs, kslot * 128 * TT + T, [[TT, 128], [1, 128]]), writes=[("kTt", b)])
            dma("sp", vat[b][:, 0:128], AP(VV, T * 512 + vslot * 128, [[512, 128], [1, 128]]), writes=[("vat", b)])

        for g in range(2):
            for it in range(8):
                b = it % 2
                ni = 128 if it < 7 else 127
                scores(KcT[:, g, it * 128:it * 128 + ni], g, b, nrow=ni)
                S.add("act", I("activation", out=ecs[b][0:ni], in_=psS[b][0:ni, :], func=AF.Exp), reads=[("psS", b)], writes=[ecs[b].name])
                pv(ecs[b], Vaug[0:ni, it, g, :], it == 0, it == 7, 386, nrow=ni)
            rc_coef(0, g)
            S.add("dve", I("tensor_scalar", out=imp[:], in0=psO[0][:, 129:386], scalar1=rc[:, 0:1], scalar2=None, op0=ALU.mult),
                  reads=[("psO", 0), "rc"], writes=["imp"])
            for h in range(1, 4):
                S.add("dve", I("scalar_tensor_tensor", out=imp[:], in0=psO[h][:, 129:386], scalar=rc[:, h:h + 1], in1=imp[:], op0=ALU.mult, op1=ALU.add),
                      reads=[("psO", h), "rc", "imp"], writes=["imp"])
            S.add("dve", I("tensor_tensor", out=imp[:], in0=imp[:], in1=bons[:], op=ALU.add), reads=["imp", "bons"], writes=["imp"])
            for h in range(4):
                hh = 4 * g + h
                S.add("dve", I("tensor_scalar", out=nsa[:, hh, :], in0=psO[h][:, 0:128], scalar1=coef[:, h:h + 1], scalar2=None, op0=ALU.mult),
                      reads=[("psO", h), "coef"], writes=[("nsa", hh)])
            S.add("dve", I("max", out=m8a[:], in_=imp[:]), reads=["imp"], writes=["m8a"])
            S.add("dve", I("match_replace", out=imp2[:], in_to_replace=m8a[:], in_values=imp[:], imm_value=NEG), reads=["imp", "m8a"], writes=["imp2"])
            S.add("dve", I("max", out=m8b[:], in_=imp2[:]), reads=["imp2"], writes=["m8b"])
            S.add("dve", I("tensor_scalar", out=sel[:, 0:257], in0=imp[:], scalar1=m8b[:, 7:8], scalar2=None, op0=ALU.is_ge),
                  reads=["imp", "m8b"], writes=["sel"])
            for jt in range(3):
                S.add("pe", I("transpose", out=psX[:, 128 + jt * 128:256 + jt * 128], in_=sel[:, jt * 128:(jt + 1) * 128], identity=idf[:]),
                      reads=["sel", "idf"], writes=["psX"])
            for jt in range(3):
                S.add("act", I("activation", out=selTb[:, jt, :, :], in_=AP(psX, 128 + jt * 128, [[512, 128], [0, 4], [1, 128]]), func=AF.Identity,
                               scale=30000.0, bias=-30000.0), reads=["psX"], writes=["selT"])
            def s_gather(kt):
                if kt < 128:
                    gather(AP(kg[kt % 4], 0, [[512, 128], [1, 512]]), 1, kt, ("kg", kt % 4))

            def s_front(kt):
                b = kt % 2
                if kt < 128:
                    S.add("act", I("copy", out=kgb[b][:], in_=kg[kt % 4][:]), reads=[("kg", kt % 4)], writes=[("kgb", b)])
                    S.add("pe", I("transpose", out=psT[:, 0:128], in_=kgb[b][:, g, :], identity=idb[:]), reads=[("kgb", b), "idb"], writes=["psT"])
                    S.add("dve", I("tensor_copy", out=kTt[b][:], in_=psT[:, 0:128]), reads=["psT"], writes=[("kTt", b)])
                    S.add("dve", I("tensor_copy", out=vat[b][:, 0:128], in_=kgb[b][:, 2 + g, :]), reads=[("kgb", b)], writes=[("vat", b)])
                else:
                    load_new_tile(b, 4 + g, g)
                scores(kTt[b][:], g, b, stop=False)
                S.add("pe", I("matmul", psS[b][:], lhsT=E2b[:, (kt % 64) * 128:(kt % 64 + 1) * 128], rhs=AP(selTb, (kt // 64) * 512, [[1536, 128], [1, 512]]),
                              start=False, stop=(kt != 128)), reads=["E2b", "selT"], writes=[("psS", b)])
                if kt == 128:
                    S.add("pe", I("matmul", psS[b][:], lhsT=idb[:], rhs=AP(antiB, 0, [[512, 128], [1, 512]]), start=False, stop=True),
                          reads=["idb", "antiB"], writes=[("psS", b)])
                S.add("act", I("activation", out=esb[b][:], in_=psS[b][:], func=AF.Exp), reads=[("psS", b)], writes=[esb[b].name])

            def s_mid(kt):
                pass

            for kk in range(3):
                s_gather(kk)
            s_front(0)
            for kt in range(129):
                s_gather(kt + 3)
                s_mid(kt)
                if kt + 1 <= 128:
                    s_front(kt + 1)
                pv(esb[kt % 2], vat[kt % 2][:], kt == 0, kt == 128, 129)
            finish(1, g)
            for w in range(5):
                b = w % 2
                if w < 4:
                    dma("sp", AP(kg[b], 0, [[512, 128], [1, 512]]), SM["cwin"].ap()[l, w * 128:(w + 1) * 128, :], writes=[("kg", b)])
                    S.add("act", I("copy", out=kgb[b][:], in_=kg[b][:]), reads=[("kg", b)], writes=[("kgb", b)])
                    S.add("pe", I("transpose", out=psT[:, 0:128], in_=kgb[b][:, g, :], identity=idb[:]), reads=[("kgb", b), "idb"], writes=["psT"])
                    S.add("dve", I("tensor_copy", out=kTt[b][:], in_=psT[:, 0:128]), reads=["psT"], writes=[("kTt", b)])
                    S.add("dve", I("tensor_copy", out=vat[b][:, 0:128], in_=kgb[b][:, 2 + g, :]), reads=[("kgb", b)], writes=[("vat", b)])
                else:
                    load_new_tile(b, 6 + g, 2 + g)
                scores(kTt[b][:], g, b)
                S.add("act", I("activation", out=esb[b][:], in_=psS[b][:], func=AF.Exp), reads=[("psS", b)], writes=[esb[b].name])
                mk = anti if w == 0 else (tri if w == 4 else None)
                if mk is not None:
                    S.add("pool", I("tensor_tensor", out=esb[b][:], in0=esb[b][:], in1=AP(mk, 0, [[128, 128], [0, 4], [1, 128]]), op=ALU.mult),
                          reads=[esb[b].name, "tri", "anti"], writes=[esb[b].name])
                pv(esb[b], vat[b][:], w == 0, w == 4, 129)
            finish(2, g)
        S.add("dve", I("tensor_tensor", out=nsa[:], in0=nsa[:], in1=AP(nz, 0, [[1024, 128], [128, 8], [1, 128]]), op=ALU.mult),
              reads=[("nsa", h) for h in range(8)] + ["nz"], writes=[("nsa", h) for h in range(8)])
        for half in range(2):
            for j in range(4):
                hh = half * 4 + j
                S.add("pe", I("transpose", out=psS[half][:, j * 128:(j + 1) * 128], in_=nsa[:, hh, :], identity=idf[:]),
                      reads=[("nsa", hh), "idf"], writes=[("psS", half)])
            S.add("act" if half == 0 else "dve", I("copy" if half == 0 else "tensor_copy", out=brs[:, half * 4:(half + 1) * 4, :],
                                                    in_=AP(psS[half], 0, [[512, 128], [128, 4], [1, 128]])),
                  reads=[("psS", half)], writes=[("brs", half)])
        dma("sp", AP(BRT, NT * 393216, [[3072, 128], [128, 8], [1, 128]]), brs[:], reads=[("brs", 0), ("brs", 1)])
        S.emit()

    with ExitStack() as ph:
        sb = lambda n, s, d: ph.enter_context(nc.sbuf_tensor(f"{n}_{l}", s, d))
        pu = [sb(f"pu{i}", [128, 1024], F32) for i in range(2)]
        pub = [sb(f"pub{i}", [128, 1024], BF16) for i in range(2)]
        pz = [sb(f"pz{i}", [128, 1024], F32) for i in range(2)]
        PA = sb("PA", [128, 3, 4, 128], BF16)
        pw = sb("pw", [128, 4, 2, 256], BF16)
        psc = sb("psc", [128, 1024], F32)
        pT = sb("pT", [128, 8, 128], BF16)
        po = sb("po", [128, 1024], F32)
        pob = sb("pob", [128, 1024], BF16)
        brs = [sb(f"brs6{i}", [128, 8, 128], BF16) for i in range(2)]
        idf = sb("idf6", [128, 128], F32)
        idb = sb("idb6", [128, 128], BF16)
        psP = [ph.enter_context(nc.psum_tensor(f"psP{i}_{l}", [128, 512], F32)) for i in range(2)]
        psM = [ph.enter_context(nc.psum_tensor(f"psM{i}_{l}", [128, 512], F32)) for i in range(2)]
        psT = ph.enter_context(nc.psum_tensor(f"psT6_{l}", [128, 1024], BF16))
        dma("sp", idf[:], Cd["ident"].ap(), writes=["idf"])
        S.add("dve", I("tensor_copy", out=idb[:], in_=idf[:]), reads=["idf"], writes=["idb"])
        dma("pool", PA[:], Cd["poolA"].ap(), writes=["PA"])
        dma("pool", pw[:], AP(Wd["pool_w"], woff["pool_w"], [[256, 128], [256 * 256, 4], [128 * 256, 2], [1, 256]]), writes=["pw"])
        dma("sp", psc[:], bc_row(Wd["pool_scale"], woff["pool_scale"], 1024), writes=["psc"])

        PAs = sb("PAs", [128, 4, 128], BF16)
        dma("pool", PAs[:], SM["poolAs"].ap(), writes=["PAs"])

        def p6_load(t):
            s = t % 2
            if t < NT:
                dma("sp", pu[s][:], Z.ap()[t * 128:(t + 1) * 128, C_PU:C_PU + 1024], writes=[("pu", s)])
            else:
                S.add("pool", I("memset", pu[s][:], 0.0), writes=[("pu", s)])
                dma("sp", pu[s][0:15, :], SM["spool"].ap()[l], writes=[("pu", s)])
                dma("sp", pu[s][15:19, :], Z.ap()[T:T + 4, C_PU:C_PU + 1024], writes=[("pu", s)])
            dma("sp", pz[s][:], Z.ap()[t * 128:(t + 1) * 128, C_PZ:C_PZ + 1024], writes=[("pz", s)])

        p6_load(0)
        for t in range(NTT):
            s = t % 2
            if t + 1 < NTT:
                p6_load(t + 1)
            S.add("act", I("copy", out=pub[s][:], in_=pu[s][:]), reads=[("pu", s)], writes=[("pub", s)])
            S.add("act", I("activation", out=pz[s][:], in_=pz[s][:], func=AF.Silu), reads=[("pz", s)], writes=[("pz", s)])
            kind = 1 if t == 0 else 0
            for cc in range(8):
                gi = cc // 2
                outp = psP[cc // 4][:, (cc % 4) * 128:(cc % 4 + 1) * 128]
                S.add("pe", I("matmul",
                    outp, lhsT=pub[s][:, cc * 128:(cc + 1) * 128], rhs=(PA[:, kind, gi, :] if t < NT else PAs[:, gi, :]),
                    start=True, stop=(t == 0 or t == NT)),
                    reads=[("pub", s), "PA", "PAs"], writes=[("psP", cc // 4)])
                if 0 < t < NT:
                    S.add("pe", I("matmul",
                        outp, lhsT=pub[1 - s][:, cc * 128:(cc + 1) * 128], rhs=PA[:, 2, gi, :], start=False, stop=True),
                        reads=[("pub", 1 - s), "PA"], writes=[("psP", cc // 4)])
            S.add("act", I("copy", out=pT[:, 0:4, :], in_=AP(psP[0], 0, [[512, 128], [128, 4], [1, 128]])), reads=[("psP", 0)], writes=["pT0"])
            S.add("dve", I("tensor_copy", out=pT[:, 4:8, :], in_=AP(psP[1], 0, [[512, 128], [128, 4], [1, 128]])), reads=[("psP", 1)], writes=["pT1"])
            for gi in range(4):
                for kk in range(2):
                    S.add("pe", I("matmul", psM[gi // 2][:, (gi % 2) * 256:(gi % 2 + 1) * 256], lhsT=pT[:, 2 * gi + kk, :],
                                                                 rhs=pw[:, gi, kk, :], start=(kk == 0), stop=(kk == 1)),
                          reads=["pT0", "pT1", "pw"], writes=[("psM", gi // 2)])
            for hf in range(2):
                S.add("dve", I("tensor_tensor", out=po[:, hf * 512:(hf + 1) * 512], in0=psM[hf][:], in1=psc[:, hf * 512:(hf + 1) * 512], op=ALU.mult),
                      reads=[("psM", hf), "psc"], writes=[("po", hf)])
            S.add("pool", I("tensor_tensor", out=pob[:], in0=po[:], in1=pz[s][:], op=ALU.mult),
                  reads=[("po", 0), ("po", 1), ("pz", s)], writes=["pob"])
            for j in range(8):
                S.add("pe", I("transpose", out=psT[:, j * 128:(j + 1) * 128], in_=pob[:, j * 128:(j + 1) * 128], identity=idb[:]),
                      reads=["pob", "idb"], writes=["psT"])
            S.add("act", I("copy", out=brs[s][:], in_=AP(psT, 0, [[1024, 128], [128, 8], [1, 128]])), reads=["psT"], writes=[("brs", s)])
            dma("sp", AP(BRT, t * 393216 + 1024, [[3072, 128], [128, 8], [1, 128]]), brs[s][:], reads=[("brs", s)])
            if t == NT - 1:
                dma("sp", pool_p.ap()[l], pu[s][113:128, :], reads=[("pu", s)])
            if t == NT:
                dma("sp", SM["pool_s"].ap()[l], pu[s][4:19, :], reads=[("pu", s)])
        S.emit()

    build_ssd(nc, S, env)

    with ExitStack() as ph:
        sb = lambda n, s, d: ph.enter_context(nc.sbuf_tensor(f"{n}_{l}", s, d))
        wbr = [sb(f"wbr{i}", [128, 3, 8, 512], BF16) for i in range(2)]
        brt = [sb(f"brt{i}", [128, 24, 128], BF16) for i in range(2)]
        gz = [sb(f"gz{i}", [128, 3, 512], F32) for i in range(2)]
        mg = sb("mg", [128, 512], F32)
        tm = sb("tm", [128, 512], F32)
        mgb = sb("mgb", [128, 512], BF16)
        mts = [sb(f"mts{i}", [128, 4, 128], BF16) for i in range(2)]
        idf = sb("idf8", [128, 128], F32)
        idb = sb("idb8", [128, 128], BF16)
        psP = [ph.enter_context(nc.psum_tensor(f"psQ{i}_{l}", [128, 512], F32)) for i in range(6)]
        psT = ph.enter_context(nc.psum_tensor(f"psT8_{l}", [128, 1024], BF16))
        dma("sp", idf[:], Cd["ident"].ap(), writes=["idf"])
        S.add("dve", I("tensor_copy", out=idb[:], in_=idf[:]), reads=["idf"], writes=["idb"])
        n = 0
        def wbr_load(cc):
            dma("pool", wbr[cc % 2][:], AP(Wd["w_branch"], woff["w_branch"] + cc * 512, [[2048, 128], [1024 * 2048, 3], [128 * 2048, 8], [1, 512]]),
                writes=[("wbr", cc % 2)])

        wbr_load(0)
        for cc in range(4):
            ws = cc % 2
            if cc + 1 < 4:
                wbr_load(cc + 1)
            def m_front(t, s):
                dma("sp", brt[s][:], AP(BRT, t * 393216, [[3072, 128], [128, 24], [1, 128]]), writes=[("brt", s)])
                dma("sp", gz[s][:], AP(Z, t * 128 * NIN + C_MG + cc * 512, [[NIN, 128], [2048, 3], [1, 512]]), writes=[("gz", s)])
                S.add("act", I("activation", out=gz[s][:], in_=gz[s][:], func=AF.Sigmoid), reads=[("gz", s)], writes=[("gz", s)])
                for k in range(3):
                    pb = psP[3 * s + k]
                    for kt in range(8):
                        S.add("pe", I("matmul", pb[:], lhsT=brt[s][:, 8 * k + kt, :], rhs=wbr[ws][:, k, kt, :],
                                                                                   start=(kt == 0), stop=(kt == 7)),
                              reads=[("brt", s), ("wbr", ws)], writes=[("psQ", 3 * s + k)])

            m_front(0, n % 2)
            for t in range(NTT):
                s = n % 2
                n += 1
                if t + 1 < NTT:
                    m_front(t + 1, n % 2)
                S.add("dve", I("tensor_tensor", out=mg[:], in0=psP[3 * s][:], in1=gz[s][:, 0, :], op=ALU.mult),
                      reads=[("psQ", 3 * s), ("gz", s)], writes=["mg"])
                S.add("dve", I("tensor_tensor", out=tm[:], in0=psP[3 * s + 1][:], in1=gz[s][:, 1, :], op=ALU.mult),
                      reads=[("psQ", 3 * s + 1), ("gz", s)], writes=["tm"])
                S.add("pool", I("tensor_tensor", out=mg[:], in0=mg[:], in1=tm[:], op=ALU.add), reads=["mg", "tm"], writes=["mg"])
                S.add("dve", I("tensor_tensor", out=tm[:], in0=psP[3 * s + 2][:], in1=gz[s][:, 2, :], op=ALU.mult),
                      reads=[("psQ", 3 * s + 2), ("gz", s)], writes=["tm"])
                S.add("pool", I("tensor_tensor", out=mgb[:], in0=mg[:], in1=tm[:], op=ALU.add), reads=["mg", "tm"], writes=["mgb"])
                for j in range(4):
                    S.add("pe", I("transpose", out=psT[:, j * 128:(j + 1) * 128], in_=mgb[:, j * 128:(j + 1) * 128], identity=idb[:]),
                          reads=["mgb", "idb"], writes=["psT"])
                S.add("act", I("copy", out=mts[s][:], in_=AP(psT, 0, [[1024, 128], [128, 4], [1, 128]])), reads=["psT"], writes=[("mts", s)])
                dma("sp", AP(MT, cc * 4 * 128 * TT + t * 128, [[TT, 128], [128 * TT, 4], [1, 128]]), mts[s][:], reads=[("mts", s)])
        S.emit()

    with ExitStack() as ph:
        sb = lambda n, s, d: ph.enter_context(nc.sbuf_tensor(f"{n}_{l}", s, d))
        mT = sb("mT", [128, 16, TT], BF16)
        wo = [sb(f"wo{i}", [128, 16, 512], BF16) for i in range(2)]
        xr = [sb(f"xr{i}", [128, 512], F32) for i in range(4)]
        ps = [ph.enter_context(nc.psum_tensor(f"ps9{i}_{l}", [128, 512], F32)) for i in range(4)]
        dma("sp", mT[:], AP(MT, 0, [[TT, 128], [128 * TT, 16], [1, TT]]), writes=["mT"])
        n = 0
        for cc in range(4):
            ws = cc % 2
            dma("pool", wo[ws][:], AP(Wd["w_out"], woff["w_out"] + cc * 512, [[2048, 128], [128 * 2048, 16], [1, 512]]), writes=[("wo", ws)])
            for t in range(NTT):
                b = n % 4
                n += 1
                dma("sp", xr[b][:], x_src.ap()[t * 128:(t + 1) * 128, cc * 512:(cc + 1) * 512], writes=[("xr", b)])
                for kt in range(16):
                    S.add("pe", I("matmul", ps[b][:], lhsT=mT[:, kt, t * 128:(t + 1) * 128], rhs=wo[ws][:, kt, :],
                                                                           start=(kt == 0), stop=(kt == 15)),
                          reads=["mT", ("wo", ws)], writes=[("ps", b)])
                S.add("dve", I("tensor_tensor", out=xr[b][:], in0=ps[b][:], in1=xr[b][:], op=ALU.add),
                      reads=[("ps", b), ("xr", b)], writes=[("xr", b)])
                if t < NT or l < L - 1:
                    dma("sp", x_dst.ap()[t * 128:(t + 1) * 128, cc * 512:(cc + 1) * 512], xr[b][:], reads=[("xr", b)])
                else:
                    dma("sp", SM["y_s"].ap()[:, cc * 512:(cc + 1) * 512], xr[b][0:4, :], reads=[("xr", b)])
        S.emit()


def build_ssd(nc, S, env):
    l, T, NT = env["l"], env["T"], env["NT"]
    TT, NTT, SM = env["TT"], env["NTT"], env["SM"]
    Wd, woff, Cd = env["Wd"], env["woff"], env["Cd"]
    Z, Z2, BRT = env["Z"], env["Z2"], env["BRT"]
    conv_p, ssm_p = env["conv_p"], env["ssm_p"]
    dma, bc_row = env["dma"], env["bc_row"]
    with ExitStack() as ssd:
        sbo = lambda n, s, d: ssd.enter_context(nc.sbuf_tensor(f"{n}_{l}", s, d))
        xcT = sbo("xcT", [128, 12, TT], BF16)
        idf = sbo("idf7", [128, 128], F32)
        idb = sbo("idb7", [128, 128], BF16)
        with ExitStack() as ph:
            sb = lambda n, s, d: ph.enter_context(nc.sbuf_tensor(f"{n}_{l}", s, d))
            xin = [sb(f"xin{i}", [128, T + 4], F32) for i in range(2)]
            acc = sb("acc", [128, T], F32)
            cw_in = sb("cw_in", [60, 128], F32)
            cwb = sb("cwb", [128, 60], F32)
            pcw = ph.enter_context(nc.psum_tensor(f"pcw_{l}", [128, 512], F32))
            dma("sp", idf[:], Cd["ident"].ap(), writes=["idf"])
            S.add("dve", I("tensor_copy", out=idb[:], in_=idf[:]), reads=["idf"], writes=["idb"])
            dma("sp", cw_in[0:48, :], AP(Wd["conv_w"], woff["conv_w"], [[128, 48], [1, 128]]), writes=["cw_in"])
            dma("sp", cw_in[48:60, :], AP(Wd["conv_b"], woff["conv_b"], [[128, 12], [1, 128]]), writes=["cw_in"])
            S.add("pe", I("transpose", out=pcw[:, 0:60], in_=cw_in[:], identity=idf[0:60, 0:60]), reads=["cw_in", "idf"], writes=["pcw"])
            S.add("dve", I("tensor_copy", out=cwb[:], in_=pcw[:, 0:60]), reads=["pcw"], writes=["cwb"])
            for i in range(2):
                S.add("pool", I("memset", xin[i][:, 0:3], 0.0), writes=[("xin", i)])
            xsi = [sb(f"xsi{i}", [128, 8], F32) for i in range(2)]
            accs = sb("accs", [128, 4], F32)
            S.add("pool", I("memset", xcT[:, :, T:TT], 0.0), writes=[("xcT", k) for k in range(12)])
            for kc in range(12):
                s = kc % 2
                dma("sp", xsi[s][:, 0:3], AP(SM["sconv"], l * 3 * 1536 + kc * 128, [[1, 128], [1536, 3]]), writes=[("xsi", s)],
                    allow_slow_non_contiguous=True)
                dma("sp", xsi[s][:, 3:7], Z2.ap()[kc * 128:(kc + 1) * 128, T:T + 4], writes=[("xsi", s)])
                S.add("dve", I("tensor_scalar", out=accs[:], in0=xsi[s][:, 0:4], scalar1=cwb[:, kc:kc + 1],
                               scalar2=cwb[:, 48 + kc:49 + kc], op0=ALU.mult, op1=ALU.add), reads=[("xsi", s), "cwb"], writes=["accs"])
                for j in range(1, 4):
                    S.add("dve", I("scalar_tensor_tensor", out=accs[:], in0=xsi[s][:, j:j + 4], scalar=cwb[:, j * 12 + kc:j * 12 + kc + 1],
                                   in1=accs[:], op0=ALU.mult, op1=ALU.add), reads=[("xsi", s), "cwb", "accs"], writes=["accs"])
                S.add("act", I("activation", out=xcT[:, kc, T:T + 4], in_=accs[:], func=AF.Silu), reads=["accs"], writes=[("xcT", kc)])
                dma("sp", AP(SM["conv_s"], l * 3 * 1536 + kc * 128, [[1, 128], [1536, 3]]), xsi[s][:, 4:7], reads=[("xsi", s)],
                    allow_slow_non_contiguous=True)
                dma("sp", xin[s][:, 3:3 + T], Z2.ap()[kc * 128:(kc + 1) * 128, 0:T], writes=[("xin", s)])
                S.add("dve", I("tensor_scalar", out=acc[:], in0=xin[s][:, 0:T], scalar1=cwb[:, kc:kc + 1],
                                                                    scalar2=cwb[:, 48 + kc:49 + kc], op0=ALU.mult, op1=ALU.add),
                      reads=[("xin", s), "cwb"], writes=["acc"])
                for j in range(1, 4):
                    S.add("dve", I("scalar_tensor_tensor", out=acc[:], in0=xin[s][:, j:j + T], scalar=cwb[:, j * 12 + kc:j * 12 + kc + 1],
                                                                                in1=acc[:], op0=ALU.mult, op1=ALU.add),
                          reads=[("xin", s), "cwb", "acc"], writes=["acc"])
                S.add("act", I("activation", out=xcT[:, kc, 0:T], in_=acc[:], func=AF.Silu), reads=["acc"], writes=[("xcT", kc)])
                dma("sp", AP(conv_p, l * 3 * 1536 + kc * 128, [[1, 128], [1536, 3]]), xin[s][:, T:T + 3], reads=[("xin", s)],
                    allow_slow_non_contiguous=True)
            S.emit()
        with ExitStack() as ph:
            sb = lambda n, s, d: ph.enter_context(nc.sbuf_tensor(f"{n}_{l}", s, d))
            xtok = sb("xtok", [128, 1024], BF16)
            Btok = sb("Btok", [128, 2, 128], BF16)
            dtr = [sb(f"dtr{i}", [128, 16], F32) for i in range(2)]
            mz = [sb(f"mz{i}", [128, 1024], F32) for i in range(2)]
            dtb = sb("dtb", [128, 16], F32)
            abc = sb("abc", [128, 16], F32)
            dsk = sb("dsk", [128, 16], F32)
            mnw = sb("mnw", [128, 1024], F32)
            TRIf = sb("TRIf", [128, 128], F32)
            Uf = sb("Uf", [128, 128], F32)
            ONESf = sb("ONESf", [128, 128], F32)
            TRIb = sb("TRIb", [128, 128], BF16)
            epsb = sb("eps7", [128, 1], F32)
            dt = sb("dt", [128, 16], F32)
            la = sb("la", [128, 16], F32)
            acs = sb("acs", [128, 16], F32)
            eacs = sb("eacs", [128, 16], F32)
            dte = sb("dte", [128, 16], F32)
            dch = sb("dch", [128, 16], F32)
            Y = sb("Yk", [128, 16, 128], F32)
            expD = sb("expD", [128, 16, 128], BF16)
            GM = sb("GM", [128, 2, 128], BF16)
            MTt = sb("MTt", [128, 16, 128], BF16)
            xdt = sb("xdt", [128, 1024], BF16)
            xdtd = sb("xdtd", [128, 1024], BF16)
            Sf = sb("Sf", [128, 16, 64], F32)
            Sb = sb("Sb", [128, 16, 64], BF16)
            ytmp = sb("ytmp", [128, 1024], F32)
            yy = sb("yy", [128, 1024], F32)
            junk = sb("junk7", [128, 512], BF16)
            ssg = sb("ssg", [128, 2], F32)
            rsg = sb("rsg", [128, 2], F32)
            mo = sb("mo", [128, 1024], BF16)
            brs = [sb(f"brs7{i}", [128, 8, 128], BF16) for i in range(2)]
            fin = sb("fin", [128, 8, 128], F32)
            psXT = ph.enter_context(nc.psum_tensor(f"psXT_{l}", [128, 1024], BF16))
            psD = [ph.enter_context(nc.psum_tensor(f"psD{i}_{l}", [128, 512], F32)) for i in range(2)]
            psA = ph.enter_context(nc.psum_tensor(f"psA_{l}", [128, 512], F32))
            psY = [ph.enter_context(nc.psum_tensor(f"psY{i}_{l}", [128, 512], F32)) for i in range(2)]
            psF = [ph.enter_context(nc.psum_tensor(f"psF{i}_{l}", [128, 512], F32)) for i in range(2)]
            dma("sp", dtb[:], bc_row(Wd["dt_bias"], woff["dt_bias"], 16), writes=["dtb"])
            dma("sp", abc[:], bc_row(Wd["a_log"], woff["a_log"], 16), writes=["abc"])
            dma("sp", dsk[:], bc_row(Wd["d_skip"], woff["d_skip"], 16), writes=["dsk"])
            dma("sp", mnw[:], bc_row(Wd["mnorm_w"], woff["mnorm_w"], 1024), writes=["mnw"])
            dma("sp", TRIf[:], Cd["tri"].ap(), writes=["TRIf"])
            dma("sp", Uf[:], Cd["ustrict"].ap(), writes=["Uf"])
            dma("sp", ONESf[:], Cd["ones"].ap(), writes=["ONESf"])
            S.add("dve", I("tensor_copy", out=TRIb[:], in_=TRIf[:]), reads=["TRIf"], writes=["TRIb"])
            S.add("dve", I("memset", epsb[:], EPS), writes=["eps"])
            S.add("act", I("activation", out=abc[:], in_=abc[:], func=AF.Exp), reads=["abc"], writes=["abc"])
            S.add("act", I("mul", out=abc[:], in_=abc[:], mul=-1.0), reads=["abc"], writes=["abc"])
            S.add("pool", I("memset", Sf[:], 0.0), writes=["Sf"])
            S.add("pool", I("memset", Sb[:], 0.0), writes=["Sb"])

            def p7_load(c):
                s = c % 2
                dma("sp", dtr[s][:], Z.ap()[c * 128:(c + 1) * 128, C_DT:C_DT + 16], writes=[("dtr", s)])
                dma("sp", mz[s][:], Z.ap()[c * 128:(c + 1) * 128, C_MZ:C_MZ + 1024], writes=[("mz", s)])

            p7_load(0)
            XC = [("xcT", k) for k in range(12)]
            rowm = sb("rowm", [128, 1], F32)
            dma("sp", rowm[:], SM["rowm"].ap(), writes=["rowm"])

            def dump_state(dst, off):
                for j in range(8):
                    S.add("pe", I("transpose", out=psY[j // 4][:, (j % 4) * 128:(j % 4 + 1) * 128], in_=AP(Sf, j * 128, [[1024, 128], [1, 128]]),
                                  identity=idf[:]), reads=["Sf", "idf"], writes=[("psY", j // 4)])
                for hf in range(2):
                    S.add("dve", I("tensor_copy", out=fin[:, hf * 4:(hf + 1) * 4, :], in_=AP(psY[hf], 0, [[512, 128], [128, 4], [1, 128]])),
                          reads=[("psY", hf)], writes=[("fin", hf)])
                dma("sp", AP(dst, off, [[128, 128], [128 * 128, 8], [1, 128]]), fin[:], reads=[("fin", 0), ("fin", 1)])

            for c in range(NTT):
                s = c % 2
                if c + 1 < NTT:
                    p7_load(c + 1)
                if c == NT:
                    dump_state(ssm_p, l * 1024 * 128)
                    dma("sp", fin[:], AP(SM["sssm"], l * 1024 * 128, [[128, 128], [128 * 128, 8], [1, 128]]), writes=[("fin", 0), ("fin", 1)])
                    for j in range(8):
                        S.add("pe", I("transpose", out=psY[j // 4][:, (j % 4) * 128:(j % 4 + 1) * 128], in_=fin[:, j, :], identity=idf[:]),
                              reads=[("fin", 0), ("fin", 1), "idf"], writes=[("psY", j // 4)])
                    for hf in range(2):
                        S.add("dve", I("tensor_copy", out=AP(Sf, hf * 512, [[1024, 128], [1, 512]]), in_=psY[hf][:]),
                              reads=[("psY", hf)], writes=["Sf"])
                    S.add("act", I("copy", out=Sb[:], in_=Sf[:]), reads=["Sf"], writes=["Sb"])
                tok = slice(c * 128, (c + 1) * 128)
                for kc in range(8):
                    S.add("pe", I("transpose", out=psXT[:, kc * 128:(kc + 1) * 128], in_=xcT[:, kc, tok], identity=idb[:]),
                          reads=XC + ["idb"], writes=["psXT"])
                S.add("act", I("copy", out=xtok[:], in_=psXT[:]), reads=["psXT"], writes=["xtok"])
                for g in range(2):
                    S.add("pe", I("transpose", out=psXT[:, g * 128:(g + 1) * 128], in_=xcT[:, 8 + g, tok], identity=idb[:]),
                          reads=XC + ["idb"], writes=["psXT"])
                S.add("dve", I("tensor_copy", out=Btok[:], in_=AP(psXT, 0, [[1024, 128], [128, 2], [1, 128]])), reads=["psXT"], writes=["Btok"])
                S.add("dve", I("tensor_tensor", out=dt[:], in0=dtr[s][:], in1=dtb[:], op=ALU.add), reads=[("dtr", s), "dtb"], writes=["dt"])
                S.add("act", I("activation", out=dt[:], in_=dt[:], func=AF.Exp), reads=["dt"], writes=["dt"])
                S.add("act", I("activation", out=dt[:], in_=dt[:], func=AF.Ln, bias=1.0), reads=["dt"], writes=["dt"])
                if c == NT:
                    S.add("dve", I("tensor_scalar", out=dt[:], in0=dt[:], scalar1=rowm[:, 0:1], scalar2=None, op0=ALU.mult),
                          reads=["dt", "rowm"], writes=["dt"])
                S.add("dve", I("tensor_tensor", out=la[:], in0=dt[:], in1=abc[:], op=ALU.mult), reads=["dt", "abc"], writes=["la"])
                S.add("dve", I("tensor_tensor", out=Y[:], in0=AP(TRIf, 0, [[128, 128], [0, 16], [1, 128]]),
                                                       in1=AP(la, 0, [[16, 128], [1, 16], [0, 128]]), op=ALU.mult),
                      reads=["TRIf", "la"], writes=["Y"])
                for qd in range(4):
                    S.add("pe", I("matmul", psD[qd % 2][:], lhsT=Uf[:], rhs=AP(Y, qd * 512, [[2048, 128], [1, 512]]), start=True, stop=True),
                          reads=["Uf", "Y"], writes=[("psD", qd % 2)])
                    S.add("act", I("activation", out=AP(expD, qd * 512, [[2048, 128], [1, 512]]), in_=psD[qd % 2][:], func=AF.Exp),
                          reads=[("psD", qd % 2)], writes=[("expD", qd)])
                S.add("pe", I("matmul", psA[:, 0:16], lhsT=TRIf[:], rhs=la[:], start=True, stop=True), reads=["TRIf", "la"], writes=["psA0"])
                S.add("pe", I("matmul", psA[:, 16:32], lhsT=ONESf[:], rhs=la[:], start=True, stop=True), reads=["ONESf", "la"], writes=["psA1"])
                S.add("act", I("copy", out=acs[:], in_=psA[:, 0:16]), reads=["psA0"], writes=["acs"])
                S.add("act", I("activation", out=eacs[:], in_=psA[:, 0:16], func=AF.Exp), reads=["psA0"], writes=["eacs"])
                S.add("act", I("activation", out=dch[:], in_=psA[:, 16:32], func=AF.Exp), reads=["psA1"], writes=["dch"])
                S.add("dve", I("tensor_tensor", out=dte[:], in0=psA[:, 16:32], in1=acs[:], op=ALU.subtract), reads=["psA1", "acs"], writes=["dte"])
                S.add("act", I("activation", out=dte[:], in_=dte[:], func=AF.Exp), reads=["dte"], writes=["dte"])
                for g in range(2):
                    S.add("pe", I("matmul", psA[:, 32 + g * 128:160 + g * 128], lhsT=xcT[:, 8 + g, tok], rhs=xcT[:, 10 + g, tok],
                                                                 start=True, stop=True), reads=XC, writes=["psAG"])
                S.add("dve", I("tensor_tensor", out=GM[:], in0=AP(psA, 32, [[512, 128], [128, 2], [1, 128]]),
                                                       in1=AP(TRIb, 0, [[128, 128], [0, 2], [1, 128]]), op=ALU.mult),
                      reads=["psAG", "TRIb"], writes=["GM"])
                for g in range(2):
                    S.add("dve" if g == 0 else "pool", I("tensor_tensor",
                        out=MTt[:, 8 * g:8 * g + 8, :], in0=expD[:, 8 * g:8 * g + 8, :], in1=AP(GM, g * 128, [[256, 128], [0, 8], [1, 128]]), op=ALU.mult),
                        reads=[("expD", q) for q in range(4)] + ["GM"], writes=[("MTt", g)])
                S.add("dve", I("tensor_tensor", out=AP(xdt, 0, [[1024, 128], [64, 16], [1, 64]]), in0=AP(xtok, 0, [[1024, 128], [64, 16], [1, 64]]),
                                                       in1=AP(dt, 0, [[16, 128], [1, 16], [0, 64]]), op=ALU.mult), reads=["xtok", "dt"], writes=["xdt"])
                S.add("pool", I("tensor_tensor", out=AP(xdtd, 0, [[1024, 128], [64, 16], [1, 64]]), in0=AP(xdt, 0, [[1024, 128], [64, 16], [1, 64]]),
                                                        in1=AP(dte, 0, [[16, 128], [1, 16], [0, 64]]), op=ALU.mult), reads=["xdt", "dte"], writes=["xdtd"])
                for h in range(16):
                    S.add("pe", I("matmul", psY[h // 8][:, (h % 8) * 64:(h % 8 + 1) * 64], lhsT=MTt[:, h, :], rhs=xdt[:, h * 64:(h + 1) * 64],
                                                        start=True, stop=True), reads=[("MTt", h // 8), "xdt"], writes=[("psY", h // 8)])
                for h in range(16):
                    S.add("pe", I("matmul", psF[h // 8][:, (h % 8) * 64:(h % 8 + 1) * 64], lhsT=xcT[:, 10 + h // 8, tok], rhs=Sb[:, h, :],
                                                                 start=True, stop=True), reads=XC + ["Sb"], writes=[("psF", h // 8)])
                for hf in range(2):
                    S.add("dve", I("tensor_tensor", out=AP(ytmp, hf * 512, [[1024, 128], [64, 8], [1, 64]]),
                                                                  in0=AP(psF[hf], 0, [[512, 128], [64, 8], [1, 64]]),
                                                                  in1=AP(eacs, hf * 8, [[16, 128], [1, 8], [0, 64]]), op=ALU.mult),
                          reads=[("psF", hf), "eacs"], writes=[("ytmp", hf)])
                    S.add("dve", I("tensor_tensor", out=yy[:, hf * 512:(hf + 1) * 512], in0=psY[hf][:], in1=ytmp[:, hf * 512:(hf + 1) * 512], op=ALU.add),
                          reads=[("psY", hf), ("ytmp", hf)], writes=[("yy", hf)])
                S.add("pool", I("tensor_tensor", out=AP(ytmp, 0, [[1024, 128], [64, 16], [1, 64]]), in0=AP(xtok, 0, [[1024, 128], [64, 16], [1, 64]]),
                                                        in1=AP(dsk, 0, [[16, 128], [1, 16], [0, 64]]), op=ALU.mult),
                      reads=["xtok", "dsk", ("yy", 0), ("yy", 1)], writes=[("ytmp", 0), ("ytmp", 1)])
                S.add("pool", I("tensor_tensor", out=yy[:], in0=yy[:], in1=ytmp[:], op=ALU.add),
                      reads=[("ytmp", 0), ("ytmp", 1), ("yy", 0), ("yy", 1)], writes=[("yy", 0), ("yy", 1)])
                S.add("act", I("activation", out=mz[s][:], in_=mz[s][:], func=AF.Silu), reads=[("mz", s)], writes=[("mz", s)])
                S.add("dve", I("tensor_tensor", out=yy[:], in0=yy[:], in1=mz[s][:], op=ALU.mult),
                      reads=[("mz", s), ("yy", 0), ("yy", 1)], writes=[("yy", 0), ("yy", 1)])
                for g in range(2):
                    S.add("act", I("activation", out=junk[:], in_=yy[:, g * 512:(g + 1) * 512], func=AF.Square, accum_out=ssg[:, g:g + 1]),
                          reads=[("yy", 0), ("yy", 1)], writes=["junk", "ssg"])
                S.add("act", I("activation", out=rsg[:], in_=ssg[:], func=AF.Sqrt, bias=epsb[:], scale=1.0 / 512), reads=["ssg", "eps"], writes=["rsg"])
                S.add("dve", I("reciprocal", out=rsg[:], in_=rsg[:]), reads=["rsg"], writes=["rsg"])
                S.add("dve", I("tensor_tensor", out=AP(yy, 0, [[1024, 128], [512, 2], [1, 512]]), in0=AP(yy, 0, [[1024, 128], [512, 2], [1, 512]]),
                                                       in1=AP(rsg, 0, [[2, 128], [1, 2], [0, 512]]), op=ALU.mult),
                      reads=["rsg", ("yy", 0), ("yy", 1)], writes=[("yy", 0), ("yy", 1)])
                S.add("pool", I("tensor_tensor", out=mo[:], in0=yy[:], in1=mnw[:], op=ALU.mult), reads=["mnw", ("yy", 0), ("yy", 1)], writes=["mo"])
                for j in range(8):
                    S.add("pe", I("transpose", out=psXT[:, j * 128:(j + 1) * 128], in_=mo[:, j * 128:(j + 1) * 128], identity=idb[:]),
                          reads=["mo", "idb"], writes=["psXT"])
                S.add("act", I("copy", out=brs[s][:], in_=AP(psXT, 0, [[1024, 128], [128, 8], [1, 128]])), reads=["psXT"], writes=[("brs", s)])
                dma("sp", AP(BRT, c * 393216 + 2048, [[3072, 128], [128, 8], [1, 128]]), brs[s][:], reads=[("brs", s)])
                for h in range(16):
                    S.add("pe", I("matmul", psD[h // 8][:, (h % 8) * 64:(h % 8 + 1) * 64], lhsT=Btok[:, h // 8, :], rhs=xdtd[:, h * 64:(h + 1) * 64],
                                                        start=True, stop=True), reads=["Btok", "xdtd"], writes=[("psD", h // 8)])
                S.add("dve", I("tensor_tensor", out=Sf[:], in0=Sf[:], in1=AP(dch, 0, [[16, 128], [1, 16], [0, 64]]), op=ALU.mult),
                      reads=["Sf", "dch"], writes=["Sf"])
                for hf in range(2):
                    S.add("dve", I("tensor_tensor", out=AP(Sf, hf * 512, [[1024, 128], [1, 512]]), in0=psD[hf][:],
                                                                  in1=AP(Sf, hf * 512, [[1024, 128], [1, 512]]), op=ALU.add),
                          reads=[("psD", hf), "Sf"], writes=["Sf"])
                S.add("act", I("copy", out=Sb[:], in_=Sf[:]), reads=["Sf"], writes=["Sb"])
            dump_state(SM["ssm_s"], l * 1024 * 128)
            S.emit()


_CACHE = {}


def make_in_map(T, L, x_prompt_seq, x_sample_seq, weights, cache_flat, cwin, spool, sconv, sssm, pt, consts):
    xp = np.zeros((T + 128, D), np.float32)
    xp[:T] = x_prompt_seq
    xp[T:T + 4] = x_sample_seq
    m = {"xp": xp, "cache": cache_flat, "cwin": np.ascontiguousarray(cwin).reshape(L, 512, 512),
         "spool": np.ascontiguousarray(spool), "sconv": np.ascontiguousarray(sconv),
         "sssm": np.ascontiguousarray(sssm).reshape(L, 1024, 128), "pt": np.ascontiguousarray(pt).reshape(1, 128).astype(np.int32)}
    for k in WEIGHTS:
        m[k] = weights[k]
    for k, v in consts.items():
        m["c_" + k] = v
    return m


def kernel(**inputs):
    T, L = 4096, 4
    npool = inputs["cache_kv"].shape[1]
    key = (T, L, npool)
    if key not in _CACHE:
        _CACHE[key] = build(T, L, NPOOL=npool)
    nc = _CACHE[key]
    consts = make_consts(T)
    weights = {k: np.ascontiguousarray(inputs[k]) for k in WEIGHTS}
    cache_flat = np.ascontiguousarray(inputs["cache_kv"]).reshape(L * npool * 128 * 8, 128)
    in_maps = []
    for c in range(8):
        in_maps.append(make_in_map(T, L, inputs["x_prompt"][c % 2], inputs["x_sample"][c], weights, cache_flat,
                                   inputs["cache_win"][:, c], inputs["state_pool"][:, c], inputs["state_conv"][:, c],
                                   inputs["state_ssm"][:, c], inputs["page_table"][c], consts))
    res = run_bass_kernel_spmd(nc, in_maps, core_ids=list(range(8))).results
    f = lambda name: np.stack([res[b][name] for b in range(2)])
    y_prompt = f("y_p")
    kv_p = np.transpose(f("kv_p"), (1, 0, 2, 3)).reshape(L, 2, T, 4, 2, 128)
    win_p = np.transpose(f("win_p"), (1, 0, 2, 3)).reshape(L, 2, 512, 2, 2, 128)
    pool_p = np.transpose(f("pool_p"), (1, 0, 2, 3))
    conv_p = np.transpose(f("conv_p"), (1, 0, 2, 3))
    ssm_p = np.transpose(f("ssm_p"), (1, 0, 2, 3)).reshape(L, 2, 16, 64, 128)
    s8 = lambda name: np.stack([res[b][name] for b in range(8)])
    y_sample = s8("y_s")
    kv_s = np.transpose(s8("kv_s"), (1, 0, 2, 3)).reshape(L, 8, 4, 4, 2, 128)
    win_s = np.transpose(s8("win_s"), (1, 0, 2, 3)).reshape(L, 8, 512, 2, 2, 128)
    pool_s = np.transpose(s8("pool_s"), (1, 0, 2, 3))
    conv_s = np.transpose(s8("conv_s"), (1, 0, 2, 3))
    ssm_s = np.transpose(s8("ssm_s"), (1, 0, 2, 3)).reshape(L, 8, 16, 64, 128)
    return (y_prompt, y_sample, kv_p, win_p, pool_p, conv_p, ssm_p, kv_s, win_s, pool_s, conv_s, ssm_s)
```
